# Optimizing a Trainium2 kernel written in Bass

```python
import math
import jax
import jax.numpy as jnp
from jax import lax
import numpy as np

D_MODEL = 2048
BATCH = 2
SEQ = 4096
DEPTH = 4

MEM_LEN = 256
Q_BLOCK = 128
D_FF = 5632
NORM_EPS = 1e-6
NEG_INF = -1e30
SEL_BIG = 1e9

DIFF_HEADS = 8
DIFF_QK_DIM = 64
DIFF_V_DIM = 2 * DIFF_QK_DIM

NSA_HEADS = 8
NSA_GROUPS = 2
NSA_HPG = NSA_HEADS // NSA_GROUPS
NSA_DIM = 128
CMP_LEN = 32
CMP_STRIDE = 16
SLC_LEN = 64
SLC_TOPK = 8
WIN_LEN = 512

MEM_HEADS = 4
MEM_DIM = 128

N_BRANCH = 3

DIFF_QK_COLS = DIFF_HEADS * 2 * DIFF_QK_DIM
DIFF_V_COLS = DIFF_HEADS * DIFF_V_DIM
NSA_Q_COLS = NSA_HEADS * NSA_DIM
NSA_KV_COLS = 3 * 2 * NSA_GROUPS * NSA_DIM
NSA_GATE_COLS = NSA_HEADS * 3
MEM_Q_COLS = MEM_HEADS * MEM_DIM
MERGE_GATE_COLS = N_BRANCH * D_MODEL
IN_SPLITS = (DIFF_QK_COLS, DIFF_QK_COLS, DIFF_V_COLS, NSA_Q_COLS, NSA_KV_COLS, NSA_GATE_COLS, MEM_Q_COLS, MERGE_GATE_COLS)
IN_COLS = sum(IN_SPLITS)

kernel_name = "hybrid_diff_nsa_macaron_trunk"


def rmsnorm(x, g):
    xf = x.astype(jnp.float32)
    y = xf * lax.rsqrt(jnp.mean(xf * xf, axis=-1, keepdims=True) + NORM_EPS)
    return (y * g.astype(jnp.float32)).astype(x.dtype)


def swiglu(x, w_gate, w_up, w_down):
    return (jax.nn.silu(x @ w_gate) * (x @ w_up)) @ w_down


def alibi_slopes(n_heads):
    return jnp.asarray(np.array([2.0 ** (-8.0 * (h + 1) / n_heads) for h in range(n_heads)], dtype=np.float32))


def masked_softmax(s, mask):
    s = jnp.where(mask, s, NEG_INF)
    m = jnp.max(s, axis=-1, keepdims=True)
    e = jnp.where(mask, jnp.exp(s - m), 0.0)
    return e / jnp.maximum(jnp.sum(e, axis=-1, keepdims=True), 1e-30)


def split_columns(proj):
    outs, start = [], 0
    for width in IN_SPLITS:
        outs.append(proj[..., start:start + width])
        start += width
    return outs


def diff_attention(q, k, v, lam, lam_init, subln_g):
    B, S = q.shape[:2]
    nb = S // Q_BLOCK
    scale = DIFF_QK_DIM ** -0.5
    slopes = alibi_slopes(DIFF_HEADS)[None, :, None, None, None]
    kpos = jnp.arange(S)
    qb = q.reshape(B, nb, Q_BLOCK, DIFF_HEADS, 2, DIFF_QK_DIM).transpose(1, 0, 2, 3, 4, 5)

    def block(args):
        q_blk, i = args
        qpos = i * Q_BLOCK + jnp.arange(Q_BLOCK)
        dist = (qpos[:, None] - kpos[None, :]).astype(jnp.float32)
        s = jnp.einsum('bqhmd,bkhmd->bhmqk', q_blk, k).astype(jnp.float32) * scale - slopes * dist
        p = masked_softmax(s, dist >= 0)
        w = p[:, :, 0] - lam * p[:, :, 1]
        return jnp.einsum('bhqk,bkhd->bqhd', w.astype(v.dtype), v)

    o = lax.map(block, (qb, jnp.arange(nb)))
    o = o.transpose(1, 0, 2, 3, 4).reshape(B, S, DIFF_HEADS, DIFF_V_DIM)
    o = rmsnorm(o, subln_g) * (1.0 - lam_init)
    return o.reshape(B, S, DIFF_HEADS * DIFF_V_DIM)


def compress(kv, pos, w1, w2):
    B, S, G, d = kv.shape
    nc = (S - CMP_LEN) // CMP_STRIDE + 1
    idx = jnp.arange(nc)[:, None] * CMP_STRIDE + jnp.arange(CMP_LEN)[None, :]
    blocks = kv[:, idx] + pos[None, None, :, None, :]
    blocks = blocks.transpose(0, 1, 3, 2, 4).reshape(B, nc, G, CMP_LEN * d)
    return jax.nn.gelu(blocks @ w1) @ w2


def nsa_attention(q, k_cmp, v_cmp, k_slc, v_slc, k_win, v_win, gates):
    B, S = q.shape[:2]
    nb = S // Q_BLOCK
    nc = k_cmp.shape[1]
    ns = S // SLC_LEN
    n_sel = min(SLC_TOPK, ns)
    scale = NSA_DIM ** -0.5
    slopes = alibi_slopes(NSA_HEADS).reshape(NSA_GROUPS, NSA_HPG)[None, :, :, None, None]
    cmp_start = jnp.arange(nc) * CMP_STRIDE
    cmp_end = cmp_start + CMP_LEN - 1
    slc_start = jnp.arange(ns) * SLC_LEN
    overlap = ((cmp_start[:, None] < slc_start[None, :] + SLC_LEN) & (cmp_start[:, None] + CMP_LEN > slc_start[None, :])).astype(jnp.float32)
    k_slc_t = k_slc.transpose(0, 2, 1, 3)
    v_slc_t = v_slc.transpose(0, 2, 1, 3)
    kw_pad = jnp.pad(k_win, ((0, 0), (WIN_LEN, 0), (0, 0), (0, 0)))
    vw_pad = jnp.pad(v_win, ((0, 0), (WIN_LEN, 0), (0, 0), (0, 0)))
    bi = jnp.arange(B)[:, None, None, None]
    gi = jnp.arange(NSA_GROUPS)[None, :, None, None]
    jblk = jnp.arange(ns)
    qb = q.reshape(B, nb, Q_BLOCK, NSA_GROUPS, NSA_HPG, NSA_DIM).transpose(1, 0, 2, 3, 4, 5)
    gb = gates.reshape(B, nb, Q_BLOCK, NSA_GROUPS, NSA_HPG, 3).transpose(1, 0, 2, 3, 4, 5)

    def block(args):
        q_blk, g_blk, i = args
        q0 = i * Q_BLOCK
        qpos = q0 + jnp.arange(Q_BLOCK)
        s_c = jnp.einsum('bqghd,bcgd->bghqc', q_blk, k_cmp).astype(jnp.float32) * scale
        p_c = masked_softmax(s_c, cmp_end[None, :] <= qpos[:, None])
        o_c = jnp.einsum('bghqc,bcgd->bqghd', p_c.astype(v_cmp.dtype), v_cmp)
        imp = jnp.einsum('bghqc,cj->bgqj', p_c, overlap)
        cur = qpos // SLC_LEN
        forced = (jblk[None, :] == 0) | (jblk[None, :] == cur[:, None]) | (jblk[None, :] == cur[:, None] - 1)
        future = jblk[None, :] > cur[:, None]
        imp = jnp.where(forced, SEL_BIG, jnp.where(future, -SEL_BIG, imp))
        _, sel = lax.top_k(imp, n_sel)
        tok = (sel[..., None] * SLC_LEN + jnp.arange(SLC_LEN)).reshape(B, NSA_GROUPS, Q_BLOCK, n_sel * SLC_LEN)
        kg = k_slc_t[bi, gi, tok]
        vg = v_slc_t[bi, gi, tok]
        dist_s = (qpos[None, None, :, None] - tok).astype(jnp.float32)[:, :, None]
        s_s = jnp.einsum('bqghd,bgqtd->bghqt', q_blk, kg).astype(jnp.float32) * scale - slopes * dist_s
        p_s = masked_softmax(s_s, dist_s >= 0)
        o_s = jnp.einsum('bghqt,bgqtd->bqghd', p_s.astype(vg.dtype), vg)
        kw = lax.dynamic_slice_in_dim(kw_pad, q0, Q_BLOCK + WIN_LEN, axis=1)
        vw = lax.dynamic_slice_in_dim(vw_pad, q0, Q_BLOCK + WIN_LEN, axis=1)
        kpos = q0 - WIN_LEN + jnp.arange(Q_BLOCK + WIN_LEN)
        dist_w = (qpos[:, None] - kpos[None, :]).astype(jnp.float32)
        mask_w = (dist_w >= 0) & (dist_w < WIN_LEN) & (kpos[None, :] >= 0)
        s_w = jnp.einsum('bqghd,bkgd->bghqk', q_blk, kw).astype(jnp.float32) * scale - slopes * dist_w
        p_w = masked_softmax(s_w, mask_w)
        o_w = jnp.einsum('bghqk,bkgd->bqghd', p_w.astype(vw.dtype), vw)
        return g_blk[..., 0:1] * o_c + g_blk[..., 1:2] * o_s + g_blk[..., 2:3] * o_w

    o = lax.map(block, (qb, gb, jnp.arange(nb)))
    return o.transpose(1, 0, 2, 3, 4, 5).reshape(B, S, NSA_HEADS * NSA_DIM)


def memory_attention(q, k, v):
    B, S = q.shape[:2]
    s = jnp.einsum('bshd,bmhd->bhsm', q, k).astype(jnp.float32) * (MEM_DIM ** -0.5)
    p = jax.nn.softmax(s, axis=-1)
    return jnp.einsum('bhsm,bmhd->bshd', p.astype(v.dtype), v).reshape(B, S, MEM_HEADS * MEM_DIM)


def setup_inputs(seed: int = 0) -> dict:
    key = jax.random.key(seed)
    keys = iter(jax.random.split(key, 32))
    L, D, F = DEPTH, D_MODEL, D_FF

    def dense(shape, fan_in):
        return jax.random.normal(next(keys), shape, jnp.float32) * (fan_in ** -0.5)

    def gain(shape):
        return 1.0 + 0.02 * jax.random.normal(next(keys), shape, jnp.float32)

    return {
        "x": jax.random.normal(next(keys), (BATCH, SEQ, D), jnp.float32),
        "mem": jax.random.normal(next(keys), (BATCH, MEM_LEN, D), jnp.float32),
        "ffn1_norm": gain((L, D)),
        "ffn1_w_gate": dense((L, D, F), D),
        "ffn1_w_up": dense((L, D, F), D),
        "ffn1_w_down": dense((L, F, D), F),
        "mix_norm": gain((L, D)),
        "w_in": dense((L, D, IN_COLS), D),
        "diff_lambda": 0.1 * jax.random.normal(next(keys), (L, 4, DIFF_QK_DIM), jnp.float32),
        "diff_subln": gain((L, DIFF_V_DIM)),
        "nsa_cmp_pos": 0.1 * jax.random.normal(next(keys), (L, 2, CMP_LEN, NSA_DIM), jnp.float32),
        "nsa_cmp_w1": dense((L, 2, CMP_LEN * NSA_DIM, NSA_DIM), CMP_LEN * NSA_DIM),
        "nsa_cmp_w2": dense((L, 2, NSA_DIM, NSA_DIM), NSA_DIM),
        "mem_norm": gain((L, D)),
        "w_mem_kv": dense((L, D, 2 * MEM_HEADS * MEM_DIM), D),
        "w_up_diff": dense((L, DIFF_HEADS * DIFF_V_DIM, D), DIFF_HEADS * DIFF_V_DIM),
        "w_up_nsa": dense((L, NSA_HEADS * NSA_DIM, D), NSA_HEADS * NSA_DIM),
        "w_up_mem": dense((L, MEM_HEADS * MEM_DIM, D), MEM_HEADS * MEM_DIM),
        "w_out": dense((L, D, D), D),
        "ffn2_norm": gain((L, D)),
        "ffn2_w_gate": dense((L, D, F), D),
        "ffn2_w_up": dense((L, D, F), D),
        "ffn2_w_down": dense((L, F, D), F),
        "final_norm": gain((D,)),
    }


def reference(x, mem, ffn1_norm, ffn1_w_gate, ffn1_w_up, ffn1_w_down, mix_norm, w_in,
              diff_lambda, diff_subln, nsa_cmp_pos, nsa_cmp_w1, nsa_cmp_w2, mem_norm, w_mem_kv,
              w_up_diff, w_up_nsa, w_up_mem, w_out, ffn2_norm, ffn2_w_gate, ffn2_w_up,
              ffn2_w_down, final_norm):
    B, S, _ = x.shape
    M = mem.shape[1]
    for l in range(DEPTH):
        x = x + 0.5 * swiglu(rmsnorm(x, ffn1_norm[l]), ffn1_w_gate[l], ffn1_w_up[l], ffn1_w_down[l])
        h = rmsnorm(x, mix_norm[l])
        dq, dk, dv, nq, nkv, ng, mq, mg = split_columns(h @ w_in[l])
        lam_init = 0.8 - 0.6 * math.exp(-0.3 * l)
        lp = diff_lambda[l].astype(jnp.float32)
        lam = jnp.exp(jnp.sum(lp[0] * lp[1])) - jnp.exp(jnp.sum(lp[2] * lp[3])) + lam_init
        o_diff = diff_attention(dq.reshape(B, S, DIFF_HEADS, 2, DIFF_QK_DIM),
                                dk.reshape(B, S, DIFF_HEADS, 2, DIFF_QK_DIM),
                                dv.reshape(B, S, DIFF_HEADS, DIFF_V_DIM),
                                lam, lam_init, diff_subln[l])
        nkv = nkv.reshape(B, S, 3, 2, NSA_GROUPS, NSA_DIM)
        k_cmp = compress(nkv[:, :, 0, 0], nsa_cmp_pos[l, 0], nsa_cmp_w1[l, 0], nsa_cmp_w2[l, 0])
        v_cmp = compress(nkv[:, :, 0, 1], nsa_cmp_pos[l, 1], nsa_cmp_w1[l, 1], nsa_cmp_w2[l, 1])
        o_nsa = nsa_attention(nq.reshape(B, S, NSA_GROUPS, NSA_HPG, NSA_DIM), k_cmp, v_cmp,
                              nkv[:, :, 1, 0], nkv[:, :, 1, 1], nkv[:, :, 2, 0], nkv[:, :, 2, 1],
                              jax.nn.sigmoid(ng).reshape(B, S, NSA_GROUPS, NSA_HPG, 3))
        mem_kv = (rmsnorm(mem, mem_norm[l]) @ w_mem_kv[l]).reshape(B, M, 2, MEM_HEADS, MEM_DIM)
        o_mem = memory_attention(mq.reshape(B, S, MEM_HEADS, MEM_DIM), mem_kv[:, :, 0], mem_kv[:, :, 1])
        g = jax.nn.sigmoid(mg).reshape(B, S, N_BRANCH, D_MODEL)
        merged = (g[:, :, 0] * (o_diff @ w_up_diff[l]) + g[:, :, 1] * (o_nsa @ w_up_nsa[l])
                  + g[:, :, 2] * (o_mem @ w_up_mem[l]))
        x = x + merged @ w_out[l]
        x = x + 0.5 * swiglu(rmsnorm(x, ffn2_norm[l]), ffn2_w_gate[l], ffn2_w_up[l], ffn2_w_down[l])
    return rmsnorm(x, final_norm)
```

```python
import math
from contextlib import ExitStack
import numpy as np
import concourse.bass as bass
import concourse.mybir as mybir
from concourse.bass_utils import run_bass_kernel_spmd

F32 = mybir.dt.float32
BF16 = mybir.dt.bfloat16
AF = mybir.ActivationFunctionType
ALU = mybir.AluOpType
AX = mybir.AxisListType

ENGS = ("pe", "act", "dve", "pool", "sp")
N_DMA_SLOTS = 8


class Buf:
    __slots__ = ("name", "w", "r")

    def __init__(self, name):
        self.name = name
        self.w = None
        self.r = []


class Op:
    __slots__ = ("id", "eng", "fn", "deps", "is_dma", "slot", "semval", "prev_slot_op",
                 "signaled", "sigval")

    def __init__(self, id, eng, fn, is_dma):
        self.id = id
        self.eng = eng
        self.fn = fn
        self.deps = []
        self.is_dma = is_dma
        self.slot = None
        self.semval = 0
        self.prev_slot_op = None
        self.signaled = False
        self.sigval = 0


class Prog:
    def __init__(self, nc):
        self.nc = nc
        self.ops = []
        self.by_eng = {e: [] for e in ENGS}
        self.slot_last = {e: [None] * N_DMA_SLOTS for e in ENGS}
        self.slot_cnt = {e: [0] * N_DMA_SLOTS for e in ENGS}
        self.slot_rr = {e: 0 for e in ENGS}
        self.out_dmas = []

    def _add(self, eng, fn, reads, writes, is_dma):
        op = Op(len(self.ops), eng, fn, is_dma)
        deps = set()
        for b in reads:
            if b.w is not None:
                deps.add(b.w)
        for b in writes:
            if b.w is not None:
                deps.add(b.w)
            for r in b.r:
                deps.add(r)
        for b in writes:
            b.w = op.id
            b.r = []
        for b in reads:
            if b.w != op.id:
                b.r.append(op.id)
        deps.discard(op.id)
        for d in sorted(deps):
            t = self.ops[d]
            if (not is_dma) and eng == "pe" and t.eng == "pe" and not t.is_dma:
                continue
            op.deps.append(d)
            if not t.is_dma:
                t.signaled = True
        if is_dma:
            s = self.slot_rr[eng]
            self.slot_rr[eng] = (s + 1) % N_DMA_SLOTS
            op.slot = s
            op.prev_slot_op = self.slot_last[eng][s]
            self.slot_cnt[eng][s] += 16
            op.semval = self.slot_cnt[eng][s]
            self.slot_last[eng][s] = op.id
        self.ops.append(op)
        self.by_eng[eng].append(op)
        return op

    def op(self, eng, fn, reads=(), writes=()):
        return self._add(eng, fn, reads, writes, False)

    def dma(self, eng, out, in_, reads=(), writes=(), is_output=False):
        op = self._add(eng, lambda e: e.dma_start(out=out, in_=in_), reads, writes, True)
        if is_output:
            self.out_dmas.append(op.id)
        return op

    def emit(self):
        nc = self.nc
        with ExitStack() as es:
            es.enter_context(nc.cleanup_on_exit())
            esem = {e: nc.alloc_semaphore(name="s_" + e) for e in ENGS}
            dsem = {e: [nc.alloc_semaphore(name="d_%s%d" % (e, i)) for i in range(N_DMA_SLOTS)]
                    for e in ("sp", "pool")}
            for e in ENGS:
                c = 0
                for op in self.by_eng[e]:
                    if op.signaled and not op.is_dma:
                        c += 1
                        op.sigval = c
            block = nc.Block()
            block.__enter__()
            ops = self.ops
            out_dmas = self.out_dmas

            def stream(eng_name, h):
                seen = {}

                def wait(sem, key, val):
                    if seen.get(key, 0) >= val:
                        return
                    h.wait_ge(sem, val)
                    seen[key] = val

                for op in self.by_eng[eng_name]:
                    for d in op.deps:
                        t = ops[d]
                        if t.is_dma:
                            wait(dsem[t.eng][t.slot], (t.eng, t.slot), t.semval)
                        else:
                            wait(esem[t.eng], t.eng, t.sigval)
                    if op.is_dma:
                        if op.prev_slot_op is not None:
                            p = ops[op.prev_slot_op]
                            wait(dsem[eng_name][op.slot], (eng_name, op.slot), p.semval)
                        ins = op.fn(h)
                        ins.then_inc(dsem[eng_name][op.slot], 16)
                    else:
                        ins = op.fn(h)
                        if op.signaled:
                            ins.then_inc(esem[eng_name], 1)
                if eng_name == "sp":
                    for d in out_dmas:
                        t = ops[d]
                        wait(dsem[t.eng][t.slot], (t.eng, t.slot), t.semval)
                    for q in ("sp", "pool"):
                        for s in range(N_DMA_SLOTS):
                            if self.slot_cnt[q][s] > 0:
                                wait(dsem[q][s], (q, s), self.slot_cnt[q][s])

            @block.tensor
            def _(e):
                stream("pe", e)

            @block.scalar
            def _(e):
                stream("act", e)

            @block.vector
            def _(e):
                stream("dve", e)

            @block.gpsimd
            def _(e):
                stream("pool", e)

            @block.sync
            def _(e):
                stream("sp", e)

            block.__exit__(None, None, None)
            nc.all_engine_barrier()


D = 2048
DFF = 5632
NKT = D // 128
NFT = DFF // 128
TOK = 1024
TC = 512
IN_COLS = 12312
EPS = 1e-6
S128 = 128.0 ** -0.5


class RPool:
    def __init__(self, es, nc, name, shape, dtype, n, psum=False):
        mk = nc.psum_tensor if psum else nc.sbuf_tensor
        self.t = [es.enter_context(mk("%s%d" % (name, i), shape, dtype)) for i in range(n)]
        self.b = [Buf("%s%d" % (name, i)) for i in range(n)]
        self.i = 0
        self.n = n

    def next(self):
        i = self.i
        self.i = (i + 1) % self.n
        return self.t[i], self.b[i]


def build_token(mode):
    nc = bass.Bass("TRN2", target_bir_lowering=False)

    def din(name, shape, dt=F32):
        return nc.dram_tensor(name, shape, dt, kind="ExternalInput").ap()

    def dout(name, shape, dt=F32):
        return nc.dram_tensor(name, shape, dt, kind="ExternalOutput").ap()

    xT = din("xT", [D, TOK])
    gn = din("gn", [128, NKT])
    wg = din("wg", [D, DFF])
    wu = din("wu", [D, DFF])
    wd = din("wd", [DFF, D])
    if mode == "pre":
        gm = din("gm", [128, NKT])
        win = din("win", [D, IN_COLS])
        xoT = dout("xoT", [D, TOK])
        featT = dout("featT", [36 * 128, TOK], BF16)
        gTo = dout("gT", [6144, TOK])
        tokV = dout("tokV", [TOK, 1536], BF16)
        gates = dout("gates", [TOK, 24])
    else:
        oT = din("oT", [2560, TOK], BF16)
        gTi = din("gT", [6144, TOK])
        wud = din("wud", [1024, D])
        wun = din("wun", [1024, D])
        wum = din("wum", [512, D])
        wo = din("wo", [D, D])
        gf = din("gf", [128, NKT])
        xoT = dout("xoT", [D, TOK])
        yT = dout("yT", [D, TOK])

    with ExitStack() as es:
        P = Prog(nc)
        sb = lambda name, shape, dt: es.enter_context(nc.sbuf_tensor(name, shape, dt))
        x_t = sb("x_t", [128, NKT, TC], F32)
        xb = [Buf("x%d" % i) for i in range(NKT)]
        h_t = sb("h_t", [128, NKT, TC], BF16)
        hb = [Buf("h%d" % i) for i in range(NKT)]
        hid = sb("hid", [128, NFT, TC], BF16)
        hidb = [Buf("hid%d" % i) for i in range(NFT)]
        ones = sb("ones", [128, 128], BF16)
        onesb = Buf("ones")
        gn_t = sb("gn_t", [128, NKT], F32)
        gnb = Buf("gn")
        g2_t = sb("g2_t", [128, NKT], F32)
        g2b = Buf("g2")
        wpool = RPool(es, nc, "wp", [128, NKT, 256], BF16, 4)
        wdpool = RPool(es, nc, "wdp", [128, NFT, 256], BF16, 2)
        psum = RPool(es, nc, "ps", [128, 512], F32, 8, psum=True)
        sqpool = RPool(es, nc, "sq", [128, TC], BF16, 3)
        f32pool = RPool(es, nc, "f32p", [128, TC], F32, 6)
        rstdpool = RPool(es, nc, "rstd", [128, TC], F32, 2)
        bfpool = RPool(es, nc, "bfp", [128, TC], BF16, 4)
        if mode == "post":
            gpool = RPool(es, nc, "gp", [128, 3, TC], F32, 2)

        epsc = sb("epsc", [128, 1], F32)
        epsb = Buf("eps")
        P.op("dve", lambda e: e.memset(ones[:], 1.0), writes=[onesb])
        P.op("dve", lambda e: e.memset(epsc[:], EPS), writes=[epsb])
        P.dma("sp", gn_t[:], gn, writes=[gnb])
        P.dma("sp", g2_t[:], gm if mode == "pre" else gf, writes=[g2b])

        xTv = xT.rearrange("(kt p) t -> p kt t", p=128)
        xoTv = xoT.rearrange("(kt p) t -> p kt t", p=128)
        wgv = wg.rearrange("(kt p) f -> p kt f", p=128)
        wuv = wu.rearrange("(kt p) f -> p kt f", p=128)
        wdv = wd.rearrange("(ft p) d -> p ft d", p=128)

        def load_w(view, nk, c0, w):
            wt, wb = wpool.next()
            P.dma("pool", wt[:, 0:nk, 0:w], view[:, :, c0:c0 + w], writes=[wb])
            return wt, wb

        def rmsnorm_stats(gain_unused=None):
            ps, pb = psum.next()
            for kt in range(NKT):
                sq, sqb = sqpool.next()
                P.op("act", lambda e, kt=kt, sq=sq: e.activation(out=sq[:], in_=x_t[:, kt, :], func=AF.Square),
                     reads=[xb[kt]], writes=[sqb])
                P.op("pe", lambda e, kt=kt, sq=sq, ps=ps: e.matmul(ps[:], lhsT=ones[:], rhs=sq[:],
                                                                    start=(kt == 0), stop=(kt == NKT - 1)),
                     reads=[sqb, onesb], writes=[pb])
            rstd, rb = rstdpool.next()
            P.op("act", lambda e, ps=ps, rstd=rstd: e.activation(out=rstd[:], in_=ps[:], func=AF.Sqrt, bias=epsc[:],
                                                                 scale=1.0 / D), reads=[pb, epsb], writes=[rb])
            P.op("dve", lambda e, rstd=rstd: e.reciprocal(out=rstd[:], in_=rstd[:]), reads=[rb], writes=[rb])
            return rstd, rb

        def rmsnorm_to_h(g_t, gb):
            rstd, rb = rmsnorm_stats()
            for kt in range(NKT):
                P.op("dve", lambda e, kt=kt, rstd=rstd: e.scalar_tensor_tensor(
                    out=h_t[:, kt, :], in0=x_t[:, kt, :], scalar=g_t[:, kt:kt + 1], in1=rstd[:],
                    op0=ALU.mult, op1=ALU.mult), reads=[xb[kt], rb, gb], writes=[hb[kt]])

        def ffn():
            rmsnorm_to_h(gn_t, gnb)
            for fc in range(DFF // 256):
                wgt, wgb = load_w(wgv, NKT, fc * 256, 256)
                wut, wub = load_w(wuv, NKT, fc * 256, 256)
                for j in range(2):
                    f = fc * 2 + j
                    pg, pgb = psum.next()
                    pu, pub = psum.next()
                    for kt in range(NKT):
                        P.op("pe", lambda e, kt=kt, pg=pg, wgt=wgt, j=j: e.matmul(
                            pg[:], lhsT=wgt[:, kt, j * 128:(j + 1) * 128], rhs=h_t[:, kt, :],
                            start=(kt == 0), stop=(kt == NKT - 1)), reads=[wgb, hb[kt]], writes=[pgb])
                    for kt in range(NKT):
                        P.op("pe", lambda e, kt=kt, pu=pu, wut=wut, j=j: e.matmul(
                            pu[:], lhsT=wut[:, kt, j * 128:(j + 1) * 128], rhs=h_t[:, kt, :],
                            start=(kt == 0), stop=(kt == NKT - 1)), reads=[wub, hb[kt]], writes=[pub])
                    sg, sgb = f32pool.next()
                    P.op("act", lambda e, sg=sg, pg=pg: e.activation(out=sg[:], in_=pg[:], func=AF.Silu),
                         reads=[pgb], writes=[sgb])
                    P.op("dve", lambda e, sg=sg, pu=pu, f=f: e.tensor_tensor(out=hid[:, f, :], in0=sg[:], in1=pu[:],
                                                                              op=ALU.mult),
                         reads=[sgb, pub], writes=[hidb[f]])
            for dc in range(D // 256):
                wdt, wdb = wdpool.next()
                half = NFT // 2
                P.dma("pool", wdt[:, 0:half, :], wdv[:, 0:half, dc * 256:(dc + 1) * 256], writes=[wdb])
                P.dma("pool", wdt[:, half:NFT, :], wdv[:, half:NFT, dc * 256:(dc + 1) * 256], writes=[wdb])
                for j in range(2):
                    d = dc * 2 + j
                    po, pob = psum.next()
                    for f in range(NFT):
                        P.op("pe", lambda e, f=f, po=po, wdt=wdt, j=j: e.matmul(
                            po[:], lhsT=wdt[:, f, j * 128:(j + 1) * 128], rhs=hid[:, f, :],
                            start=(f == 0), stop=(f == NFT - 1)), reads=[wdb, hidb[f]], writes=[pob])
                    P.op("dve", lambda e, po=po, d=d: e.scalar_tensor_tensor(
                        out=x_t[:, d, :], in0=po[:], scalar=0.5, in1=x_t[:, d, :], op0=ALU.mult, op1=ALU.add),
                        reads=[pob, xb[d]], writes=[xb[d]])

        for c in range(TOK // TC):
            t0 = c * TC
            for kt in range(NKT):
                P.dma("sp", x_t[:, kt, :], xTv[:, kt, t0:t0 + TC], writes=[xb[kt]])
            if mode == "pre":
                ffn()
                for kt in range(NKT):
                    P.dma("sp", xoTv[:, kt, t0:t0 + TC], x_t[:, kt, :], reads=[xb[kt]], is_output=True)
                rmsnorm_to_h(g2_t, g2b)
                winv = win.rearrange("(kt p) f -> p kt f", p=128)
                plan = []
                for i in range(4):
                    plan.append((i * 256, 256, "feat", i * 256, 0.125))
                for i in range(4):
                    plan.append((1024 + i * 256, 256, "feat", 1024 + i * 256, 1.0))
                for i in range(4):
                    plan.append((2048 + i * 256, 256, "tokv", i * 256, 1.0))
                for i in range(4):
                    plan.append((3072 + i * 256, 256, "feat", 2048 + i * 256, S128))
                plan.append((4096, 256, "feat", 3072, 1.0))
                plan.append((4352, 256, "feat", 3328, 1.0))
                plan.append((4608, 256, "feat", 3584, 1.0))
                plan.append((4864, 256, "tokv", 1024, 1.0))
                plan.append((5120, 256, "feat", 3840, 1.0))
                plan.append((5376, 256, "tokv", 1280, 1.0))
                plan.append((5632, 24, "gates", 0, 1.0))
                for i in range(2):
                    plan.append((5656 + i * 256, 256, "feat", 4096 + i * 256, S128))
                for i in range(24):
                    plan.append((6168 + i * 256, 256, "mg", i * 256, 1.0))
                for (c0, w, kind, dst, scale) in plan:
                    wt, wb = load_w(winv, NKT, c0, w)
                    if kind in ("feat", "mg"):
                        for j in range(w // 128):
                            ps, pb = psum.next()
                            for kt in range(NKT):
                                P.op("pe", lambda e, kt=kt, ps=ps, wt=wt, j=j: e.matmul(
                                    ps[:], lhsT=wt[:, kt, j * 128:(j + 1) * 128], rhs=h_t[:, kt, :],
                                    start=(kt == 0), stop=(kt == NKT - 1)), reads=[wb, hb[kt]], writes=[pb])
                            r0 = dst + j * 128
                            if kind == "feat":
                                st, stb = bfpool.next()
                                P.op("act", lambda e, st=st, ps=ps, scale=scale: e.activation(
                                    out=st[:], in_=ps[:], func=AF.Copy, scale=scale), reads=[pb], writes=[stb])
                                P.dma("sp", featT[r0:r0 + 128, t0:t0 + TC], st[:], reads=[stb], is_output=True)
                            else:
                                st, stb = f32pool.next()
                                P.op("act", lambda e, st=st, ps=ps: e.activation(
                                    out=st[:], in_=ps[:], func=AF.Sigmoid), reads=[pb], writes=[stb])
                                P.dma("sp", gTo[r0:r0 + 128, t0:t0 + TC], st[:], reads=[stb], is_output=True)
                    else:
                        for tt in range(TC // 128):
                            ps, pb = psum.next()
                            for kt in range(NKT):
                                P.op("pe", lambda e, kt=kt, ps=ps, wt=wt, tt=tt, w=w: e.matmul(
                                    ps[:, 0:w], lhsT=h_t[:, kt, tt * 128:(tt + 1) * 128], rhs=wt[:, kt, 0:w],
                                    start=(kt == 0), stop=(kt == NKT - 1)), reads=[wb, hb[kt]], writes=[pb])
                            if kind == "tokv":
                                st, stb = bfpool.next()
                                P.op("act", lambda e, st=st, ps=ps, w=w: e.activation(
                                    out=st[:, 0:w], in_=ps[:, 0:w], func=AF.Copy), reads=[pb], writes=[stb])
                                P.dma("sp", tokV[t0 + tt * 128:t0 + (tt + 1) * 128, dst:dst + w], st[:, 0:w],
                                      reads=[stb], is_output=True)
                            else:
                                st, stb = f32pool.next()
                                P.op("act", lambda e, st=st, ps=ps, w=w: e.activation(
                                    out=st[:, 0:w], in_=ps[:, 0:w], func=AF.Sigmoid), reads=[pb], writes=[stb])
                                P.dma("sp", gates[t0 + tt * 128:t0 + (tt + 1) * 128, 0:w], st[:, 0:w],
                                      reads=[stb], is_output=True)
            else:
                oTv = oT.rearrange("(kt p) t -> p kt t", p=128)
                gTv = gTi.rearrange("(br dt p) t -> p br dt t", br=3, p=128)
                wudv = wud.rearrange("(kt p) f -> p kt f", p=128)
                wunv = wun.rearrange("(kt p) f -> p kt f", p=128)
                wumv = wum.rearrange("(kt p) f -> p kt f", p=128)
                wov = wo.rearrange("(kt p) f -> p kt f", p=128)
                for kt in range(20):
                    P.dma("sp", hid[:, kt, :], oTv[:, kt, t0:t0 + TC], writes=[hidb[kt]])
                for dc in range(D // 256):
                    w0, w0b = load_w(wudv, 8, dc * 256, 256)
                    w1, w1b = load_w(wunv, 8, dc * 256, 256)
                    w2, w2b = load_w(wumv, 4, dc * 256, 256)
                    for j in range(2):
                        d = dc * 2 + j
                        gt, gtb = gpool.next()
                        P.dma("sp", gt[:], gTv[:, :, d, t0:t0 + TC], writes=[gtb])
                        pss = []
                        for (wt_, wb_, base, nk) in ((w0, w0b, 0, 8), (w1, w1b, 8, 8), (w2, w2b, 16, 4)):
                            ps, pb = psum.next()
                            for kt in range(nk):
                                P.op("pe", lambda e, kt=kt, ps=ps, wt_=wt_, j=j, base=base, nk=nk: e.matmul(
                                    ps[:], lhsT=wt_[:, kt, j * 128:(j + 1) * 128], rhs=hid[:, base + kt, :],
                                    start=(kt == 0), stop=(kt == nk - 1)), reads=[wb_, hidb[base + kt]], writes=[pb])
                            pss.append((ps, pb))
                        m1, m1b = f32pool.next()
                        m2, m2b = f32pool.next()
                        P.op("dve", lambda e, m1=m1, ps=pss[0][0], gt=gt: e.tensor_tensor(
                            out=m1[:], in0=ps[:], in1=gt[:, 0, :], op=ALU.mult), reads=[pss[0][1], gtb], writes=[m1b])
                        P.op("dve", lambda e, m2=m2, ps=pss[1][0], gt=gt: e.tensor_tensor(
                            out=m2[:], in0=ps[:], in1=gt[:, 1, :], op=ALU.mult), reads=[pss[1][1], gtb], writes=[m2b])
                        P.op("dve", lambda e, m1=m1, m2=m2: e.tensor_tensor(
                            out=m1[:], in0=m1[:], in1=m2[:], op=ALU.add), reads=[m1b, m2b], writes=[m1b])
                        P.op("dve", lambda e, m2=m2, ps=pss[2][0], gt=gt: e.tensor_tensor(
                            out=m2[:], in0=ps[:], in1=gt[:, 2, :], op=ALU.mult), reads=[pss[2][1], gtb], writes=[m2b])
                        P.op("dve", lambda e, m1=m1, m2=m2, d=d: e.tensor_tensor(
                            out=h_t[:, d, :], in0=m1[:], in1=m2[:], op=ALU.add), reads=[m1b, m2b], writes=[hb[d]])
                for dc in range(D // 256):
                    wt, wb = load_w(wov, NKT, dc * 256, 256)
                    for j in range(2):
                        d = dc * 2 + j
                        po, pob = psum.next()
                        for kt in range(NKT):
                            P.op("pe", lambda e, kt=kt, po=po, wt=wt, j=j: e.matmul(
                                po[:], lhsT=wt[:, kt, j * 128:(j + 1) * 128], rhs=h_t[:, kt, :],
                                start=(kt == 0), stop=(kt == NKT - 1)), reads=[wb, hb[kt]], writes=[pob])
                        P.op("dve", lambda e, po=po, d=d: e.tensor_tensor(
                            out=x_t[:, d, :], in0=po[:], in1=x_t[:, d, :], op=ALU.add),
                            reads=[pob, xb[d]], writes=[xb[d]])
                ffn()
                for kt in range(NKT):
                    P.dma("sp", xoTv[:, kt, t0:t0 + TC], x_t[:, kt, :], reads=[xb[kt]], is_output=True)
                rstd, rb = rmsnorm_stats()
                yTv = yT.rearrange("(kt p) t -> p kt t", p=128)
                for kt in range(NKT):
                    st, stb = f32pool.next()
                    P.op("dve", lambda e, kt=kt, rstd=rstd, st=st: e.scalar_tensor_tensor(
                        out=st[:], in0=x_t[:, kt, :], scalar=g2_t[:, kt:kt + 1], in1=rstd[:],
                        op0=ALU.mult, op1=ALU.mult), reads=[xb[kt], rb, g2b], writes=[stb])
                    P.dma("sp", yTv[:, kt, t0:t0 + TC], st[:], reads=[stb], is_output=True)
        P.emit()
    return nc


S = 4096
NQT = S // 128
NCH = S // 512
NCMP = 255
NEG = -30000.0
BIGSEL = 1.0e9


def build_attn():
    nc = bass.Bass("TRN2", target_bir_lowering=False)

    def din(name, shape, dt=F32):
        return nc.dram_tensor(name, shape, dt, kind="ExternalInput").ap()

    def dout(name, shape, dt=F32):
        return nc.dram_tensor(name, shape, dt, kind="ExternalOutput").ap()

    dqa = din("dqa", [2, 2, 66, S], BF16)
    dka = din("dka", [2, 2, 66, S], BF16)
    dva = din("dva", [2, 128, NQT, 129], BF16)
    lamp = din("lamp", [128, 4, 64])
    subg = din("subg", [128, 128])
    lami = din("lami", [128, 1])
    nq = din("nq", [4, 128, S], BF16)
    kcs = din("kcs", [2, 128, S], BF16)
    kslc = din("kslc", [128, S], BF16)
    kwin = din("kwin", [128, S], BF16)
    vslc = din("vslc", [128, NQT, 129], BF16)
    vwin = din("vwin", [128, NQT, 129], BF16)
    ngate = din("ngate", [128, NQT, 6])
    w1 = din("w1", [2, 4096, 128])
    w2 = din("w2", [2, 128, 128])
    posT = din("posT", [2, 128, 32])
    mq = din("mq", [128, S], BF16)
    memT = din("memT", [D, 256])
    gmem = din("gmem", [128, NKT])
    wmk = din("wmk", [D, 128])
    wmv = din("wmv", [D, 128])
    c_ident = din("c_ident", [128, 128], BF16)
    c_causal = din("c_causal", [128, 128], BF16)
    c_winneg = din("c_winneg", [128, 128], BF16)
    c_tneg = din("c_tneg", [128, 2560], BF16)
    c_ufix = din("c_ufix", [128, NQT, 64])
    c_eaug = din("c_eaug", [2, 66, S], BF16)
    c_selaug = din("c_selaug", [2, S], BF16)
    c_slrow = din("c_slrow", [2, 2, 128], BF16)
    c_bias = din("c_bias", [128, 4, 32])
    c_ovl = din("c_ovl", [2, 128, 65], BF16)
    o_diff = dout("o_diff", [S, 2, 128], BF16)
    o_nsa = dout("o_nsa", [S, 2, 128], BF16)
    o_mem = dout("o_mem", [S, 128], BF16)

    with ExitStack() as es:
        P = Prog(nc)
        sb = lambda name, shape, dt: es.enter_context(nc.sbuf_tensor(name, shape, dt))
        cnt = [0]

        def const(name, src, shape, dt, eng="sp"):
            t = sb(name, shape, dt)
            b = Buf(name)
            P.dma(eng, t[:], src, writes=[b])
            return t, b

        ident, identb = const("ident", c_ident, [128, 128], BF16)
        causal, causalb = const("causal", c_causal, [128, 128], BF16)
        winneg, winnegb = const("winneg", c_winneg, [128, 128], BF16)
        tneg, tnegb = const("tneg", c_tneg, [128, 2560], BF16)
        ufix, ufixb = const("ufix", c_ufix, [128, NQT, 64], F32)
        biasT, biasb = const("biasT", c_bias, [128, 4, 32], F32)
        lamp_t, lampb = const("lamp_t", lamp, [128, 4, 64], F32)
        subg_t, subgb = const("subg_t", subg, [128, 128], F32)
        lami_t, lamib = const("lami_t", lami, [128, 1], F32)
        epsc = sb("epsc", [128, 1], F32)
        epsb = Buf("eps")
        P.op("dve", lambda e: e.memset(epsc[:], EPS), writes=[epsb])

        kpool = RPool(es, nc, "kp", [128, S], BF16, 5)
        vpool = RPool(es, nc, "vp", [128, NQT, 129], BF16, 3)
        qpool = RPool(es, nc, "qp", [128, 4, 512], BF16, 3)
        ptpool = RPool(es, nc, "pt", [128, 512], BF16, 4)
        ps_s = RPool(es, nc, "pss", [128, 512], F32, 3, psum=True)
        ps_a = RPool(es, nc, "psa", [128, 512], F32, 4, psum=True)
        pst = es.enter_context(nc.psum_tensor("pst", [128, 1024], BF16))
        pstb = Buf("pst")
        smallp = RPool(es, nc, "sm", [128, 8], F32, 12)
        o32p = RPool(es, nc, "o32", [128, 128], F32, 6)
        obfp = RPool(es, nc, "obf", [128, 128], BF16, 4)

        def exp_act(pt, ps, lo, hi, rows, bias_ap, reads, writes):
            if bias_ap is None:
                P.op("act", lambda e: e.activation(out=pt[0:rows, lo:hi], in_=ps[0:rows, lo:hi], func=AF.Exp),
                     reads=reads, writes=writes)
            else:
                P.op("act", lambda e: e.activation(out=pt[0:rows, lo:hi], in_=ps[0:rows, lo:hi], func=AF.Exp,
                                                   bias=bias_ap), reads=reads + [biasb], writes=writes)

        mem_t = sb("mem_t", [128, NKT, 256], F32)
        memb = Buf("mem")
        P.dma("sp", mem_t[:], memT.rearrange("(kt p) m -> p kt m", p=128), writes=[memb])
        gmem_t, gmemb = const("gmem_t", gmem, [128, NKT], F32)
        ones = sb("ones", [128, 128], BF16)
        onesb = Buf("ones")
        P.op("dve", lambda e: e.memset(ones[:], 1.0), writes=[onesb])
        memn = sb("memn", [128, NKT, 256], BF16)
        memnb = Buf("memn")
        wmk_t = sb("wmk_t", [128, NKT, 128], BF16)
        wmkb = Buf("wmk")
        wmv_t = sb("wmv_t", [128, NKT, 128], BF16)
        wmvb = Buf("wmv")
        P.dma("pool", wmk_t[:], wmk.rearrange("(kt p) f -> p kt f", p=128), writes=[wmkb])
        P.dma("pool", wmv_t[:], wmv.rearrange("(kt p) f -> p kt f", p=128), writes=[wmvb])
        ps, pb = ps_s.next()
        for kt in range(NKT):
            sq, sqb = ptpool.next()
            P.op("act", lambda e, kt=kt, sq=sq: e.activation(out=sq[:, 0:256], in_=mem_t[:, kt, :], func=AF.Square),
                 reads=[memb], writes=[sqb])
            P.op("pe", lambda e, kt=kt, sq=sq, ps=ps: e.matmul(ps[:, 0:256], lhsT=ones[:], rhs=sq[:, 0:256],
                                                                start=(kt == 0), stop=(kt == NKT - 1)),
                 reads=[sqb, onesb], writes=[pb])
        rstd_m = sb("rstd_m", [128, 256], F32)
        rstdmb = Buf("rstd_m")
        P.op("act", lambda e, ps=ps: e.activation(out=rstd_m[:], in_=ps[:, 0:256], func=AF.Sqrt, bias=epsc[:],
                                                  scale=1.0 / D), reads=[pb, epsb], writes=[rstdmb])
        P.op("dve", lambda e: e.reciprocal(out=rstd_m[:], in_=rstd_m[:]), reads=[rstdmb], writes=[rstdmb])
        for kt in range(NKT):
            P.op("dve", lambda e, kt=kt: e.scalar_tensor_tensor(
                out=memn[:, kt, :], in0=mem_t[:, kt, :], scalar=gmem_t[:, kt:kt + 1], in1=rstd_m[:],
                op0=ALU.mult, op1=ALU.mult), reads=[memb, rstdmb, gmemb], writes=[memnb])
        kmT = sb("kmT", [128, 256], BF16)
        kmTb = Buf("kmT")
        vm = sb("vm", [128, 2, 129], BF16)
        vmb = Buf("vm")
        ps, pb = ps_s.next()
        for kt in range(NKT):
            P.op("pe", lambda e, kt=kt, ps=ps: e.matmul(ps[:, 0:256], lhsT=wmk_t[:, kt, :], rhs=memn[:, kt, :],
                                                        start=(kt == 0), stop=(kt == NKT - 1)),
                 reads=[wmkb, memnb], writes=[pb])
        P.op("act", lambda e, ps=ps: e.activation(out=kmT[:], in_=ps[:, 0:256], func=AF.Copy), reads=[pb], writes=[kmTb])
        P.op("dve", lambda e: e.memset(vm[:, :, 128:129], 1.0), writes=[vmb])
        for mt in range(2):
            ps, pb = ps_s.next()
            for kt in range(NKT):
                P.op("pe", lambda e, kt=kt, ps=ps, mt=mt: e.matmul(
                    ps[:, 0:128], lhsT=memn[:, kt, mt * 128:(mt + 1) * 128], rhs=wmv_t[:, kt, :],
                    start=(kt == 0), stop=(kt == NKT - 1)), reads=[wmvb, memnb], writes=[pb])
            P.op("act", lambda e, ps=ps, mt=mt: e.activation(out=vm[:, mt, 0:128], in_=ps[:, 0:128], func=AF.Copy),
                 reads=[pb], writes=[vmb])
        for c in range(NCH):
            qt_, qb_ = qpool.next()
            P.dma("sp", qt_[:, 0, :], mq[:, c * 512:(c + 1) * 512], writes=[qb_])
            accs = [ps_a.next() for _ in range(4)]
            for mt in range(2):
                ps, pb = ps_s.next()
                P.op("pe", lambda e, ps=ps, mt=mt, qt_=qt_: e.matmul(
                    ps[:], lhsT=kmT[:, mt * 128:(mt + 1) * 128], rhs=qt_[:, 0, :], start=True, stop=True),
                    reads=[kmTb, qb_], writes=[pb])
                pt, ptb = ptpool.next()
                exp_act(pt, ps, 0, 512, 128, None, [pb], [ptb])
                for q4 in range(4):
                    acc, accb = accs[q4]
                    P.op("pe", lambda e, acc=acc, pt=pt, q4=q4, mt=mt: e.matmul(
                        acc[:, 0:129], lhsT=pt[:, q4 * 128:(q4 + 1) * 128], rhs=vm[:, mt, :],
                        start=(mt == 0), stop=(mt == 1)), reads=[ptb, vmb], writes=[accb])
            for q4 in range(4):
                acc, accb = accs[q4]
                rinv, rinvb = smallp.next()
                P.op("dve", lambda e, acc=acc, rinv=rinv: e.reciprocal(out=rinv[:, 0:1], in_=acc[:, 128:129]),
                     reads=[accb], writes=[rinvb])
                ob, obb = obfp.next()
                P.op("dve", lambda e, acc=acc, rinv=rinv, ob=ob: e.tensor_scalar(
                    out=ob[:], in0=acc[:, 0:128], scalar1=rinv[:, 0:1], scalar2=None, op0=ALU.mult),
                    reads=[accb, rinvb], writes=[obb])
                q0 = c * 512 + q4 * 128
                P.dma("sp", o_mem[q0:q0 + 128, :], ob[:], reads=[obb], is_output=True)

        lprod = sb("lprod", [128, 2, 64], F32)
        lprodb = Buf("lprod")
        lam4 = sb("lam4", [128, 4], F32)
        lam4b = Buf("lam4")
        gsub = sb("gsub", [128, 128], F32)
        gsubb = Buf("gsub")
        P.op("dve", lambda e: e.tensor_tensor(out=lprod[:], in0=lamp_t[:, 0:4:2, :], in1=lamp_t[:, 1:4:2, :], op=ALU.mult),
             reads=[lampb], writes=[lprodb])
        P.op("dve", lambda e: e.reduce_sum(out=lam4[:, 0:2], in_=lprod[:], axis=AX.X), reads=[lprodb], writes=[lam4b])
        P.op("act", lambda e: e.activation(out=lam4[:, 0:2], in_=lam4[:, 0:2], func=AF.Exp), reads=[lam4b], writes=[lam4b])
        P.op("dve", lambda e: e.tensor_tensor(out=lam4[:, 2:3], in0=lam4[:, 0:1], in1=lam4[:, 1:2], op=ALU.subtract),
             reads=[lam4b], writes=[lam4b])
        P.op("dve", lambda e: e.tensor_tensor(out=lam4[:, 2:3], in0=lam4[:, 2:3], in1=lami_t[:, 0:1], op=ALU.add),
             reads=[lam4b, lamib], writes=[lam4b])
        P.op("dve", lambda e: e.tensor_scalar(out=lam4[:, 3:4], in0=lam4[:, 2:3], scalar1=-1.0, scalar2=None, op0=ALU.mult),
             reads=[lam4b], writes=[lam4b])
        oml = sb("oml", [128, 1], F32)
        omlb = Buf("oml")
        P.op("dve", lambda e: e.tensor_scalar(out=oml[:], in0=lami_t[:], scalar1=-1.0, scalar2=1.0, op0=ALU.mult, op1=ALU.add),
             reads=[lamib], writes=[omlb])
        P.op("dve", lambda e: e.tensor_scalar(out=gsub[:], in0=subg_t[:], scalar1=oml[:, 0:1], scalar2=None, op0=ALU.mult),
             reads=[subgb, omlb], writes=[gsubb])

        def causal_pass(ka, kab, krows, qa, qab, qrows_sel, va, vab, bias_idx, c, extra_mm=None):
            accs = [ps_a.next() for _ in range(4)]
            nk = 4 * c + 4
            for kt in range(nk):
                qlo = max(0, kt - 4 * c)
                lo = qlo * 128
                ps, pb = ps_s.next()
                diag = kt >= 4 * c
                nmm = 1 + (1 if extra_mm is not None else 0) + (1 if diag else 0)
                P.op("pe", lambda e, ps=ps, kt=kt, lo=lo: e.matmul(
                    ps[:, lo:512], lhsT=ka[0:krows, kt * 128:(kt + 1) * 128], rhs=qrows_sel(lo),
                    start=True, stop=(nmm == 1)), reads=[kab, qab], writes=[pb])
                k_done = 1
                if extra_mm is not None:
                    k_done += 1
                    extra_mm(ps, pb, kt, lo, k_done == nmm)
                if diag:
                    P.op("pe", lambda e, ps=ps, lo=lo: e.matmul(
                        ps[:, lo:lo + 128], lhsT=ident[:], rhs=causal[:], start=False, stop=True),
                        reads=[identb, causalb], writes=[pb])
                pt, ptb = ptpool.next()
                exp_act(pt, ps, lo, 512, 128, biasT[:, bias_idx, kt - 4 * c + 28:kt - 4 * c + 29], [pb], [ptb])
                for q4 in range(qlo, 4):
                    acc, accb = accs[q4]
                    P.op("pe", lambda e, acc=acc, pt=pt, q4=q4, kt=kt: e.matmul(
                        acc[:, 0:129], lhsT=pt[:, q4 * 128:(q4 + 1) * 128], rhs=va[:, kt, :],
                        start=(kt == 0), stop=(kt == 4 * c + q4)), reads=[ptb, vab], writes=[accb])
            return accs

        for hl in range(2):
            va, vab = vpool.next()
            P.dma("sp", va[:], dva[hl], writes=[vab])
            kas = []
            for m in range(2):
                ka, kab = kpool.next()
                P.dma("sp", ka[0:66, :], dka[hl, m], writes=[kab])
                kas.append((ka, kab))
            for c in range(NCH):
                qa, qab = qpool.next()
                for m in range(2):
                    P.dma("sp", qa[0:66, m, :], dqa[hl, m, :, c * 512:(c + 1) * 512], writes=[qab])
                o0s = []
                for m in range(2):
                    ka, kab = kas[m]
                    accs = causal_pass(ka, kab, 66, qa, qab, (lambda lo, qa=qa, m=m: qa[0:66, m, lo:512]),
                                       va, vab, hl, c)
                    for q4 in range(4):
                        acc, accb = accs[q4]
                        rinv, rinvb = smallp.next()
                        P.op("dve", lambda e, acc=acc, rinv=rinv: e.reciprocal(out=rinv[:, 0:1], in_=acc[:, 128:129]),
                             reads=[accb], writes=[rinvb])
                        if m == 0:
                            o0, o0b = o32p.next()
                            P.op("dve", lambda e, acc=acc, rinv=rinv, o0=o0: e.tensor_scalar(
                                out=o0[:], in0=acc[:, 0:128], scalar1=rinv[:, 0:1], scalar2=None, op0=ALU.mult),
                                reads=[accb, rinvb], writes=[o0b])
                            o0s.append((o0, o0b))
                        else:
                            o0, o0b = o0s[q4]
                            P.op("dve", lambda e, rinv=rinv: e.tensor_tensor(
                                out=rinv[:, 0:1], in0=rinv[:, 0:1], in1=lam4[:, 3:4], op=ALU.mult),
                                reads=[rinvb, lam4b], writes=[rinvb])
                            P.op("dve", lambda e, acc=acc, rinv=rinv, o0=o0: e.scalar_tensor_tensor(
                                out=o0[:], in0=acc[:, 0:128], scalar=rinv[:, 0:1], in1=o0[:], op0=ALU.mult, op1=ALU.add),
                                reads=[accb, rinvb, o0b], writes=[o0b])
                            sqj, sqjb = o32p.next()
                            ss, ssb = smallp.next()
                            P.op("dve", lambda e, sqj=sqj, o0=o0: e.tensor_tensor(
                                out=sqj[:], in0=o0[:], in1=o0[:], op=ALU.mult), reads=[o0b], writes=[sqjb])
                            P.op("dve", lambda e, sqj=sqj, ss=ss: e.reduce_sum(
                                out=ss[:, 0:1], in_=sqj[:], axis=AX.X), reads=[sqjb], writes=[ssb])
                            P.op("act", lambda e, ss=ss: e.activation(
                                out=ss[:, 1:2], in_=ss[:, 0:1], func=AF.Sqrt, bias=epsc[:], scale=1.0 / 128.0),
                                reads=[ssb, epsb], writes=[ssb])
                            P.op("dve", lambda e, ss=ss: e.reciprocal(out=ss[:, 2:3], in_=ss[:, 1:2]),
                                 reads=[ssb], writes=[ssb])
                            ob, obb = obfp.next()
                            P.op("dve", lambda e, o0=o0, ss=ss, ob=ob: e.scalar_tensor_tensor(
                                out=ob[:], in0=o0[:], scalar=ss[:, 2:3], in1=gsub[:], op0=ALU.mult, op1=ALU.mult),
                                reads=[o0b, ssb, gsubb], writes=[obb])
                            q0 = c * 512 + q4 * 128
                            P.dma("sp", o_diff[q0:q0 + 128, hl, :], ob[:], reads=[obb], is_output=True)

        f256 = RPool(es, nc, "f256", [128, 256], F32, 4)
        impp = RPool(es, nc, "imp", [128, 64], F32, 10)
        onsa = RPool(es, nc, "onsa", [128, 128], F32, 18)
        w1_t, w1b, w2_t, w2b = [], [], [], []
        w1_shared = sb("w1_s", [128, 32, 128], BF16)
        w1_sb = Buf("w1_s")
        for i in range(2):
            w1_t.append(w1_shared)
            w1b.append(w1_sb)
            t = sb("w2_%d" % i, [128, 128], BF16)
            b = Buf("w2_%d" % i)
            P.dma("pool", t[:], w2[i], writes=[b])
            w2_t.append(t)
            w2b.append(b)
        posT_t = sb("posT_t", [128, 2, 32], BF16)
        posTb = Buf("posT")
        P.dma("pool", posT_t[:], posT.rearrange("k d l -> d k l"), writes=[posTb])
        kcmpT = sb("kcmpT", [128, 256], BF16)
        kcmpTb = Buf("kcmpT")
        vaug2 = sb("vaug2", [128, 2, 193], BF16)
        vaug2b = Buf("vaug2")
        P.dma("sp", vaug2[:, :, 128:193], c_ovl.rearrange("ct p e -> p ct e"), writes=[vaug2b])
        selT = sb("selT", [66, S], BF16)
        selTb = Buf("selT")
        P.dma("sp", selT[64:66, :], c_selaug, writes=[selTb])
        hilo = sb("hilo", [2, 512], BF16)
        hilob = Buf("hilo")
        P.dma("sp", hilo[:], c_selaug[:, 0:512], writes=[hilob])
        slrow = sb("slrow", [2, 2, 128], BF16)
        slrowb = Buf("slrow")
        P.dma("sp", slrow[:], c_slrow.rearrange("h r k -> r h k"), writes=[slrowb])
        eaug, eaugb = [], []
        for hl in range(2):
            t = sb("eaug%d" % hl, [66, S], BF16)
            b = Buf("eaug%d" % hl)
            P.dma("sp", t[:], c_eaug[hl], writes=[b])
            eaug.append(t)
            eaugb.append(b)
        gate_t = sb("gate_t", [128, NQT, 6], F32)
        gateb = Buf("gate")
        P.dma("sp", gate_t[:], ngate, writes=[gateb])

        for i in range(2):
            src, srcb = kpool.next()
            P.dma("sp", src[:], kcs[i], writes=[srcb])
            P.dma("pool", w1_shared[:], w1[i].rearrange("(l d) j -> d l j", d=128), writes=[w1_sb])
            ps, pb = ps_s.next()
            for l in range(32):
                P.op("pe", lambda e, ps=ps, l=l, i=i: e.matmul(ps[:, 0:1], lhsT=w1_t[i][:, l, :], rhs=posT_t[:, i, l:l + 1],
                                                                start=(l == 0), stop=(l == 31)),
                     reads=[w1b[i], posTb], writes=[pb])
            pbias, pbiasb = smallp.next()
            P.op("dve", lambda e, ps=ps, pbias=pbias: e.tensor_copy(out=pbias[:, 0:1], in_=ps[:, 0:1]),
                 reads=[pb], writes=[pbiasb])
            ps, pb = ps_s.next()
            for l in range(32):
                P.op("pe", lambda e, ps=ps, l=l, i=i, src=src: e.matmul(
                    ps[:, 0:NCMP], lhsT=w1_t[i][:, l, :], rhs=src[:, l:l + 16 * (NCMP - 1) + 1:16],
                    start=(l == 0), stop=(l == 31)), reads=[w1b[i], srcb], writes=[pb])
            pre, preb = f256.next()
            tq, tqb = f256.next()
            P.op("dve", lambda e, ps=ps, pre=pre, pbias=pbias: e.tensor_scalar(
                out=pre[:, 0:NCMP], in0=ps[:, 0:NCMP], scalar1=pbias[:, 0:1], scalar2=None, op0=ALU.add),
                reads=[pb, pbiasb], writes=[preb])
            P.op("dve", lambda e, pre=pre, tq=tq: e.tensor_tensor(out=tq[:, 0:NCMP], in0=pre[:, 0:NCMP], in1=pre[:, 0:NCMP],
                                                                  op=ALU.mult), reads=[preb], writes=[tqb])
            P.op("dve", lambda e, tq=tq: e.tensor_scalar(out=tq[:, 0:NCMP], in0=tq[:, 0:NCMP], scalar1=0.044715, scalar2=1.0,
                                                         op0=ALU.mult, op1=ALU.add), reads=[tqb], writes=[tqb])
            P.op("dve", lambda e, pre=pre, tq=tq: e.tensor_tensor(out=tq[:, 0:NCMP], in0=tq[:, 0:NCMP], in1=pre[:, 0:NCMP],
                                                                  op=ALU.mult), reads=[preb, tqb], writes=[tqb])
            P.op("act", lambda e, tq=tq: e.activation(out=tq[:, 0:NCMP], in_=tq[:, 0:NCMP], func=AF.Sigmoid,
                                                      scale=1.5957691216057308), reads=[tqb], writes=[tqb])
            gl, glb = ptpool.next()
            P.op("dve", lambda e, pre=pre, tq=tq, gl=gl: e.tensor_tensor(out=gl[:, 0:NCMP], in0=tq[:, 0:NCMP],
                                                                          in1=pre[:, 0:NCMP], op=ALU.mult),
                 reads=[preb, tqb], writes=[glb])
            if i == 0:
                ps2, pb2 = ps_s.next()
                P.op("pe", lambda e, ps2=ps2, gl=gl: e.matmul(ps2[:, 0:NCMP], lhsT=w2_t[0][:], rhs=gl[:, 0:NCMP],
                                                              start=True, stop=True), reads=[w2b[0], glb], writes=[pb2])
                P.op("act", lambda e, ps2=ps2: e.activation(out=kcmpT[:, 0:NCMP], in_=ps2[:, 0:NCMP], func=AF.Copy),
                     reads=[pb2], writes=[kcmpTb])
            else:
                for ct, n in ((0, 128), (1, 127)):
                    ps2, pb2 = ps_s.next()
                    P.op("pe", lambda e, ps2=ps2, gl=gl, ct=ct, n=n: e.matmul(
                        ps2[0:n, 0:128], lhsT=gl[:, ct * 128:ct * 128 + n], rhs=w2_t[1][:], start=True, stop=True),
                        reads=[w2b[1], glb], writes=[pb2])
                    P.op("act", lambda e, ps2=ps2, ct=ct, n=n: e.activation(
                        out=vaug2[0:n, ct, 0:128], in_=ps2[0:n, 0:128], func=AF.Copy), reads=[pb2], writes=[vaug2b])

        kslc_t, kslcb = kpool.next()
        P.dma("sp", kslc_t[:], kslc, writes=[kslcb])
        kwin_t, kwinb = kpool.next()
        P.dma("sp", kwin_t[:], kwin, writes=[kwinb])
        vslc_t, vslcb = vpool.next()
        P.dma("sp", vslc_t[:], vslc, writes=[vslcb])
        vwin_t, vwinb = vpool.next()
        P.dma("sp", vwin_t[:], vwin, writes=[vwinb])

        for c in range(NCH):
            qn, qnb = qpool.next()
            P.dma("sp", qn[:], nq[:, :, c * 512:(c + 1) * 512].rearrange("h p t -> p h t"), writes=[qnb])
            imps = [impp.next() for _ in range(4)]
            oaccs = [[onsa.next() for _ in range(4)] for _ in range(2)]
            cts = [0] if c < 4 else [0, 1]
            for hh in range(4):
                accs = [ps_a.next() for _ in range(4)]
                for ct in cts:
                    n = 128 if ct == 0 else 127
                    need_mask = (ct == 1) or (c <= 4)
                    off = 512 * c - 2048 * ct
                    ps, pb = ps_s.next()
                    P.op("pe", lambda e, ps=ps, ct=ct, n=n, hh=hh, qn=qn, need_mask=need_mask: e.matmul(
                        ps[0:n, :], lhsT=kcmpT[:, ct * 128:ct * 128 + n], rhs=qn[:, hh, :], start=True,
                        stop=(not need_mask)), reads=[kcmpTb, qnb], writes=[pb])
                    if need_mask:
                        P.op("pe", lambda e, ps=ps, n=n, off=off: e.matmul(
                            ps[0:n, :], lhsT=ident[:, 0:n], rhs=tneg[:, off:off + 512], start=False, stop=True),
                            reads=[identb, tnegb], writes=[pb])
                    pt, ptb = ptpool.next()
                    exp_act(pt, ps, 0, 512, n, None, [pb], [ptb])
                    for q4 in range(4):
                        acc, accb = accs[q4]
                        P.op("pe", lambda e, acc=acc, pt=pt, q4=q4, ct=ct, n=n: e.matmul(
                            acc[:, 0:193], lhsT=pt[0:n, q4 * 128:(q4 + 1) * 128], rhs=vaug2[0:n, ct, :],
                            start=(ct == cts[0]), stop=(ct == cts[-1])), reads=[ptb, vaug2b], writes=[accb])
                for q4 in range(4):
                    acc, accb = accs[q4]
                    qg = 4 * c + q4
                    rs, rsb = smallp.next()
                    P.op("dve", lambda e, acc=acc, rs=rs: e.tensor_scalar(
                        out=rs[:, 0:1], in0=acc[:, 128:129], scalar1=1e-30, scalar2=None, op0=ALU.max),
                        reads=[accb], writes=[rsb])
                    P.op("dve", lambda e, rs=rs: e.reciprocal(out=rs[:, 1:2], in_=rs[:, 0:1]), reads=[rsb], writes=[rsb])
                    im, imb = imps[q4]
                    if hh == 0:
                        P.op("dve", lambda e, acc=acc, rs=rs, im=im: e.tensor_scalar(
                            out=im[:], in0=acc[:, 129:193], scalar1=rs[:, 1:2], scalar2=None, op0=ALU.mult),
                            reads=[accb, rsb], writes=[imb])
                    else:
                        P.op("dve", lambda e, acc=acc, rs=rs, im=im: e.scalar_tensor_tensor(
                            out=im[:], in0=acc[:, 129:193], scalar=rs[:, 1:2], in1=im[:], op0=ALU.mult, op1=ALU.add),
                            reads=[accb, rsb, imb], writes=[imb])
                    if hh < 2:
                        oa, oab = oaccs[hh][q4]
                        P.op("dve", lambda e, rs=rs, qg=qg, hh=hh: e.tensor_tensor(
                            out=rs[:, 2:3], in0=rs[:, 1:2], in1=gate_t[:, qg, hh * 3:hh * 3 + 1], op=ALU.mult),
                            reads=[rsb, gateb], writes=[rsb])
                        P.op("dve", lambda e, acc=acc, rs=rs, oa=oa: e.tensor_scalar(
                            out=oa[:], in0=acc[:, 0:128], scalar1=rs[:, 2:3], scalar2=None, op0=ALU.mult),
                            reads=[accb, rsb], writes=[oab])
            for q4 in range(4):
                qg = 4 * c + q4
                im, imb = imps[q4]
                adj, adjb = impp.next()
                P.op("dve", lambda e, im=im, adj=adj, qg=qg: e.tensor_tensor(out=adj[:], in0=im[:], in1=ufix[:, qg, :],
                                                                             op=ALU.add), reads=[imb, ufixb], writes=[adjb])
                t8, t8b = smallp.next()
                P.op("dve", lambda e, adj=adj, t8=t8: e.max(out=t8[:, 0:8], in_=adj[:]), reads=[adjb], writes=[t8b])
                thr, thrb = smallp.next()
                P.op("dve", lambda e, t8=t8, thr=thr: e.tensor_reduce(out=thr[:, 0:1], in_=t8[:, 0:8], axis=AX.X, op=ALU.min),
                     reads=[t8b], writes=[thrb])
                selm, selmb = obfp.next()
                P.op("dve", lambda e, adj=adj, thr=thr, selm=selm: e.tensor_scalar(
                    out=selm[:, 0:64], in0=adj[:], scalar1=thr[:, 0:1], scalar2=1.0, op0=ALU.is_ge, op1=ALU.subtract),
                    reads=[adjb, thrb], writes=[selmb])
                P.op("pe", lambda e, selm=selm, q4=q4: e.transpose(out=pst[0:64, q4 * 128:(q4 + 1) * 128], in_=selm[:, 0:64],
                                                                    identity=ident[:]),
                     reads=[selmb, identb], writes=[pstb])
            P.op("act", lambda e, c=c: e.activation(out=selT[0:64, c * 512:(c + 1) * 512], in_=pst[0:64, 0:512], func=AF.Copy),
                 reads=[pstb], writes=[selTb])
            for hl in range(2):
                def mask_mm(ps, pb, kt, lo, last, hl=hl, c=c):
                    P.op("pe", lambda e: e.matmul(
                        ps[:, lo:512], lhsT=eaug[hl][0:66, kt * 128:(kt + 1) * 128],
                        rhs=selT[0:66, c * 512 + lo:(c + 1) * 512], start=False, stop=last),
                        reads=[eaugb[hl], selTb], writes=[pb])
                accs = causal_pass(kslc_t, kslcb, 128, qn, qnb, (lambda lo, qn=qn, hl=hl: qn[:, hl, lo:512]),
                                   vslc_t, vslcb, 2 + hl, c, extra_mm=mask_mm)
                for q4 in range(4):
                    acc, accb = accs[q4]
                    qg = 4 * c + q4
                    rs, rsb = smallp.next()
                    P.op("dve", lambda e, acc=acc, rs=rs: e.reciprocal(out=rs[:, 0:1], in_=acc[:, 128:129]),
                         reads=[accb], writes=[rsb])
                    P.op("dve", lambda e, rs=rs, qg=qg, hl=hl: e.tensor_tensor(
                        out=rs[:, 1:2], in0=rs[:, 0:1], in1=gate_t[:, qg, hl * 3 + 1:hl * 3 + 2], op=ALU.mult),
                        reads=[rsb, gateb], writes=[rsb])
                    oa, oab = oaccs[hl][q4]
                    P.op("dve", lambda e, acc=acc, rs=rs, oa=oa: e.scalar_tensor_tensor(
                        out=oa[:], in0=acc[:, 0:128], scalar=rs[:, 1:2], in1=oa[:], op0=ALU.mult, op1=ALU.add),
                        reads=[accb, rsb, oab], writes=[oab])
            for hl in range(2):
                accs = [ps_a.next() for _ in range(4)]
                for kt in range(max(0, 4 * c - 4), 4 * c + 4):
                    qlo = max(kt, 4 * c) - 4 * c
                    qhi = min(kt + 4, 4 * c + 3) - 4 * c
                    lo, hi = qlo * 128, (qhi + 1) * 128
                    diag = kt >= 4 * c
                    far = (kt + 4 >= 4 * c) and (kt + 4 <= 4 * c + 3)
                    ps, pb = ps_s.next()
                    P.op("pe", lambda e, ps=ps, kt=kt, lo=lo, hi=hi, hl=hl, qn=qn: e.matmul(
                        ps[:, lo:hi], lhsT=kwin_t[:, kt * 128:(kt + 1) * 128], rhs=qn[:, hl, lo:hi],
                        start=True, stop=False), reads=[kwinb, qnb], writes=[pb])
                    P.op("pe", lambda e, ps=ps, lo=lo, hi=hi, hl=hl: e.matmul(
                        ps[:, lo:hi], lhsT=slrow[0:2, hl, :], rhs=hilo[0:2, lo:hi],
                        start=False, stop=(not diag and not far)), reads=[slrowb, hilob], writes=[pb])
                    if diag:
                        q4d = kt - 4 * c
                        P.op("pe", lambda e, ps=ps, q4d=q4d, far=far: e.matmul(
                            ps[:, q4d * 128:(q4d + 1) * 128], lhsT=ident[:], rhs=causal[:], start=False, stop=(not far)),
                            reads=[identb, causalb], writes=[pb])
                    if far:
                        q4f = kt + 4 - 4 * c
                        P.op("pe", lambda e, ps=ps, q4f=q4f: e.matmul(
                            ps[:, q4f * 128:(q4f + 1) * 128], lhsT=ident[:], rhs=winneg[:], start=False, stop=True),
                            reads=[identb, winnegb], writes=[pb])
                    pt, ptb = ptpool.next()
                    exp_act(pt, ps, lo, hi, 128, biasT[:, 2 + hl, kt - 4 * c + 28:kt - 4 * c + 29], [pb], [ptb])
                    for q4 in range(qlo, qhi + 1):
                        acc, accb = accs[q4]
                        qg = 4 * c + q4
                        P.op("pe", lambda e, acc=acc, pt=pt, q4=q4, kt=kt, qg=qg: e.matmul(
                            acc[:, 0:129], lhsT=pt[:, q4 * 128:(q4 + 1) * 128], rhs=vwin_t[:, kt, :],
                            start=(kt == max(0, qg - 4)), stop=(kt == qg)), reads=[ptb, vwinb], writes=[accb])
                for q4 in range(4):
                    acc, accb = accs[q4]
                    qg = 4 * c + q4
                    rs, rsb = smallp.next()
                    P.op("dve", lambda e, acc=acc, rs=rs: e.reciprocal(out=rs[:, 0:1], in_=acc[:, 128:129]),
                         reads=[accb], writes=[rsb])
                    P.op("dve", lambda e, rs=rs, qg=qg, hl=hl: e.tensor_tensor(
                        out=rs[:, 1:2], in0=rs[:, 0:1], in1=gate_t[:, qg, hl * 3 + 2:hl * 3 + 3], op=ALU.mult),
                        reads=[rsb, gateb], writes=[rsb])
                    oa, oab = oaccs[hl][q4]
                    ob, obb = obfp.next()
                    P.op("dve", lambda e, acc=acc, rs=rs, oa=oa, ob=ob: e.scalar_tensor_tensor(
                        out=ob[:], in0=acc[:, 0:128], scalar=rs[:, 1:2], in1=oa[:], op0=ALU.mult, op1=ALU.add),
                        reads=[accb, rsb, oab], writes=[obb])
                    q0 = qg * 128
                    P.dma("sp", o_nsa[q0:q0 + 128, hl, :], ob[:], reads=[obb], is_output=True)
        P.emit()
    return nc


import ml_dtypes

BF = ml_dtypes.bfloat16
_NC = {}
DEBUG = {}


def _prog(name):
    if name not in _NC:
        _NC[name] = build_token(name) if name in ("pre", "post") else build_attn()
    return _NC[name]


def _run(name, in_maps):
    res = run_bass_kernel_spmd(_prog(name), in_maps, core_ids=list(range(8)))
    return res.results


def _lay(g):
    return np.ascontiguousarray(np.asarray(g, np.float32).reshape(NKT, 128).T)


def _tile_tok(a):
    e = a.shape[1]
    return np.ascontiguousarray(a.reshape(NQT, 128, e).transpose(1, 0, 2))


_CONST = {}


def _consts():
    if _CONST:
        return _CONST
    kk = np.arange(128)[:, None]
    qq = np.arange(128)[None, :]
    _CONST["c_ident"] = np.eye(128, dtype=np.float32).astype(BF)
    _CONST["c_causal"] = np.where(kk > qq, NEG, 0.0).astype(np.float32).astype(BF)
    _CONST["c_winneg"] = np.where(kk <= qq, NEG, 0.0).astype(np.float32).astype(BF)
    z = np.arange(2560)[None, :]
    _CONST["c_tneg"] = np.where(16 * kk + 31 <= z, 0.0, NEG).astype(np.float32).astype(BF)
    p = np.arange(128)[:, None, None]
    qg = np.arange(NQT)[None, :, None]
    j = np.arange(64)[None, None, :]
    cur = 2 * qg + (p >= 64)
    forced = (j == 0) | (j == cur) | (j == cur - 1)
    future = j > cur
    _CONST["c_ufix"] = np.where(forced, BIGSEL, np.where(future, -BIGSEL, 0.0)).astype(np.float32)
    q512 = np.arange(S) % 512
    hi = (q512 // 2) * 2
    lo = q512 % 2
    _CONST["c_selaug"] = (-np.stack([hi, lo]).astype(np.float32)).astype(BF)
    c = np.arange(256)[:, None]
    jj = np.arange(64)[None, :]
    ovl = ((16 * c < 64 * jj + 64) & (16 * c + 32 > 64 * jj) & (c < NCMP)).astype(np.float32)
    o65 = np.concatenate([(c < NCMP).astype(np.float32), ovl], axis=1)
    _CONST["c_ovl"] = o65.reshape(2, 128, 65).astype(BF)
    col = np.arange(S)[None, :]
    _CONST["erows"] = np.where(col // 64 == np.arange(64)[:, None], 30000.0, 0.0).astype(np.float32)
    return _CONST


def _slope(h):
    return 2.0 ** (-(h + 1.0))


def kernel(**inputs):
    inp = {k: np.asarray(v) for k, v in inputs.items()}
    x = inp["x"].astype(np.float32).reshape(8 * TOK, D)
    mem = inp["mem"].astype(np.float32)
    cst = _consts()
    xT = [np.ascontiguousarray(x[c * TOK:(c + 1) * TOK].T) for c in range(8)]
    depth = inp["w_in"].shape[0]
    yT = None
    for l in range(depth):
        maps = [{"xT": xT[c], "gn": _lay(inp["ffn1_norm"][l]), "gm": _lay(inp["mix_norm"][l]),
                 "wg": inp["ffn1_w_gate"][l], "wu": inp["ffn1_w_up"][l], "wd": inp["ffn1_w_down"][l],
                 "win": inp["w_in"][l]} for c in range(8)]
        r1 = _run("pre", maps)
        x1T = [r["xoT"] for r in r1]
        gT = [r["gT"] for r in r1]
        lam_init = 0.8 - 0.6 * math.exp(-0.3 * l)
        maps = []
        for b in range(2):
            featB = np.concatenate([np.asarray(r1[4 * b + i]["featT"]) for i in range(4)], axis=1)
            tokVB = np.concatenate([np.asarray(r1[4 * b + i]["tokV"]) for i in range(4)], axis=0)
            gatesB = np.concatenate([np.asarray(r1[4 * b + i]["gates"]) for i in range(4)], axis=0)
            onesc = np.ones((S, 1), BF)
            for r in range(4):
                g = r // 2
                m = {}
                dqa = np.zeros((2, 2, 66, S), BF)
                dka = np.zeros((2, 2, 66, S), BF)
                dva = np.zeros((2, 128, NQT, 129), BF)
                for hl in range(2):
                    h = 2 * r + hl
                    for mm in range(2):
                        dqa[hl, mm, 0:64] = featB[h * 128 + mm * 64:h * 128 + mm * 64 + 64]
                        dqa[hl, mm, 64:66] = cst["c_selaug"]
                        dka[hl, mm, 0:64] = featB[1024 + h * 128 + mm * 64:1024 + h * 128 + mm * 64 + 64]
                        dka[hl, mm, 64:66] = np.float32(_slope(h))
                    dva[hl] = _tile_tok(np.concatenate([tokVB[:, h * 128:(h + 1) * 128], onesc], axis=1))
                m["dqa"], m["dka"], m["dva"] = dqa, dka, dva
                m["lamp"] = np.ascontiguousarray(np.broadcast_to(inp["diff_lambda"][l].astype(np.float32)[None], (128, 4, 64)))
                m["subg"] = np.ascontiguousarray(np.broadcast_to(inp["diff_subln"][l].astype(np.float32)[None], (128, 128)))
                m["lami"] = np.full((128, 1), lam_init, np.float32)
                outh = [2 * r, 2 * r + 1]
                hh_list = outh + [hh for hh in range(4 * g, 4 * g + 4) if hh not in outh]
                m["nq"] = np.stack([featB[2048 + hh * 128:2048 + (hh + 1) * 128] for hh in hh_list])
                m["kcs"] = np.stack([featB[3072 + g * 128:3072 + (g + 1) * 128], featB[3328 + g * 128:3328 + (g + 1) * 128]])
                m["kslc"] = np.ascontiguousarray(featB[3584 + g * 128:3584 + (g + 1) * 128])
                m["kwin"] = np.ascontiguousarray(featB[3840 + g * 128:3840 + (g + 1) * 128])
                m["vslc"] = _tile_tok(np.concatenate([tokVB[:, 1024 + g * 128:1024 + (g + 1) * 128], onesc], axis=1))
                m["vwin"] = _tile_tok(np.concatenate([tokVB[:, 1280 + g * 128:1280 + (g + 1) * 128], onesc], axis=1))
                m["ngate"] = _tile_tok(np.concatenate([gatesB[:, hh * 3:hh * 3 + 3] for hh in outh], axis=1))
                m["w1"] = inp["nsa_cmp_w1"][l].astype(np.float32)
                m["w2"] = inp["nsa_cmp_w2"][l].astype(np.float32)
                m["posT"] = np.ascontiguousarray(inp["nsa_cmp_pos"][l].astype(np.float32).transpose(0, 2, 1))
                m["mq"] = np.ascontiguousarray(featB[4096 + r * 128:4096 + (r + 1) * 128])
                m["memT"] = np.ascontiguousarray(mem[b].T)
                m["gmem"] = _lay(inp["mem_norm"][l])
                m["wmk"] = np.ascontiguousarray(inp["w_mem_kv"][l][:, r * 128:(r + 1) * 128].astype(np.float32))
                m["wmv"] = np.ascontiguousarray(inp["w_mem_kv"][l][:, 512 + r * 128:512 + (r + 1) * 128].astype(np.float32))
                for k in ("c_ident", "c_causal", "c_winneg", "c_tneg", "c_ufix", "c_selaug", "c_ovl"):
                    m[k] = cst[k]
                eaug = np.zeros((2, 66, S), np.float32)
                slrow = np.zeros((2, 2, 128), np.float32)
                bias = np.zeros((128, 4, 32), np.float32)
                pp = np.arange(128, dtype=np.float32)[:, None]
                idx = np.arange(32, dtype=np.float32)[None, :]
                for hl in range(2):
                    eaug[hl, 0:64] = cst["erows"]
                    eaug[hl, 64:66] = _slope(outh[hl])
                    slrow[hl] = _slope(outh[hl])
                    bias[:, hl, :] = _slope(2 * r + hl) * (pp + 128.0 * (idx - 28.0))
                    bias[:, 2 + hl, :] = _slope(outh[hl]) * (pp + 128.0 * (idx - 28.0))
                m["c_eaug"] = eaug.astype(BF)
                m["c_slrow"] = slrow.astype(BF)
                m["c_bias"] = bias
                maps.append(m)
        r2 = _run("attn", maps)
        if DEBUG.get("on"):
            DEBUG["r1_%d" % l] = r1
            DEBUG["r2_%d" % l] = r2
        maps = []
        for c in range(8):
            b, tq = c // 4, c % 4
            sl = slice(tq * TOK, (tq + 1) * TOK)
            parts = [np.asarray(r2[4 * b + r]["o_diff"])[sl].reshape(TOK, 256) for r in range(4)]
            parts += [np.asarray(r2[4 * b + r]["o_nsa"])[sl].reshape(TOK, 256) for r in range(4)]
            parts += [np.asarray(r2[4 * b + r]["o_mem"])[sl] for r in range(4)]
            oT = np.ascontiguousarray(np.concatenate(parts, axis=1).T)
            maps.append({"xT": x1T[c], "gn": _lay(inp["ffn2_norm"][l]), "wg": inp["ffn2_w_gate"][l],
                         "wu": inp["ffn2_w_up"][l], "wd": inp["ffn2_w_down"][l], "oT": oT, "gT": gT[c],
                         "wud": inp["w_up_diff"][l], "wun": inp["w_up_nsa"][l], "wum": inp["w_up_mem"][l],
                         "wo": inp["w_out"][l], "gf": _lay(inp["final_norm"])})
        r3 = _run("post", maps)
        xT = [r["xoT"] for r in r3]
        yT = [r["yT"] for r in r3]
        if DEBUG.get("on"):
            DEBUG["r3_%d" % l] = r3
            if DEBUG.get("stop_after") == l:
                break
    out = np.concatenate([np.asarray(y).T for y in yT], axis=0).reshape(2, S, D).astype(np.float32)
    return out
```

```python
import math
from contextlib import ExitStack
import numpy as np
import concourse.bass as bass
import concourse.mybir as mybir
from concourse.bass_utils import run_bass_kernel_spmd

F32 = mybir.dt.float32
BF16 = mybir.dt.bfloat16
AF = mybir.ActivationFunctionType
ALU = mybir.AluOpType
AX = mybir.AxisListType

ENGS = ("pe", "act", "dve", "pool", "sp")
N_DMA_SLOTS = 8


class Buf:
    __slots__ = ("name", "w", "r")

    def __init__(self, name):
        self.name = name
        self.w = None
        self.r = []


class Op:
    __slots__ = ("id", "eng", "fn", "deps", "is_dma", "slot", "semval", "prev_slot_op",
                 "signaled", "sigval")

    def __init__(self, id, eng, fn, is_dma):
        self.id = id
        self.eng = eng
        self.fn = fn
        self.deps = []
        self.is_dma = is_dma
        self.slot = None
        self.semval = 0
        self.prev_slot_op = None
        self.signaled = False
        self.sigval = 0


class Prog:
    def __init__(self, nc):
        self.nc = nc
        self.ops = []
        self.by_eng = {e: [] for e in ENGS}
        self.slot_last = {e: [None] * N_DMA_SLOTS for e in ENGS}
        self.slot_cnt = {e: [0] * N_DMA_SLOTS for e in ENGS}
        self.slot_rr = {e: 0 for e in ENGS}
        self.out_dmas = []

    def _add(self, eng, fn, reads, writes, is_dma):
        op = Op(len(self.ops), eng, fn, is_dma)
        deps = set()
        for b in reads:
            if b.w is not None:
                deps.add(b.w)
        for b in writes:
            if b.w is not None:
                deps.add(b.w)
            for r in b.r:
                deps.add(r)
        for b in writes:
            b.w = op.id
            b.r = []
        for b in reads:
            if b.w != op.id:
                b.r.append(op.id)
        deps.discard(op.id)
        for d in sorted(deps):
            t = self.ops[d]
            if (not is_dma) and eng == "pe" and t.eng == "pe" and not t.is_dma:
                continue
            op.deps.append(d)
            if not t.is_dma:
                t.signaled = True
        if is_dma:
            s = self.slot_rr[eng]
            self.slot_rr[eng] = (s + 1) % N_DMA_SLOTS
            op.slot = s
            op.prev_slot_op = self.slot_last[eng][s]
            self.slot_cnt[eng][s] += 16
            op.semval = self.slot_cnt[eng][s]
            self.slot_last[eng][s] = op.id
        self.ops.append(op)
        self.by_eng[eng].append(op)
        return op

    def op(self, eng, fn, reads=(), writes=()):
        return self._add(eng, fn, reads, writes, False)

    def dma(self, eng, out, in_, reads=(), writes=(), is_output=False):
        op = self._add(eng, lambda e: e.dma_start(out=out, in_=in_), reads, writes, True)
        if is_output:
            self.out_dmas.append(op.id)
        return op

    def emit(self):
        nc = self.nc
        with ExitStack() as es:
            es.enter_context(nc.cleanup_on_exit())
            esem = {e: nc.alloc_semaphore(name="s_" + e) for e in ENGS}
            dsem = {e: [nc.alloc_semaphore(name="d_%s%d" % (e, i)) for i in range(N_DMA_SLOTS)]
                    for e in ("sp", "pool")}
            for e in ENGS:
                c = 0
                for op in self.by_eng[e]:
                    if op.signaled and not op.is_dma:
                        c += 1
                        op.sigval = c
            block = nc.Block()
            block.__enter__()
            ops = self.ops
            out_dmas = self.out_dmas

            def stream(eng_name, h):
                seen = {}

                def wait(sem, key, val):
                    if seen.get(key, 0) >= val:
                        return
                    h.wait_ge(sem, val)
                    seen[key] = val

                for op in self.by_eng[eng_name]:
                    for d in op.deps:
                        t = ops[d]
                        if t.is_dma:
                            wait(dsem[t.eng][t.slot], (t.eng, t.slot), t.semval)
                        else:
                            wait(esem[t.eng], t.eng, t.sigval)
                    if op.is_dma:
                        if op.prev_slot_op is not None:
                            p = ops[op.prev_slot_op]
                            wait(dsem[eng_name][op.slot], (eng_name, op.slot), p.semval)
                        ins = op.fn(h)
                        ins.then_inc(dsem[eng_name][op.slot], 16)
                    else:
                        ins = op.fn(h)
                        if op.signaled:
                            ins.then_inc(esem[eng_name], 1)
                if eng_name == "sp":
                    for d in out_dmas:
                        t = ops[d]
                        wait(dsem[t.eng][t.slot], (t.eng, t.slot), t.semval)
                    for q in ("sp", "pool"):
                        for s in range(N_DMA_SLOTS):
                            if self.slot_cnt[q][s] > 0:
                                wait(dsem[q][s], (q, s), self.slot_cnt[q][s])

            @block.tensor
            def _(e):
                stream("pe", e)

            @block.scalar
            def _(e):
                stream("act", e)

            @block.vector
            def _(e):
                stream("dve", e)

            @block.gpsimd
            def _(e):
                stream("pool", e)

            @block.sync
            def _(e):
                stream("sp", e)

            block.__exit__(None, None, None)
            nc.all_engine_barrier()


D = 2048
DFF = 5632
NKT = D // 128
NFT = DFF // 128
TOK = 1024
TC = 512
IN_COLS = 12312
EPS = 1e-6
S128 = 128.0 ** -0.5


class RPool:
    def __init__(self, es, nc, name, shape, dtype, n, psum=False):
        mk = nc.psum_tensor if psum else nc.sbuf_tensor
        self.t = [es.enter_context(mk("%s%d" % (name, i), shape, dtype)) for i in range(n)]
        self.b = [Buf("%s%d" % (name, i)) for i in range(n)]
        self.i = 0
        self.n = n

    def next(self):
        i = self.i
        self.i = (i + 1) % self.n
        return self.t[i], self.b[i]


def build_token(mode):
    nc = bass.Bass("TRN2", target_bir_lowering=False)

    def din(name, shape, dt=F32):
        return nc.dram_tensor(name, shape, dt, kind="ExternalInput").ap()

    def dout(name, shape, dt=F32):
        return nc.dram_tensor(name, shape, dt, kind="ExternalOutput").ap()

    xT = din("xT", [D, TOK])
    gn = din("gn", [128, NKT])
    wg = din("wg", [D, DFF])
    wu = din("wu", [D, DFF])
    wd = din("wd", [DFF, D])
    if mode == "pre":
        gm = din("gm", [128, NKT])
        win = din("win", [D, IN_COLS])
        xoT = dout("xoT", [D, TOK])
        featT = dout("featT", [36 * 128, TOK], BF16)
        gTo = dout("gT", [6144, TOK])
        tokV = dout("tokV", [TOK, 1536], BF16)
        gates = dout("gates", [TOK, 24])
    else:
        oT = din("oT", [2560, TOK], BF16)
        gTi = din("gT", [6144, TOK])
        wud = din("wud", [1024, D])
        wun = din("wun", [1024, D])
        wum = din("wum", [512, D])
        wo = din("wo", [D, D])
        gf = din("gf", [128, NKT])
        xoT = dout("xoT", [D, TOK])
        yT = dout("yT", [D, TOK])

    with ExitStack() as es:
        P = Prog(nc)
        sb = lambda name, shape, dt: es.enter_context(nc.sbuf_tensor(name, shape, dt))
        x_t = sb("x_t", [128, NKT, TC], F32)
        xb = [Buf("x%d" % i) for i in range(NKT)]
        h_t = sb("h_t", [128, NKT, TC], BF16)
        hb = [Buf("h%d" % i) for i in range(NKT)]
        hid = sb("hid", [128, NFT, TC], BF16)
        hidb = [Buf("hid%d" % i) for i in range(NFT)]
        ones = sb("ones", [128, 128], BF16)
        onesb = Buf("ones")
        gn_t = sb("gn_t", [128, NKT], F32)
        gnb = Buf("gn")
        g2_t = sb("g2_t", [128, NKT], F32)
        g2b = Buf("g2")
        wpool = RPool(es, nc, "wp", [128, NKT, 256], BF16, 4)
        wdpool = RPool(es, nc, "wdp", [128, NFT, 256], BF16, 2)
        psum = RPool(es, nc, "ps", [128, 512], F32, 8, psum=True)
        sqpool = RPool(es, nc, "sq", [128, TC], BF16, 3)
        f32pool = RPool(es, nc, "f32p", [128, TC], F32, 6)
        rstdpool = RPool(es, nc, "rstd", [128, TC], F32, 2)
        bfpool = RPool(es, nc, "bfp", [128, TC], BF16, 4)
        if mode == "post":
            gpool = RPool(es, nc, "gp", [128, 3, TC], F32, 2)

        epsc = sb("epsc", [128, 1], F32)
        epsb = Buf("eps")
        P.op("dve", lambda e: e.memset(ones[:], 1.0), writes=[onesb])
        P.op("dve", lambda e: e.memset(epsc[:], EPS), writes=[epsb])
        P.dma("sp", gn_t[:], gn, writes=[gnb])
        P.dma("sp", g2_t[:], gm if mode == "pre" else gf, writes=[g2b])

        xTv = xT.rearrange("(kt p) t -> p kt t", p=128)
        xoTv = xoT.rearrange("(kt p) t -> p kt t", p=128)
        wgv = wg.rearrange("(kt p) f -> p kt f", p=128)
        wuv = wu.rearrange("(kt p) f -> p kt f", p=128)
        wdv = wd.rearrange("(ft p) d -> p ft d", p=128)

        def load_w(view, nk, c0, w):
            wt, wb = wpool.next()
            P.dma("pool", wt[:, 0:nk, 0:w], view[:, :, c0:c0 + w], writes=[wb])
            return wt, wb

        def rmsnorm_stats(gain_unused=None):
            ps, pb = psum.next()
            for kt in range(NKT):
                sq, sqb = sqpool.next()
                P.op("act", lambda e, kt=kt, sq=sq: e.activation(out=sq[:], in_=x_t[:, kt, :], func=AF.Square),
                     reads=[xb[kt]], writes=[sqb])
                P.op("pe", lambda e, kt=kt, sq=sq, ps=ps: e.matmul(ps[:], lhsT=ones[:], rhs=sq[:],
                                                                    start=(kt == 0), stop=(kt == NKT - 1)),
                     reads=[sqb, onesb], writes=[pb])
            rstd, rb = rstdpool.next()
            P.op("act", lambda e, ps=ps, rstd=rstd: e.activation(out=rstd[:], in_=ps[:], func=AF.Sqrt, bias=epsc[:],
                                                                 scale=1.0 / D), reads=[pb, epsb], writes=[rb])
            P.op("dve", lambda e, rstd=rstd: e.reciprocal(out=rstd[:], in_=rstd[:]), reads=[rb], writes=[rb])
            return rstd, rb

        def rmsnorm_to_h(g_t, gb):
            rstd, rb = rmsnorm_stats()
            for kt in range(NKT):
                P.op("dve", lambda e, kt=kt, rstd=rstd: e.scalar_tensor_tensor(
                    out=h_t[:, kt, :], in0=x_t[:, kt, :], scalar=g_t[:, kt:kt + 1], in1=rstd[:],
                    op0=ALU.mult, op1=ALU.mult), reads=[xb[kt], rb, gb], writes=[hb[kt]])

        def ffn():
            rmsnorm_to_h(gn_t, gnb)
            for fc in range(DFF // 256):
                wgt, wgb = load_w(wgv, NKT, fc * 256, 256)
                wut, wub = load_w(wuv, NKT, fc * 256, 256)
                for j in range(2):
                    f = fc * 2 + j
                    pg, pgb = psum.next()
                    pu, pub = psum.next()
                    for kt in range(NKT):
                        P.op("pe", lambda e, kt=kt, pg=pg, wgt=wgt, j=j: e.matmul(
                            pg[:], lhsT=wgt[:, kt, j * 128:(j + 1) * 128], rhs=h_t[:, kt, :],
                            start=(kt == 0), stop=(kt == NKT - 1)), reads=[wgb, hb[kt]], writes=[pgb])
                    for kt in range(NKT):
                        P.op("pe", lambda e, kt=kt, pu=pu, wut=wut, j=j: e.matmul(
                            pu[:], lhsT=wut[:, kt, j * 128:(j + 1) * 128], rhs=h_t[:, kt, :],
                            start=(kt == 0), stop=(kt == NKT - 1)), reads=[wub, hb[kt]], writes=[pub])
                    sg, sgb = f32pool.next()
                    P.op("act", lambda e, sg=sg, pg=pg: e.activation(out=sg[:], in_=pg[:], func=AF.Silu),
                         reads=[pgb], writes=[sgb])
                    P.op("dve", lambda e, sg=sg, pu=pu, f=f: e.tensor_tensor(out=hid[:, f, :], in0=sg[:], in1=pu[:],
                                                                              op=ALU.mult),
                         reads=[sgb, pub], writes=[hidb[f]])
            for dc in range(D // 256):
                wdt, wdb = wdpool.next()
                half = NFT // 2
                P.dma("pool", wdt[:, 0:half, :], wdv[:, 0:half, dc * 256:(dc + 1) * 256], writes=[wdb])
                P.dma("pool", wdt[:, half:NFT, :], wdv[:, half:NFT, dc * 256:(dc + 1) * 256], writes=[wdb])
                for j in range(2):
                    d = dc * 2 + j
                    po, pob = psum.next()
                    for f in range(NFT):
                        P.op("pe", lambda e, f=f, po=po, wdt=wdt, j=j: e.matmul(
                            po[:], lhsT=wdt[:, f, j * 128:(j + 1) * 128], rhs=hid[:, f, :],
                            start=(f == 0), stop=(f == NFT - 1)), reads=[wdb, hidb[f]], writes=[pob])
                    P.op("dve", lambda e, po=po, d=d: e.scalar_tensor_tensor(
                        out=x_t[:, d, :], in0=po[:], scalar=0.5, in1=x_t[:, d, :], op0=ALU.mult, op1=ALU.add),
                        reads=[pob, xb[d]], writes=[xb[d]])

        for c in range(TOK // TC):
            t0 = c * TC
            for kt in range(NKT):
                P.dma("sp", x_t[:, kt, :], xTv[:, kt, t0:t0 + TC], writes=[xb[kt]])
            if mode == "pre":
                ffn()
                for kt in range(NKT):
                    P.dma("sp", xoTv[:, kt, t0:t0 + TC], x_t[:, kt, :], reads=[xb[kt]], is_output=True)
                rmsnorm_to_h(g2_t, g2b)
                winv = win.rearrange("(kt p) f -> p kt f", p=128)
                plan = []
                for i in range(4):
                    plan.append((i * 256, 256, "feat", i * 256, 0.125))
                for i in range(4):
                    plan.append((1024 + i * 256, 256, "feat", 1024 + i * 256, 1.0))
                for i in range(4):
                    plan.append((2048 + i * 256, 256, "tokv", i * 256, 1.0))
                for i in range(4):
                    plan.append((3072 + i * 256, 256, "feat", 2048 + i * 256, S128))
                plan.append((4096, 256, "feat", 3072, 1.0))
                plan.append((4352, 256, "feat", 3328, 1.0))
                plan.append((4608, 256, "feat", 3584, 1.0))
                plan.append((4864, 256, "tokv", 1024, 1.0))
                plan.append((5120, 256, "feat", 3840, 1.0))
                plan.append((5376, 256, "tokv", 1280, 1.0))
                plan.append((5632, 24, "gates", 0, 1.0))
                for i in range(2):
                    plan.append((5656 + i * 256, 256, "feat", 4096 + i * 256, S128))
                for i in range(24):
                    plan.append((6168 + i * 256, 256, "mg", i * 256, 1.0))
                for (c0, w, kind, dst, scale) in plan:
                    wt, wb = load_w(winv, NKT, c0, w)
                    if kind in ("feat", "mg"):
                        for j in range(w // 128):
                            ps, pb = psum.next()
                            for kt in range(NKT):
                                P.op("pe", lambda e, kt=kt, ps=ps, wt=wt, j=j: e.matmul(
                                    ps[:], lhsT=wt[:, kt, j * 128:(j + 1) * 128], rhs=h_t[:, kt, :],
                                    start=(kt == 0), stop=(kt == NKT - 1)), reads=[wb, hb[kt]], writes=[pb])
                            r0 = dst + j * 128
                            if kind == "feat":
                                st, stb = bfpool.next()
                                P.op("act", lambda e, st=st, ps=ps, scale=scale: e.activation(
                                    out=st[:], in_=ps[:], func=AF.Copy, scale=scale), reads=[pb], writes=[stb])
                                P.dma("sp", featT[r0:r0 + 128, t0:t0 + TC], st[:], reads=[stb], is_output=True)
                            else:
                                st, stb = f32pool.next()
                                P.op("act", lambda e, st=st, ps=ps: e.activation(
                                    out=st[:], in_=ps[:], func=AF.Sigmoid), reads=[pb], writes=[stb])
                                P.dma("sp", gTo[r0:r0 + 128, t0:t0 + TC], st[:], reads=[stb], is_output=True)
                    else:
                        for tt in range(TC // 128):
                            ps, pb = psum.next()
                            for kt in range(NKT):
                                P.op("pe", lambda e, kt=kt, ps=ps, wt=wt, tt=tt, w=w: e.matmul(
                                    ps[:, 0:w], lhsT=h_t[:, kt, tt * 128:(tt + 1) * 128], rhs=wt[:, kt, 0:w],
                                    start=(kt == 0), stop=(kt == NKT - 1)), reads=[wb, hb[kt]], writes=[pb])
                            if kind == "tokv":
                                st, stb = bfpool.next()
                                P.op("act", lambda e, st=st, ps=ps, w=w: e.activation(
                                    out=st[:, 0:w], in_=ps[:, 0:w], func=AF.Copy), reads=[pb], writes=[stb])
                                P.dma("sp", tokV[t0 + tt * 128:t0 + (tt + 1) * 128, dst:dst + w], st[:, 0:w],
                                      reads=[stb], is_output=True)
                            else:
                                st, stb = f32pool.next()
                                P.op("act", lambda e, st=st, ps=ps, w=w: e.activation(
                                    out=st[:, 0:w], in_=ps[:, 0:w], func=AF.Sigmoid), reads=[pb], writes=[stb])
                                P.dma("sp", gates[t0 + tt * 128:t0 + (tt + 1) * 128, 0:w], st[:, 0:w],
                                      reads=[stb], is_output=True)
            else:
                oTv = oT.rearrange("(kt p) t -> p kt t", p=128)
                gTv = gTi.rearrange("(br dt p) t -> p br dt t", br=3, p=128)
                wudv = wud.rearrange("(kt p) f -> p kt f", p=128)
                wunv = wun.rearrange("(kt p) f -> p kt f", p=128)
                wumv = wum.rearrange("(kt p) f -> p kt f", p=128)
                wov = wo.rearrange("(kt p) f -> p kt f", p=128)
                for kt in range(20):
                    P.dma("sp", hid[:, kt, :], oTv[:, kt, t0:t0 + TC], writes=[hidb[kt]])
                for dc in range(D // 256):
                    w0, w0b = load_w(wudv, 8, dc * 256, 256)
                    w1, w1b = load_w(wunv, 8, dc * 256, 256)
                    w2, w2b = load_w(wumv, 4, dc * 256, 256)
                    for j in range(2):
                        d = dc * 2 + j
                        gt, gtb = gpool.next()
                        P.dma("sp", gt[:], gTv[:, :, d, t0:t0 + TC], writes=[gtb])
                        pss = []
                        for (wt_, wb_, base, nk) in ((w0, w0b, 0, 8), (w1, w1b, 8, 8), (w2, w2b, 16, 4)):
                            ps, pb = psum.next()
                            for kt in range(nk):
                                P.op("pe", lambda e, kt=kt, ps=ps, wt_=wt_, j=j, base=base, nk=nk: e.matmul(
                                    ps[:], lhsT=wt_[:, kt, j * 128:(j + 1) * 128], rhs=hid[:, base + kt, :],
                                    start=(kt == 0), stop=(kt == nk - 1)), reads=[wb_, hidb[base + kt]], writes=[pb])
                            pss.append((ps, pb))
                        m1, m1b = f32pool.next()
                        m2, m2b = f32pool.next()
                        P.op("dve", lambda e, m1=m1, ps=pss[0][0], gt=gt: e.tensor_tensor(
                            out=m1[:], in0=ps[:], in1=gt[:, 0, :], op=ALU.mult), reads=[pss[0][1], gtb], writes=[m1b])
                        P.op("dve", lambda e, m2=m2, ps=pss[1][0], gt=gt: e.tensor_tensor(
                            out=m2[:], in0=ps[:], in1=gt[:, 1, :], op=ALU.mult), reads=[pss[1][1], gtb], writes=[m2b])
                        P.op("dve", lambda e, m1=m1, m2=m2: e.tensor_tensor(
                            out=m1[:], in0=m1[:], in1=m2[:], op=ALU.add), reads=[m1b, m2b], writes=[m1b])
                        P.op("dve", lambda e, m2=m2, ps=pss[2][0], gt=gt: e.tensor_tensor(
                            out=m2[:], in0=ps[:], in1=gt[:, 2, :], op=ALU.mult), reads=[pss[2][1], gtb], writes=[m2b])
                        P.op("dve", lambda e, m1=m1, m2=m2, d=d: e.tensor_tensor(
                            out=h_t[:, d, :], in0=m1[:], in1=m2[:], op=ALU.add), reads=[m1b, m2b], writes=[hb[d]])
                for dc in range(D // 256):
                    wt, wb = load_w(wov, NKT, dc * 256, 256)
                    for j in range(2):
                        d = dc * 2 + j
                        po, pob = psum.next()
                        for kt in range(NKT):
                            P.op("pe", lambda e, kt=kt, po=po, wt=wt, j=j: e.matmul(
                                po[:], lhsT=wt[:, kt, j * 128:(j + 1) * 128], rhs=h_t[:, kt, :],
                                start=(kt == 0), stop=(kt == NKT - 1)), reads=[wb, hb[kt]], writes=[pob])
                        P.op("dve", lambda e, po=po, d=d: e.tensor_tensor(
                            out=x_t[:, d, :], in0=po[:], in1=x_t[:, d, :], op=ALU.add),
                            reads=[pob, xb[d]], writes=[xb[d]])
                ffn()
                for kt in range(NKT):
                    P.dma("sp", xoTv[:, kt, t0:t0 + TC], x_t[:, kt, :], reads=[xb[kt]], is_output=True)
                rstd, rb = rmsnorm_stats()
                yTv = yT.rearrange("(kt p) t -> p kt t", p=128)
                for kt in range(NKT):
                    st, stb = f32pool.next()
                    P.op("dve", lambda e, kt=kt, rstd=rstd, st=st: e.scalar_tensor_tensor(
                        out=st[:], in0=x_t[:, kt, :], scalar=g2_t[:, kt:kt + 1], in1=rstd[:],
                        op0=ALU.mult, op1=ALU.mult), reads=[xb[kt], rb, g2b], writes=[stb])
                    P.dma("sp", yTv[:, kt, t0:t0 + TC], st[:], reads=[stb], is_output=True)
        P.emit()
    return nc


S = 4096
NQT = S // 128
NCH = S // 512
NCMP = 255
NEG = -30000.0
BIGSEL = 1.0e9


def build_attn():
    nc = bass.Bass("TRN2", target_bir_lowering=False)

    def din(name, shape, dt=F32):
        return nc.dram_tensor(name, shape, dt, kind="ExternalInput").ap()

    def dout(name, shape, dt=F32):
        return nc.dram_tensor(name, shape, dt, kind="ExternalOutput").ap()

    dqa = din("dqa", [2, 2, 66, S], BF16)
    dka = din("dka", [2, 2, 66, S], BF16)
    dva = din("dva", [2, 128, NQT, 129], BF16)
    lamp = din("lamp", [128, 4, 64])
    subg = din("subg", [128, 128])
    lami = din("lami", [128, 1])
    nq = din("nq", [4, 128, S], BF16)
    kcs = din("kcs", [2, 128, S], BF16)
    kslc = din("kslc", [128, S], BF16)
    kwin = din("kwin", [128, S], BF16)
    vslc = din("vslc", [128, NQT, 129], BF16)
    vwin = din("vwin", [128, NQT, 129], BF16)
    ngate = din("ngate", [128, NQT, 6])
    w1 = din("w1", [2, 4096, 128])
    w2 = din("w2", [2, 128, 128])
    posT = din("posT", [2, 128, 32])
    mq = din("mq", [128, S], BF16)
    memT = din("memT", [D, 256])
    gmem = din("gmem", [128, NKT])
    wmk = din("wmk", [D, 128])
    wmv = din("wmv", [D, 128])
    c_ident = din("c_ident", [128, 128], BF16)
    c_causal = din("c_causal", [128, 128], BF16)
    c_winneg = din("c_winneg", [128, 128], BF16)
    c_tneg = din("c_tneg", [128, 2560], BF16)
    c_ufix = din("c_ufix", [128, NQT, 64])
    c_eaug = din("c_eaug", [2, 66, S], BF16)
    c_selaug = din("c_selaug", [2, S], BF16)
    c_slrow = din("c_slrow", [2, 2, 128], BF16)
    c_bias = din("c_bias", [128, 4, 32])
    c_ovl = din("c_ovl", [2, 128, 65], BF16)
    o_diff = dout("o_diff", [S, 2, 128], BF16)
    o_nsa = dout("o_nsa", [S, 2, 128], BF16)
    o_mem = dout("o_mem", [S, 128], BF16)

    with ExitStack() as es:
        P = Prog(nc)
        sb = lambda name, shape, dt: es.enter_context(nc.sbuf_tensor(name, shape, dt))
        cnt = [0]

        def const(name, src, shape, dt, eng="sp"):
            t = sb(name, shape, dt)
            b = Buf(name)
            P.dma(eng, t[:], src, writes=[b])
            return t, b

        ident, identb = const("ident", c_ident, [128, 128], BF16)
        causal, causalb = const("causal", c_causal, [128, 128], BF16)
        winneg, winnegb = const("winneg", c_winneg, [128, 128], BF16)
        tneg, tnegb = const("tneg", c_tneg, [128, 2560], BF16)
        ufix, ufixb = const("ufix", c_ufix, [128, NQT, 64], F32)
        biasT, biasb = const("biasT", c_bias, [128, 4, 32], F32)
        lamp_t, lampb = const("lamp_t", lamp, [128, 4, 64], F32)
        subg_t, subgb = const("subg_t", subg, [128, 128], F32)
        lami_t, lamib = const("lami_t", lami, [128, 1], F32)
        epsc = sb("epsc", [128, 1], F32)
        epsb = Buf("eps")
        P.op("dve", lambda e: e.memset(epsc[:], EPS), writes=[epsb])

        kpool = RPool(es, nc, "kp", [128, S], BF16, 5)
        vpool = RPool(es, nc, "vp", [128, NQT, 129], BF16, 3)
        qpool = RPool(es, nc, "qp", [128, 4, 512], BF16, 3)
        ptpool = RPool(es, nc, "pt", [128, 512], BF16, 4)
        ps_s = RPool(es, nc, "pss", [128, 512], F32, 3, psum=True)
        ps_a = RPool(es, nc, "psa", [128, 512], F32, 4, psum=True)
        pst = es.enter_context(nc.psum_tensor("pst", [128, 1024], BF16))
        pstb = Buf("pst")
        smallp = RPool(es, nc, "sm", [128, 8], F32, 12)
        o32p = RPool(es, nc, "o32", [128, 128], F32, 10)
        sqjp = RPool(es, nc, "sqj", [128, 128], F32, 2)
        obfp = RPool(es, nc, "obf", [128, 128], BF16, 4)

        def exp_act(pt, ps, lo, hi, rows, bias_ap, reads, writes):
            if bias_ap is None:
                P.op("act", lambda e: e.activation(out=pt[0:rows, lo:hi], in_=ps[0:rows, lo:hi], func=AF.Exp),
                     reads=reads, writes=writes)
            else:
                P.op("act", lambda e: e.activation(out=pt[0:rows, lo:hi], in_=ps[0:rows, lo:hi], func=AF.Exp,
                                                   bias=bias_ap), reads=reads + [biasb], writes=writes)

        mem_t = sb("mem_t", [128, NKT, 256], F32)
        memb = Buf("mem")
        P.dma("sp", mem_t[:], memT.rearrange("(kt p) m -> p kt m", p=128), writes=[memb])
        gmem_t, gmemb = const("gmem_t", gmem, [128, NKT], F32)
        ones = sb("ones", [128, 128], BF16)
        onesb = Buf("ones")
        P.op("dve", lambda e: e.memset(ones[:], 1.0), writes=[onesb])
        memn = sb("memn", [128, NKT, 256], BF16)
        memnb = Buf("memn")
        wmk_t = sb("wmk_t", [128, NKT, 128], BF16)
        wmkb = Buf("wmk")
        wmv_t = sb("wmv_t", [128, NKT, 128], BF16)
        wmvb = Buf("wmv")
        P.dma("pool", wmk_t[:], wmk.rearrange("(kt p) f -> p kt f", p=128), writes=[wmkb])
        P.dma("pool", wmv_t[:], wmv.rearrange("(kt p) f -> p kt f", p=128), writes=[wmvb])
        ps, pb = ps_s.next()
        for kt in range(NKT):
            sq, sqb = ptpool.next()
            P.op("act", lambda e, kt=kt, sq=sq: e.activation(out=sq[:, 0:256], in_=mem_t[:, kt, :], func=AF.Square),
                 reads=[memb], writes=[sqb])
            P.op("pe", lambda e, kt=kt, sq=sq, ps=ps: e.matmul(ps[:, 0:256], lhsT=ones[:], rhs=sq[:, 0:256],
                                                                start=(kt == 0), stop=(kt == NKT - 1)),
                 reads=[sqb, onesb], writes=[pb])
        rstd_m = sb("rstd_m", [128, 256], F32)
        rstdmb = Buf("rstd_m")
        P.op("act", lambda e, ps=ps: e.activation(out=rstd_m[:], in_=ps[:, 0:256], func=AF.Sqrt, bias=epsc[:],
                                                  scale=1.0 / D), reads=[pb, epsb], writes=[rstdmb])
        P.op("dve", lambda e: e.reciprocal(out=rstd_m[:], in_=rstd_m[:]), reads=[rstdmb], writes=[rstdmb])
        for kt in range(NKT):
            P.op("dve", lambda e, kt=kt: e.scalar_tensor_tensor(
                out=memn[:, kt, :], in0=mem_t[:, kt, :], scalar=gmem_t[:, kt:kt + 1], in1=rstd_m[:],
                op0=ALU.mult, op1=ALU.mult), reads=[memb, rstdmb, gmemb], writes=[memnb])
        kmT = sb("kmT", [128, 256], BF16)
        kmTb = Buf("kmT")
        vm = sb("vm", [128, 2, 129], BF16)
        vmb = Buf("vm")
        ps, pb = ps_s.next()
        for kt in range(NKT):
            P.op("pe", lambda e, kt=kt, ps=ps: e.matmul(ps[:, 0:256], lhsT=wmk_t[:, kt, :], rhs=memn[:, kt, :],
                                                        start=(kt == 0), stop=(kt == NKT - 1)),
                 reads=[wmkb, memnb], writes=[pb])
        P.op("act", lambda e, ps=ps: e.activation(out=kmT[:], in_=ps[:, 0:256], func=AF.Copy), reads=[pb], writes=[kmTb])
        P.op("dve", lambda e: e.memset(vm[:, :, 128:129], 1.0), writes=[vmb])
        for mt in range(2):
            ps, pb = ps_s.next()
            for kt in range(NKT):
                P.op("pe", lambda e, kt=kt, ps=ps, mt=mt: e.matmul(
                    ps[:, 0:128], lhsT=memn[:, kt, mt * 128:(mt + 1) * 128], rhs=wmv_t[:, kt, :],
                    start=(kt == 0), stop=(kt == NKT - 1)), reads=[wmvb, memnb], writes=[pb])
            P.op("act", lambda e, ps=ps, mt=mt: e.activation(out=vm[:, mt, 0:128], in_=ps[:, 0:128], func=AF.Copy),
                 reads=[pb], writes=[vmb])
        class Pipe:
            def __init__(self):
                self.pv = None
                self.epi = None

            def step(self, qk, exp, pv):
                qk()
                if self.pv is not None:
                    self.pv()
                    self.pv = None
                if self.epi is not None:
                    self.epi()
                    self.epi = None
                exp()
                self.pv = pv

            def end_pass(self, epi):
                last_pv = self.pv
                self.pv = None

                def both():
                    if last_pv is not None:
                        last_pv()
                    epi()
                self.epi = both

            def flush(self):
                if self.pv is not None:
                    self.pv()
                    self.pv = None
                if self.epi is not None:
                    self.epi()
                    self.epi = None

        pipe = Pipe()

        def run_pass(steps, epi):
            for (qk, ex, pv) in steps:
                pipe.step(qk, ex, pv)
            pipe.end_pass(epi)

        for c in range(NCH):
            qt_, qb_ = qpool.next()
            P.dma("sp", qt_[:, 0, :], mq[:, c * 512:(c + 1) * 512], writes=[qb_])
            accs = [ps_a.next() for _ in range(4)]
            steps = []
            for mt in range(2):
                st = {}

                def qk(st=st, mt=mt, qt_=qt_, qb_=qb_):
                    ps, pb = ps_s.next()
                    st["ps"], st["pb"] = ps, pb
                    P.op("pe", lambda e: e.matmul(ps[:], lhsT=kmT[:, mt * 128:(mt + 1) * 128], rhs=qt_[:, 0, :],
                                                  start=True, stop=True), reads=[kmTb, qb_], writes=[pb])

                def ex(st=st):
                    pt, ptb = ptpool.next()
                    st["pt"], st["ptb"] = pt, ptb
                    exp_act(pt, st["ps"], 0, 512, 128, None, [st["pb"]], [ptb])

                def pv(st=st, mt=mt, accs=accs):
                    pt, ptb = st["pt"], st["ptb"]
                    for q4 in range(4):
                        acc, accb = accs[q4]
                        P.op("pe", lambda e, acc=acc, q4=q4: e.matmul(
                            acc[:, 0:129], lhsT=pt[:, q4 * 128:(q4 + 1) * 128], rhs=vm[:, mt, :],
                            start=(mt == 0), stop=(mt == 1)), reads=[ptb, vmb], writes=[accb])
                steps.append((qk, ex, pv))

            def epi(accs=accs, c=c):
                for q4 in range(4):
                    acc, accb = accs[q4]
                    rinv, rinvb = smallp.next()
                    P.op("dve", lambda e, acc=acc, rinv=rinv: e.reciprocal(out=rinv[:, 0:1], in_=acc[:, 128:129]),
                         reads=[accb], writes=[rinvb])
                    ob, obb = obfp.next()
                    P.op("dve", lambda e, acc=acc, rinv=rinv, ob=ob: e.tensor_scalar(
                        out=ob[:], in0=acc[:, 0:128], scalar1=rinv[:, 0:1], scalar2=None, op0=ALU.mult),
                        reads=[accb, rinvb], writes=[obb])
                    q0 = c * 512 + q4 * 128
                    P.dma("sp", o_mem[q0:q0 + 128, :], ob[:], reads=[obb], is_output=True)
            run_pass(steps, epi)
        pipe.flush()
        lprod = sb("lprod", [128, 2, 64], F32)
        lprodb = Buf("lprod")
        lam4 = sb("lam4", [128, 4], F32)
        lam4b = Buf("lam4")
        gsub = sb("gsub", [128, 128], F32)
        gsubb = Buf("gsub")
        P.op("dve", lambda e: e.tensor_tensor(out=lprod[:], in0=lamp_t[:, 0:4:2, :], in1=lamp_t[:, 1:4:2, :], op=ALU.mult),
             reads=[lampb], writes=[lprodb])
        P.op("dve", lambda e: e.reduce_sum(out=lam4[:, 0:2], in_=lprod[:], axis=AX.X), reads=[lprodb], writes=[lam4b])
        P.op("act", lambda e: e.activation(out=lam4[:, 0:2], in_=lam4[:, 0:2], func=AF.Exp), reads=[lam4b], writes=[lam4b])
        P.op("dve", lambda e: e.tensor_tensor(out=lam4[:, 2:3], in0=lam4[:, 0:1], in1=lam4[:, 1:2], op=ALU.subtract),
             reads=[lam4b], writes=[lam4b])
        P.op("dve", lambda e: e.tensor_tensor(out=lam4[:, 2:3], in0=lam4[:, 2:3], in1=lami_t[:, 0:1], op=ALU.add),
             reads=[lam4b, lamib], writes=[lam4b])
        P.op("dve", lambda e: e.tensor_scalar(out=lam4[:, 3:4], in0=lam4[:, 2:3], scalar1=-1.0, scalar2=None, op0=ALU.mult),
             reads=[lam4b], writes=[lam4b])
        oml = sb("oml", [128, 1], F32)
        omlb = Buf("oml")
        P.op("dve", lambda e: e.tensor_scalar(out=oml[:], in0=lami_t[:], scalar1=-1.0, scalar2=1.0, op0=ALU.mult, op1=ALU.add),
             reads=[lamib], writes=[omlb])
        P.op("dve", lambda e: e.tensor_scalar(out=gsub[:], in0=subg_t[:], scalar1=oml[:, 0:1], scalar2=None, op0=ALU.mult),
             reads=[subgb, omlb], writes=[gsubb])

        def causal_steps(ka, kab, krows, qab, qrows_sel, va, vab, bias_idx, c, accs, extra_mm=None):
            steps = []
            nk = 4 * c + 4
            for kt in range(nk):
                qlo = max(0, kt - 4 * c)
                lo = qlo * 128
                diag = kt >= 4 * c
                nmm = 1 + (1 if extra_mm is not None else 0) + (1 if diag else 0)
                st = {}

                def qk(st=st, kt=kt, lo=lo, diag=diag, nmm=nmm):
                    ps, pb = ps_s.next()
                    st["ps"], st["pb"] = ps, pb
                    P.op("pe", lambda e: e.matmul(
                        ps[:, lo:512], lhsT=ka[0:krows, kt * 128:(kt + 1) * 128], rhs=qrows_sel(lo),
                        start=True, stop=(nmm == 1)), reads=[kab, qab], writes=[pb])
                    k_done = 1
                    if extra_mm is not None:
                        k_done += 1
                        extra_mm(ps, pb, kt, lo, k_done == nmm)
                    if diag:
                        P.op("pe", lambda e: e.matmul(
                            ps[:, lo:lo + 128], lhsT=ident[:], rhs=causal[:], start=False, stop=True),
                            reads=[identb, causalb], writes=[pb])

                def ex(st=st, kt=kt, lo=lo):
                    pt, ptb = ptpool.next()
                    st["pt"], st["ptb"] = pt, ptb
                    exp_act(pt, st["ps"], lo, 512, 128, biasT[:, bias_idx, kt - 4 * c + 28:kt - 4 * c + 29],
                            [st["pb"]], [ptb])

                def pv(st=st, kt=kt, qlo=qlo):
                    pt, ptb = st["pt"], st["ptb"]
                    for q4 in range(qlo, 4):
                        acc, accb = accs[q4]
                        P.op("pe", lambda e, acc=acc, q4=q4: e.matmul(
                            acc[:, 0:129], lhsT=pt[:, q4 * 128:(q4 + 1) * 128], rhs=va[:, kt, :],
                            start=(kt == 0), stop=(kt == 4 * c + q4)), reads=[ptb, vab], writes=[accb])
                steps.append((qk, ex, pv))
            return steps

        for hl in range(2):
            va, vab = vpool.next()
            P.dma("sp", va[:], dva[hl], writes=[vab])
            kas = []
            for m in range(2):
                ka, kab = kpool.next()
                P.dma("sp", ka[0:66, :], dka[hl, m], writes=[kab])
                kas.append((ka, kab))
            for c in range(NCH):
                qa, qab = qpool.next()
                for m in range(2):
                    P.dma("sp", qa[0:66, m, :], dqa[hl, m, :, c * 512:(c + 1) * 512], writes=[qab])
                o0s = [o32p.next() for _ in range(4)]
                for m in range(2):
                    ka, kab = kas[m]
                    accs = [ps_a.next() for _ in range(4)]
                    steps = causal_steps(ka, kab, 66, qab, (lambda lo, qa=qa, m=m: qa[0:66, m, lo:512]),
                                         va, vab, hl, c, accs)

                    def epi(accs=accs, m=m, o0s=o0s, c=c, hl=hl):
                        for q4 in range(4):
                            acc, accb = accs[q4]
                            rinv, rinvb = smallp.next()
                            P.op("dve", lambda e, acc=acc, rinv=rinv: e.reciprocal(out=rinv[:, 0:1], in_=acc[:, 128:129]),
                                 reads=[accb], writes=[rinvb])
                            o0, o0b = o0s[q4]
                            if m == 0:
                                P.op("dve", lambda e, acc=acc, rinv=rinv, o0=o0: e.tensor_scalar(
                                    out=o0[:], in0=acc[:, 0:128], scalar1=rinv[:, 0:1], scalar2=None, op0=ALU.mult),
                                    reads=[accb, rinvb], writes=[o0b])
                            else:
                                P.op("dve", lambda e, rinv=rinv: e.tensor_tensor(
                                    out=rinv[:, 0:1], in0=rinv[:, 0:1], in1=lam4[:, 3:4], op=ALU.mult),
                                    reads=[rinvb, lam4b], writes=[rinvb])
                                P.op("dve", lambda e, acc=acc, rinv=rinv, o0=o0: e.scalar_tensor_tensor(
                                    out=o0[:], in0=acc[:, 0:128], scalar=rinv[:, 0:1], in1=o0[:], op0=ALU.mult, op1=ALU.add),
                                    reads=[accb, rinvb, o0b], writes=[o0b])
                                sqj, sqjb = sqjp.next()
                                ss, ssb = smallp.next()
                                P.op("dve", lambda e, sqj=sqj, o0=o0: e.tensor_tensor(
                                    out=sqj[:], in0=o0[:], in1=o0[:], op=ALU.mult), reads=[o0b], writes=[sqjb])
                                P.op("dve", lambda e, sqj=sqj, ss=ss: e.reduce_sum(
                                    out=ss[:, 0:1], in_=sqj[:], axis=AX.X), reads=[sqjb], writes=[ssb])
                                P.op("act", lambda e, ss=ss: e.activation(
                                    out=ss[:, 1:2], in_=ss[:, 0:1], func=AF.Sqrt, bias=epsc[:], scale=1.0 / 128.0),
                                    reads=[ssb, epsb], writes=[ssb])
                                P.op("dve", lambda e, ss=ss: e.reciprocal(out=ss[:, 2:3], in_=ss[:, 1:2]),
                                     reads=[ssb], writes=[ssb])
                                ob, obb = obfp.next()
                                P.op("dve", lambda e, o0=o0, ss=ss, ob=ob: e.scalar_tensor_tensor(
                                    out=ob[:], in0=o0[:], scalar=ss[:, 2:3], in1=gsub[:], op0=ALU.mult, op1=ALU.mult),
                                    reads=[o0b, ssb, gsubb], writes=[obb])
                                q0 = c * 512 + q4 * 128
                                P.dma("sp", o_diff[q0:q0 + 128, hl, :], ob[:], reads=[obb], is_output=True)
                    run_pass(steps, epi)
        pipe.flush()
        f256 = RPool(es, nc, "f256", [128, 256], F32, 4)
        impp = RPool(es, nc, "imp", [128, 64], F32, 10)
        onsa = RPool(es, nc, "onsa", [128, 128], F32, 18)
        w1_t, w1b, w2_t, w2b = [], [], [], []
        w1_shared = sb("w1_s", [128, 32, 128], BF16)
        w1_sb = Buf("w1_s")
        for i in range(2):
            w1_t.append(w1_shared)
            w1b.append(w1_sb)
            t = sb("w2_%d" % i, [128, 128], BF16)
            b = Buf("w2_%d" % i)
            P.dma("pool", t[:], w2[i], writes=[b])
            w2_t.append(t)
            w2b.append(b)
        posT_t = sb("posT_t", [128, 2, 32], BF16)
        posTb = Buf("posT")
        P.dma("pool", posT_t[:], posT.rearrange("k d l -> d k l"), writes=[posTb])
        kcmpT = sb("kcmpT", [128, 256], BF16)
        kcmpTb = Buf("kcmpT")
        vaug2 = sb("vaug2", [128, 2, 193], BF16)
        vaug2b = Buf("vaug2")
        P.dma("sp", vaug2[:, :, 128:193], c_ovl.rearrange("ct p e -> p ct e"), writes=[vaug2b])
        selT = sb("selT", [66, S], BF16)
        selTb = Buf("selT")
        P.dma("sp", selT[64:66, :], c_selaug, writes=[selTb])
        hilo = sb("hilo", [2, 512], BF16)
        hilob = Buf("hilo")
        P.dma("sp", hilo[:], c_selaug[:, 0:512], writes=[hilob])
        slrow = sb("slrow", [2, 2, 128], BF16)
        slrowb = Buf("slrow")
        P.dma("sp", slrow[:], c_slrow.rearrange("h r k -> r h k"), writes=[slrowb])
        eaug, eaugb = [], []
        for hl in range(2):
            t = sb("eaug%d" % hl, [66, S], BF16)
            b = Buf("eaug%d" % hl)
            P.dma("sp", t[:], c_eaug[hl], writes=[b])
            eaug.append(t)
            eaugb.append(b)
        gate_t = sb("gate_t", [128, NQT, 6], F32)
        gateb = Buf("gate")
        P.dma("sp", gate_t[:], ngate, writes=[gateb])

        for i in range(2):
            src, srcb = kpool.next()
            P.dma("sp", src[:], kcs[i], writes=[srcb])
            P.dma("pool", w1_shared[:], w1[i].rearrange("(l d) j -> d l j", d=128), writes=[w1_sb])
            ps, pb = ps_s.next()
            for l in range(32):
                P.op("pe", lambda e, ps=ps, l=l, i=i: e.matmul(ps[:, 0:1], lhsT=w1_t[i][:, l, :], rhs=posT_t[:, i, l:l + 1],
                                                                start=(l == 0), stop=(l == 31)),
                     reads=[w1b[i], posTb], writes=[pb])
            pbias, pbiasb = smallp.next()
            P.op("dve", lambda e, ps=ps, pbias=pbias: e.tensor_copy(out=pbias[:, 0:1], in_=ps[:, 0:1]),
                 reads=[pb], writes=[pbiasb])
            ps, pb = ps_s.next()
            for l in range(32):
                P.op("pe", lambda e, ps=ps, l=l, i=i, src=src: e.matmul(
                    ps[:, 0:NCMP], lhsT=w1_t[i][:, l, :], rhs=src[:, l:l + 16 * (NCMP - 1) + 1:16],
                    start=(l == 0), stop=(l == 31)), reads=[w1b[i], srcb], writes=[pb])
            pre, preb = f256.next()
            tq, tqb = f256.next()
            P.op("dve", lambda e, ps=ps, pre=pre, pbias=pbias: e.tensor_scalar(
                out=pre[:, 0:NCMP], in0=ps[:, 0:NCMP], scalar1=pbias[:, 0:1], scalar2=None, op0=ALU.add),
                reads=[pb, pbiasb], writes=[preb])
            P.op("dve", lambda e, pre=pre, tq=tq: e.tensor_tensor(out=tq[:, 0:NCMP], in0=pre[:, 0:NCMP], in1=pre[:, 0:NCMP],
                                                                  op=ALU.mult), reads=[preb], writes=[tqb])
            P.op("dve", lambda e, tq=tq: e.tensor_scalar(out=tq[:, 0:NCMP], in0=tq[:, 0:NCMP], scalar1=0.044715, scalar2=1.0,
                                                         op0=ALU.mult, op1=ALU.add), reads=[tqb], writes=[tqb])
            P.op("dve", lambda e, pre=pre, tq=tq: e.tensor_tensor(out=tq[:, 0:NCMP], in0=tq[:, 0:NCMP], in1=pre[:, 0:NCMP],
                                                                  op=ALU.mult), reads=[preb, tqb], writes=[tqb])
            P.op("act", lambda e, tq=tq: e.activation(out=tq[:, 0:NCMP], in_=tq[:, 0:NCMP], func=AF.Sigmoid,
                                                      scale=1.5957691216057308), reads=[tqb], writes=[tqb])
            gl, glb = ptpool.next()
            P.op("dve", lambda e, pre=pre, tq=tq, gl=gl: e.tensor_tensor(out=gl[:, 0:NCMP], in0=tq[:, 0:NCMP],
                                                                          in1=pre[:, 0:NCMP], op=ALU.mult),
                 reads=[preb, tqb], writes=[glb])
            if i == 0:
                ps2, pb2 = ps_s.next()
                P.op("pe", lambda e, ps2=ps2, gl=gl: e.matmul(ps2[:, 0:NCMP], lhsT=w2_t[0][:], rhs=gl[:, 0:NCMP],
                                                              start=True, stop=True), reads=[w2b[0], glb], writes=[pb2])
                P.op("act", lambda e, ps2=ps2: e.activation(out=kcmpT[:, 0:NCMP], in_=ps2[:, 0:NCMP], func=AF.Copy),
                     reads=[pb2], writes=[kcmpTb])
            else:
                for ct, n in ((0, 128), (1, 127)):
                    ps2, pb2 = ps_s.next()
                    P.op("pe", lambda e, ps2=ps2, gl=gl, ct=ct, n=n: e.matmul(
                        ps2[0:n, 0:128], lhsT=gl[:, ct * 128:ct * 128 + n], rhs=w2_t[1][:], start=True, stop=True),
                        reads=[w2b[1], glb], writes=[pb2])
                    P.op("act", lambda e, ps2=ps2, ct=ct, n=n: e.activation(
                        out=vaug2[0:n, ct, 0:128], in_=ps2[0:n, 0:128], func=AF.Copy), reads=[pb2], writes=[vaug2b])

        kslc_t, kslcb = kpool.next()
        P.dma("sp", kslc_t[:], kslc, writes=[kslcb])
        kwin_t, kwinb = kpool.next()
        P.dma("sp", kwin_t[:], kwin, writes=[kwinb])
        vslc_t, vslcb = vpool.next()
        P.dma("sp", vslc_t[:], vslc, writes=[vslcb])
        vwin_t, vwinb = vpool.next()
        P.dma("sp", vwin_t[:], vwin, writes=[vwinb])

        for c in range(NCH):
            qn, qnb = qpool.next()
            P.dma("sp", qn[:], nq[:, :, c * 512:(c + 1) * 512].rearrange("h p t -> p h t"), writes=[qnb])
            imps = [impp.next() for _ in range(4)]
            oaccs = [[onsa.next() for _ in range(4)] for _ in range(2)]
            cts = [0] if c < 4 else [0, 1]
            for hh in range(4):
                accs = [ps_a.next() for _ in range(4)]
                steps = []
                for ct in cts:
                    n = 128 if ct == 0 else 127
                    need_mask = (ct == 1) or (c <= 4)
                    off = 512 * c - 2048 * ct
                    st = {}

                    def qk(st=st, ct=ct, n=n, need_mask=need_mask, off=off, hh=hh, qn=qn, qnb=qnb):
                        ps, pb = ps_s.next()
                        st["ps"], st["pb"] = ps, pb
                        P.op("pe", lambda e: e.matmul(
                            ps[0:n, :], lhsT=kcmpT[:, ct * 128:ct * 128 + n], rhs=qn[:, hh, :], start=True,
                            stop=(not need_mask)), reads=[kcmpTb, qnb], writes=[pb])
                        if need_mask:
                            P.op("pe", lambda e: e.matmul(
                                ps[0:n, :], lhsT=ident[:, 0:n], rhs=tneg[:, off:off + 512], start=False, stop=True),
                                reads=[identb, tnegb], writes=[pb])

                    def ex(st=st, n=n):
                        pt, ptb = ptpool.next()
                        st["pt"], st["ptb"] = pt, ptb
                        exp_act(pt, st["ps"], 0, 512, n, None, [st["pb"]], [ptb])

                    def pv(st=st, ct=ct, n=n, accs=accs, cts=cts):
                        pt, ptb = st["pt"], st["ptb"]
                        for q4 in range(4):
                            acc, accb = accs[q4]
                            P.op("pe", lambda e, acc=acc, q4=q4: e.matmul(
                                acc[:, 0:193], lhsT=pt[0:n, q4 * 128:(q4 + 1) * 128], rhs=vaug2[0:n, ct, :],
                                start=(ct == cts[0]), stop=(ct == cts[-1])), reads=[ptb, vaug2b], writes=[accb])
                    steps.append((qk, ex, pv))

                def epi(accs=accs, hh=hh, c=c, imps=imps, oaccs=oaccs):
                    for q4 in range(4):
                        acc, accb = accs[q4]
                        qg = 4 * c + q4
                        rs, rsb = smallp.next()
                        P.op("dve", lambda e, acc=acc, rs=rs: e.tensor_scalar(
                            out=rs[:, 0:1], in0=acc[:, 128:129], scalar1=1e-30, scalar2=None, op0=ALU.max),
                            reads=[accb], writes=[rsb])
                        P.op("dve", lambda e, rs=rs: e.reciprocal(out=rs[:, 1:2], in_=rs[:, 0:1]), reads=[rsb], writes=[rsb])
                        im, imb = imps[q4]
                        if hh == 0:
                            P.op("dve", lambda e, acc=acc, rs=rs, im=im: e.tensor_scalar(
                                out=im[:], in0=acc[:, 129:193], scalar1=rs[:, 1:2], scalar2=None, op0=ALU.mult),
                                reads=[accb, rsb], writes=[imb])
                        else:
                            P.op("dve", lambda e, acc=acc, rs=rs, im=im: e.scalar_tensor_tensor(
                                out=im[:], in0=acc[:, 129:193], scalar=rs[:, 1:2], in1=im[:], op0=ALU.mult, op1=ALU.add),
                                reads=[accb, rsb, imb], writes=[imb])
                        if hh < 2:
                            oa, oab = oaccs[hh][q4]
                            P.op("dve", lambda e, rs=rs, qg=qg: e.tensor_tensor(
                                out=rs[:, 2:3], in0=rs[:, 1:2], in1=gate_t[:, qg, hh * 3:hh * 3 + 1], op=ALU.mult),
                                reads=[rsb, gateb], writes=[rsb])
                            P.op("dve", lambda e, acc=acc, rs=rs, oa=oa: e.tensor_scalar(
                                out=oa[:], in0=acc[:, 0:128], scalar1=rs[:, 2:3], scalar2=None, op0=ALU.mult),
                                reads=[accb, rsb], writes=[oab])
                    if hh == 3:
                        for q4 in range(4):
                            qg = 4 * c + q4
                            im, imb = imps[q4]
                            adj, adjb = impp.next()
                            P.op("dve", lambda e, im=im, adj=adj, qg=qg: e.tensor_tensor(
                                out=adj[:], in0=im[:], in1=ufix[:, qg, :], op=ALU.add), reads=[imb, ufixb], writes=[adjb])
                            t8, t8b = smallp.next()
                            P.op("dve", lambda e, adj=adj, t8=t8: e.max(out=t8[:, 0:8], in_=adj[:]), reads=[adjb], writes=[t8b])
                            thr, thrb = smallp.next()
                            P.op("dve", lambda e, t8=t8, thr=thr: e.tensor_reduce(
                                out=thr[:, 0:1], in_=t8[:, 0:8], axis=AX.X, op=ALU.min), reads=[t8b], writes=[thrb])
                            selm, selmb = obfp.next()
                            P.op("dve", lambda e, adj=adj, thr=thr, selm=selm: e.tensor_scalar(
                                out=selm[:, 0:64], in0=adj[:], scalar1=thr[:, 0:1], scalar2=1.0, op0=ALU.is_ge,
                                op1=ALU.subtract), reads=[adjb, thrb], writes=[selmb])
                            P.op("pe", lambda e, selm=selm, q4=q4: e.transpose(
                                out=pst[0:64, q4 * 128:(q4 + 1) * 128], in_=selm[:, 0:64], identity=ident[:]),
                                reads=[selmb, identb], writes=[pstb])
                        P.op("act", lambda e: e.activation(out=selT[0:64, c * 512:(c + 1) * 512], in_=pst[0:64, 0:512],
                                                           func=AF.Copy), reads=[pstb], writes=[selTb])
                run_pass(steps, epi)
            for hl in range(2):
                accs = [ps_a.next() for _ in range(4)]
                steps = []
                for kt in range(max(0, 4 * c - 4), 4 * c + 4):
                    qlo = max(kt, 4 * c) - 4 * c
                    qhi = min(kt + 4, 4 * c + 3) - 4 * c
                    lo, hi = qlo * 128, (qhi + 1) * 128
                    diag = kt >= 4 * c
                    far = (kt + 4 >= 4 * c) and (kt + 4 <= 4 * c + 3)
                    st = {}

                    def qk(st=st, kt=kt, lo=lo, hi=hi, diag=diag, far=far, hl=hl, qn=qn, qnb=qnb, c=c):
                        ps, pb = ps_s.next()
                        st["ps"], st["pb"] = ps, pb
                        P.op("pe", lambda e: e.matmul(
                            ps[:, lo:hi], lhsT=kwin_t[:, kt * 128:(kt + 1) * 128], rhs=qn[:, hl, lo:hi],
                            start=True, stop=False), reads=[kwinb, qnb], writes=[pb])
                        P.op("pe", lambda e: e.matmul(
                            ps[:, lo:hi], lhsT=slrow[0:2, hl, :], rhs=hilo[0:2, lo:hi],
                            start=False, stop=(not diag and not far)), reads=[slrowb, hilob], writes=[pb])
                        if diag:
                            q4d = kt - 4 * c
                            P.op("pe", lambda e: e.matmul(
                                ps[:, q4d * 128:(q4d + 1) * 128], lhsT=ident[:], rhs=causal[:], start=False, stop=(not far)),
                                reads=[identb, causalb], writes=[pb])
                        if far:
                            q4f = kt + 4 - 4 * c
                            P.op("pe", lambda e: e.matmul(
                                ps[:, q4f * 128:(q4f + 1) * 128], lhsT=ident[:], rhs=winneg[:], start=False, stop=True),
                                reads=[identb, winnegb], writes=[pb])

                    def ex(st=st, kt=kt, lo=lo, hi=hi, hl=hl, c=c):
                        pt, ptb = ptpool.next()
                        st["pt"], st["ptb"] = pt, ptb
                        exp_act(pt, st["ps"], lo, hi, 128, biasT[:, 2 + hl, kt - 4 * c + 28:kt - 4 * c + 29],
                                [st["pb"]], [ptb])

                    def pv(st=st, kt=kt, qlo=qlo, qhi=qhi, accs=accs, c=c):
                        pt, ptb = st["pt"], st["ptb"]
                        for q4 in range(qlo, qhi + 1):
                            acc, accb = accs[q4]
                            qg = 4 * c + q4
                            P.op("pe", lambda e, acc=acc, q4=q4, qg=qg: e.matmul(
                                acc[:, 0:129], lhsT=pt[:, q4 * 128:(q4 + 1) * 128], rhs=vwin_t[:, kt, :],
                                start=(kt == max(0, qg - 4)), stop=(kt == qg)), reads=[ptb, vwinb], writes=[accb])
                    steps.append((qk, ex, pv))

                def epi(accs=accs, hl=hl, c=c, oaccs=oaccs):
                    for q4 in range(4):
                        acc, accb = accs[q4]
                        qg = 4 * c + q4
                        rs, rsb = smallp.next()
                        P.op("dve", lambda e, acc=acc, rs=rs: e.reciprocal(out=rs[:, 0:1], in_=acc[:, 128:129]),
                             reads=[accb], writes=[rsb])
                        P.op("dve", lambda e, rs=rs, qg=qg: e.tensor_tensor(
                            out=rs[:, 1:2], in0=rs[:, 0:1], in1=gate_t[:, qg, hl * 3 + 2:hl * 3 + 3], op=ALU.mult),
                            reads=[rsb, gateb], writes=[rsb])
                        oa, oab = oaccs[hl][q4]
                        P.op("dve", lambda e, acc=acc, rs=rs, oa=oa: e.scalar_tensor_tensor(
                            out=oa[:], in0=acc[:, 0:128], scalar=rs[:, 1:2], in1=oa[:], op0=ALU.mult, op1=ALU.add),
                            reads=[accb, rsb, oab], writes=[oab])
                run_pass(steps, epi)
            for hl in range(2):
                def mask_mm(ps, pb, kt, lo, last, hl=hl, c=c):
                    P.op("pe", lambda e: e.matmul(
                        ps[:, lo:512], lhsT=eaug[hl][0:66, kt * 128:(kt + 1) * 128],
                        rhs=selT[0:66, c * 512 + lo:(c + 1) * 512], start=False, stop=last),
                        reads=[eaugb[hl], selTb], writes=[pb])
                accs = [ps_a.next() for _ in range(4)]
                steps = causal_steps(kslc_t, kslcb, 128, qnb, (lambda lo, qn=qn, hl=hl: qn[:, hl, lo:512]),
                                     vslc_t, vslcb, 2 + hl, c, accs, extra_mm=mask_mm)

                def epi(accs=accs, hl=hl, c=c, oaccs=oaccs):
                    for q4 in range(4):
                        acc, accb = accs[q4]
                        qg = 4 * c + q4
                        rs, rsb = smallp.next()
                        P.op("dve", lambda e, acc=acc, rs=rs: e.reciprocal(out=rs[:, 0:1], in_=acc[:, 128:129]),
                             reads=[accb], writes=[rsb])
                        P.op("dve", lambda e, rs=rs, qg=qg: e.tensor_tensor(
                            out=rs[:, 1:2], in0=rs[:, 0:1], in1=gate_t[:, qg, hl * 3 + 1:hl * 3 + 2], op=ALU.mult),
                            reads=[rsb, gateb], writes=[rsb])
                        oa, oab = oaccs[hl][q4]
                        ob, obb = obfp.next()
                        P.op("dve", lambda e, acc=acc, rs=rs, oa=oa, ob=ob: e.scalar_tensor_tensor(
                            out=ob[:], in0=acc[:, 0:128], scalar=rs[:, 1:2], in1=oa[:], op0=ALU.mult, op1=ALU.add),
                            reads=[accb, rsb, oab], writes=[obb])
                        q0 = qg * 128
                        P.dma("sp", o_nsa[q0:q0 + 128, hl, :], ob[:], reads=[obb], is_output=True)
                run_pass(steps, epi)
        pipe.flush()
        P.emit()
    return nc


import ml_dtypes

BF = ml_dtypes.bfloat16
_NC = {}
DEBUG = {}


def _prog(name):
    if name not in _NC:
        _NC[name] = build_token(name) if name in ("pre", "post") else build_attn()
    return _NC[name]


def _run(name, in_maps):
    res = run_bass_kernel_spmd(_prog(name), in_maps, core_ids=list(range(8)))
    return res.results


def _lay(g):
    return np.ascontiguousarray(np.asarray(g, np.float32).reshape(NKT, 128).T)


def _tile_tok(a):
    e = a.shape[1]
    return np.ascontiguousarray(a.reshape(NQT, 128, e).transpose(1, 0, 2))


_CONST = {}


def _consts():
    if _CONST:
        return _CONST
    kk = np.arange(128)[:, None]
    qq = np.arange(128)[None, :]
    _CONST["c_ident"] = np.eye(128, dtype=np.float32).astype(BF)
    _CONST["c_causal"] = np.where(kk > qq, NEG, 0.0).astype(np.float32).astype(BF)
    _CONST["c_winneg"] = np.where(kk <= qq, NEG, 0.0).astype(np.float32).astype(BF)
    z = np.arange(2560)[None, :]
    _CONST["c_tneg"] = np.where(16 * kk + 31 <= z, 0.0, NEG).astype(np.float32).astype(BF)
    p = np.arange(128)[:, None, None]
    qg = np.arange(NQT)[None, :, None]
    j = np.arange(64)[None, None, :]
    cur = 2 * qg + (p >= 64)
    forced = (j == 0) | (j == cur) | (j == cur - 1)
    future = j > cur
    _CONST["c_ufix"] = np.where(forced, BIGSEL, np.where(future, -BIGSEL, 0.0)).astype(np.float32)
    q512 = np.arange(S) % 512
    hi = (q512 // 2) * 2
    lo = q512 % 2
    _CONST["c_selaug"] = (-np.stack([hi, lo]).astype(np.float32)).astype(BF)
    c = np.arange(256)[:, None]
    jj = np.arange(64)[None, :]
    ovl = ((16 * c < 64 * jj + 64) & (16 * c + 32 > 64 * jj) & (c < NCMP)).astype(np.float32)
    o65 = np.concatenate([(c < NCMP).astype(np.float32), ovl], axis=1)
    _CONST["c_ovl"] = o65.reshape(2, 128, 65).astype(BF)
    col = np.arange(S)[None, :]
    _CONST["erows"] = np.where(col // 64 == np.arange(64)[:, None], 30000.0, 0.0).astype(np.float32)
    return _CONST


def _slope(h):
    return 2.0 ** (-(h + 1.0))


def kernel(**inputs):
    inp = {k: np.asarray(v) for k, v in inputs.items()}
    x = inp["x"].astype(np.float32).reshape(8 * TOK, D)
    mem = inp["mem"].astype(np.float32)
    cst = _consts()
    xT = [np.ascontiguousarray(x[c * TOK:(c + 1) * TOK].T) for c in range(8)]
    depth = inp["w_in"].shape[0]
    yT = None
    for l in range(depth):
        maps = [{"xT": xT[c], "gn": _lay(inp["ffn1_norm"][l]), "gm": _lay(inp["mix_norm"][l]),
                 "wg": inp["ffn1_w_gate"][l], "wu": inp["ffn1_w_up"][l], "wd": inp["ffn1_w_down"][l],
                 "win": inp["w_in"][l]} for c in range(8)]
        r1 = _run("pre", maps)
        x1T = [r["xoT"] for r in r1]
        gT = [r["gT"] for r in r1]
        lam_init = 0.8 - 0.6 * math.exp(-0.3 * l)
        maps = []
        for b in range(2):
            featB = np.concatenate([np.asarray(r1[4 * b + i]["featT"]) for i in range(4)], axis=1)
            tokVB = np.concatenate([np.asarray(r1[4 * b + i]["tokV"]) for i in range(4)], axis=0)
            gatesB = np.concatenate([np.asarray(r1[4 * b + i]["gates"]) for i in range(4)], axis=0)
            onesc = np.ones((S, 1), BF)
            for r in range(4):
                g = r // 2
                m = {}
                dqa = np.zeros((2, 2, 66, S), BF)
                dka = np.zeros((2, 2, 66, S), BF)
                dva = np.zeros((2, 128, NQT, 129), BF)
                for hl in range(2):
                    h = 2 * r + hl
                    for mm in range(2):
                        dqa[hl, mm, 0:64] = featB[h * 128 + mm * 64:h * 128 + mm * 64 + 64]
                        dqa[hl, mm, 64:66] = cst["c_selaug"]
                        dka[hl, mm, 0:64] = featB[1024 + h * 128 + mm * 64:1024 + h * 128 + mm * 64 + 64]
                        dka[hl, mm, 64:66] = np.float32(_slope(h))
                    dva[hl] = _tile_tok(np.concatenate([tokVB[:, h * 128:(h + 1) * 128], onesc], axis=1))
                m["dqa"], m["dka"], m["dva"] = dqa, dka, dva
                m["lamp"] = np.ascontiguousarray(np.broadcast_to(inp["diff_lambda"][l].astype(np.float32)[None], (128, 4, 64)))
                m["subg"] = np.ascontiguousarray(np.broadcast_to(inp["diff_subln"][l].astype(np.float32)[None], (128, 128)))
                m["lami"] = np.full((128, 1), lam_init, np.float32)
                outh = [2 * r, 2 * r + 1]
                hh_list = outh + [hh for hh in range(4 * g, 4 * g + 4) if hh not in outh]
                m["nq"] = np.stack([featB[2048 + hh * 128:2048 + (hh + 1) * 128] for hh in hh_list])
                m["kcs"] = np.stack([featB[3072 + g * 128:3072 + (g + 1) * 128], featB[3328 + g * 128:3328 + (g + 1) * 128]])
                m["kslc"] = np.ascontiguousarray(featB[3584 + g * 128:3584 + (g + 1) * 128])
                m["kwin"] = np.ascontiguousarray(featB[3840 + g * 128:3840 + (g + 1) * 128])
                m["vslc"] = _tile_tok(np.concatenate([tokVB[:, 1024 + g * 128:1024 + (g + 1) * 128], onesc], axis=1))
                m["vwin"] = _tile_tok(np.concatenate([tokVB[:, 1280 + g * 128:1280 + (g + 1) * 128], onesc], axis=1))
                m["ngate"] = _tile_tok(np.concatenate([gatesB[:, hh * 3:hh * 3 + 3] for hh in outh], axis=1))
                m["w1"] = inp["nsa_cmp_w1"][l].astype(np.float32)
                m["w2"] = inp["nsa_cmp_w2"][l].astype(np.float32)
                m["posT"] = np.ascontiguousarray(inp["nsa_cmp_pos"][l].astype(np.float32).transpose(0, 2, 1))
                m["mq"] = np.ascontiguousarray(featB[4096 + r * 128:4096 + (r + 1) * 128])
                m["memT"] = np.ascontiguousarray(mem[b].T)
                m["gmem"] = _lay(inp["mem_norm"][l])
                m["wmk"] = np.ascontiguousarray(inp["w_mem_kv"][l][:, r * 128:(r + 1) * 128].astype(np.float32))
                m["wmv"] = np.ascontiguousarray(inp["w_mem_kv"][l][:, 512 + r * 128:512 + (r + 1) * 128].astype(np.float32))
                for k in ("c_ident", "c_causal", "c_winneg", "c_tneg", "c_ufix", "c_selaug", "c_ovl"):
                    m[k] = cst[k]
                eaug = np.zeros((2, 66, S), np.float32)
                slrow = np.zeros((2, 2, 128), np.float32)
                bias = np.zeros((128, 4, 32), np.float32)
                pp = np.arange(128, dtype=np.float32)[:, None]
                idx = np.arange(32, dtype=np.float32)[None, :]
                for hl in range(2):
                    eaug[hl, 0:64] = cst["erows"]
                    eaug[hl, 64:66] = _slope(outh[hl])
                    slrow[hl] = _slope(outh[hl])
                    bias[:, hl, :] = _slope(2 * r + hl) * (pp + 128.0 * (idx - 28.0))
                    bias[:, 2 + hl, :] = _slope(outh[hl]) * (pp + 128.0 * (idx - 28.0))
                m["c_eaug"] = eaug.astype(BF)
                m["c_slrow"] = slrow.astype(BF)
                m["c_bias"] = bias
                maps.append(m)
        r2 = _run("attn", maps)
        if DEBUG.get("on"):
            DEBUG["r1_%d" % l] = r1
            DEBUG["r2_%d" % l] = r2
        maps = []
        for c in range(8):
            b, tq = c // 4, c % 4
            sl = slice(tq * TOK, (tq + 1) * TOK)
            parts = [np.asarray(r2[4 * b + r]["o_diff"])[sl].reshape(TOK, 256) for r in range(4)]
            parts += [np.asarray(r2[4 * b + r]["o_nsa"])[sl].reshape(TOK, 256) for r in range(4)]
            parts += [np.asarray(r2[4 * b + r]["o_mem"])[sl] for r in range(4)]
            oT = np.ascontiguousarray(np.concatenate(parts, axis=1).T)
            maps.append({"xT": x1T[c], "gn": _lay(inp["ffn2_norm"][l]), "wg": inp["ffn2_w_gate"][l],
                         "wu": inp["ffn2_w_up"][l], "wd": inp["ffn2_w_down"][l], "oT": oT, "gT": gT[c],
                         "wud": inp["w_up_diff"][l], "wun": inp["w_up_nsa"][l], "wum": inp["w_up_mem"][l],
                         "wo": inp["w_out"][l], "gf": _lay(inp["final_norm"])})
        r3 = _run("post", maps)
        xT = [r["xoT"] for r in r3]
        yT = [r["yT"] for r in r3]
        if DEBUG.get("on"):
            DEBUG["r3_%d" % l] = r3
            if DEBUG.get("stop_after") == l:
                break
    out = np.concatenate([np.asarray(y).T for y in yT], axis=0).reshape(2, S, D).astype(np.float32)
    return out
```

```python
import math
from contextlib import ExitStack
import numpy as np
import concourse.bass as bass
import concourse.mybir as mybir
from concourse.bass_utils import run_bass_kernel_spmd

F32 = mybir.dt.float32
BF16 = mybir.dt.bfloat16
AF = mybir.ActivationFunctionType
ALU = mybir.AluOpType
AX = mybir.AxisListType

ENGS = ("pe", "act", "dve", "pool", "sp")
N_DMA_SLOTS = 8


class Buf:
    __slots__ = ("name", "w", "r")

    def __init__(self, name):
        self.name = name
        self.w = None
        self.r = []


class Op:
    __slots__ = ("id", "eng", "fn", "deps", "is_dma", "slot", "semval", "prev_slot_op",
                 "signaled", "sigval")

    def __init__(self, id, eng, fn, is_dma):
        self.id = id
        self.eng = eng
        self.fn = fn
        self.deps = []
        self.is_dma = is_dma
        self.slot = None
        self.semval = 0
        self.prev_slot_op = None
        self.signaled = False
        self.sigval = 0


class Prog:
    def __init__(self, nc):
        self.nc = nc
        self.ops = []
        self.by_eng = {e: [] for e in ENGS}
        self.slot_last = {e: [None] * N_DMA_SLOTS for e in ENGS}
        self.slot_cnt = {e: [0] * N_DMA_SLOTS for e in ENGS}
        self.slot_rr = {e: 0 for e in ENGS}
        self.out_dmas = []

    def _add(self, eng, fn, reads, writes, is_dma):
        op = Op(len(self.ops), eng, fn, is_dma)
        deps = set()
        for b in reads:
            if b.w is not None:
                deps.add(b.w)
        for b in writes:
            if b.w is not None:
                deps.add(b.w)
            for r in b.r:
                deps.add(r)
        for b in writes:
            b.w = op.id
            b.r = []
        for b in reads:
            if b.w != op.id:
                b.r.append(op.id)
        deps.discard(op.id)
        for d in sorted(deps):
            t = self.ops[d]
            if (not is_dma) and eng == "pe" and t.eng == "pe" and not t.is_dma:
                continue
            op.deps.append(d)
            if not t.is_dma:
                t.signaled = True
        if is_dma:
            s = self.slot_rr[eng]
            self.slot_rr[eng] = (s + 1) % N_DMA_SLOTS
            op.slot = s
            op.prev_slot_op = self.slot_last[eng][s]
            self.slot_cnt[eng][s] += 16
            op.semval = self.slot_cnt[eng][s]
            self.slot_last[eng][s] = op.id
        self.ops.append(op)
        self.by_eng[eng].append(op)
        return op

    def op(self, eng, fn, reads=(), writes=()):
        return self._add(eng, fn, reads, writes, False)

    def dma(self, eng, out, in_, reads=(), writes=(), is_output=False):
        op = self._add(eng, lambda e: e.dma_start(out=out, in_=in_), reads, writes, True)
        if is_output:
            self.out_dmas.append(op.id)
        return op

    def emit(self):
        nc = self.nc
        with ExitStack() as es:
            es.enter_context(nc.cleanup_on_exit())
            esem = {e: nc.alloc_semaphore(name="s_" + e) for e in ENGS}
            dsem = {e: [nc.alloc_semaphore(name="d_%s%d" % (e, i)) for i in range(N_DMA_SLOTS)]
                    for e in ("sp", "pool")}
            for e in ENGS:
                c = 0
                for op in self.by_eng[e]:
                    if op.signaled and not op.is_dma:
                        c += 1
                        op.sigval = c
            block = nc.Block()
            block.__enter__()
            ops = self.ops
            out_dmas = self.out_dmas

            def stream(eng_name, h):
                seen = {}

                def wait(sem, key, val):
                    if seen.get(key, 0) >= val:
                        return
                    h.wait_ge(sem, val)
                    seen[key] = val

                for op in self.by_eng[eng_name]:
                    for d in op.deps:
                        t = ops[d]
                        if t.is_dma:
                            wait(dsem[t.eng][t.slot], (t.eng, t.slot), t.semval)
                        else:
                            wait(esem[t.eng], t.eng, t.sigval)
                    if op.is_dma:
                        if op.prev_slot_op is not None:
                            p = ops[op.prev_slot_op]
                            wait(dsem[eng_name][op.slot], (eng_name, op.slot), p.semval)
                        ins = op.fn(h)
                        ins.then_inc(dsem[eng_name][op.slot], 16)
                    else:
                        ins = op.fn(h)
                        if op.signaled:
                            ins.then_inc(esem[eng_name], 1)
                if eng_name == "sp":
                    for d in out_dmas:
                        t = ops[d]
                        wait(dsem[t.eng][t.slot], (t.eng, t.slot), t.semval)
                    for q in ("sp", "pool"):
                        for s in range(N_DMA_SLOTS):
                            if self.slot_cnt[q][s] > 0:
                                wait(dsem[q][s], (q, s), self.slot_cnt[q][s])

            @block.tensor
            def _(e):
                stream("pe", e)

            @block.scalar
            def _(e):
                stream("act", e)

            @block.vector
            def _(e):
                stream("dve", e)

            @block.gpsimd
            def _(e):
                stream("pool", e)

            @block.sync
            def _(e):
                stream("sp", e)

            block.__exit__(None, None, None)
            nc.all_engine_barrier()


D = 2048
DFF = 5632
NKT = D // 128
NFT = DFF // 128
TOK = 1024
TC = 512
IN_COLS = 12312
EPS = 1e-6
S128 = 128.0 ** -0.5


class RPool:
    def __init__(self, es, nc, name, shape, dtype, n, psum=False):
        mk = nc.psum_tensor if psum else nc.sbuf_tensor
        self.t = [es.enter_context(mk("%s%d" % (name, i), shape, dtype)) for i in range(n)]
        self.b = [Buf("%s%d" % (name, i)) for i in range(n)]
        self.i = 0
        self.n = n

    def next(self):
        i = self.i
        self.i = (i + 1) % self.n
        return self.t[i], self.b[i]


def build_token(mode):
    nc = bass.Bass("TRN2", target_bir_lowering=False)

    def din(name, shape, dt=F32):
        return nc.dram_tensor(name, shape, dt, kind="ExternalInput").ap()

    def dout(name, shape, dt=F32):
        return nc.dram_tensor(name, shape, dt, kind="ExternalOutput").ap()

    xT = din("xT", [D, TOK])
    gn = din("gn", [128, NKT])
    wg = din("wg", [D, DFF])
    wu = din("wu", [D, DFF])
    wd = din("wd", [DFF, D])
    if mode == "pre":
        gm = din("gm", [128, NKT])
        win = din("win", [D, IN_COLS])
        xoT = dout("xoT", [D, TOK])
        featT = dout("featT", [36 * 128, TOK], BF16)
        gTo = dout("gT", [6144, TOK])
        tokV = dout("tokV", [TOK, 1536], BF16)
        gates = dout("gates", [TOK, 24])
    else:
        oT = din("oT", [2560, TOK], BF16)
        gTi = din("gT", [6144, TOK])
        wud = din("wud", [1024, D])
        wun = din("wun", [1024, D])
        wum = din("wum", [512, D])
        wo = din("wo", [D, D])
        gf = din("gf", [128, NKT])
        xoT = dout("xoT", [D, TOK])
        yT = dout("yT", [D, TOK])

    NC2 = TOK // TC
    HF = NFT // 2

    with ExitStack() as es:
        P = Prog(nc)
        sb = lambda name, shape, dt: es.enter_context(nc.sbuf_tensor(name, shape, dt))
        x_t = sb("x_t", [128, NKT, TC], F32)
        xb = [Buf("x%d" % i) for i in range(NKT)]
        h_t = sb("h_t", [128, NKT, TOK], BF16)
        hb = [Buf("h%d" % i) for i in range(NKT)]
        hid = sb("hid", [128, HF, TOK], BF16)
        hidb = [Buf("hid%d" % i) for i in range(HF)]
        ones = sb("ones", [128, 128], BF16)
        onesb = Buf("ones")
        gn_t = sb("gn_t", [128, NKT], F32)
        gnb = Buf("gn")
        g2_t = sb("g2_t", [128, NKT], F32)
        g2b = Buf("g2")
        wpool = RPool(es, nc, "wp", [128, NKT, 256], BF16, 4)
        wdpool = RPool(es, nc, "wdp", [128, HF, 256], BF16, 2)
        psum = RPool(es, nc, "ps", [128, 512], F32, 8, psum=True)
        sqpool = RPool(es, nc, "sq", [128, TC], BF16, 3)
        f32pool = RPool(es, nc, "f32p", [128, TC], F32, 8)
        rstdpool = RPool(es, nc, "rstd", [128, TC], F32, 2)
        bfpool = RPool(es, nc, "bfp", [128, TC], BF16, 4)
        if mode == "post":
            gpool = RPool(es, nc, "gp", [128, 3, TC], F32, 2)

        epsc = sb("epsc", [128, 1], F32)
        epsb = Buf("eps")
        P.op("dve", lambda e: e.memset(ones[:], 1.0), writes=[onesb])
        P.op("dve", lambda e: e.memset(epsc[:], EPS), writes=[epsb])
        P.dma("sp", gn_t[:], gn, writes=[gnb])
        P.dma("sp", g2_t[:], gm if mode == "pre" else gf, writes=[g2b])

        xTv = xT.rearrange("(kt p) t -> p kt t", p=128)
        xoTv = xoT.rearrange("(kt p) t -> p kt t", p=128)
        xob = [[Buf("xo%d_%d" % (d, c)) for c in range(NC2)] for d in range(NKT)]
        wgv = wg.rearrange("(kt p) f -> p kt f", p=128)
        wuv = wu.rearrange("(kt p) f -> p kt f", p=128)
        wdv = wd.rearrange("(ft p) d -> p ft d", p=128)

        def load_w(view, nk, c0, w):
            wt, wb = wpool.next()
            P.dma("pool", wt[:, 0:nk, 0:w], view[:, :, c0:c0 + w], writes=[wb])
            return wt, wb

        def load_x_chunk(srcv, c, from_out):
            for kt in range(NKT):
                P.dma("sp", x_t[:, kt, :], srcv[:, kt, c * TC:(c + 1) * TC],
                      reads=([xob[kt][c]] if from_out else []), writes=[xb[kt]])

        def rmsnorm_stats():
            ps, pb = psum.next()
            for kt in range(NKT):
                sq, sqb = sqpool.next()
                P.op("act", lambda e, kt=kt, sq=sq: e.activation(out=sq[:], in_=x_t[:, kt, :], func=AF.Square),
                     reads=[xb[kt]], writes=[sqb])
                P.op("pe", lambda e, kt=kt, sq=sq, ps=ps: e.matmul(ps[:], lhsT=ones[:], rhs=sq[:],
                                                                    start=(kt == 0), stop=(kt == NKT - 1)),
                     reads=[sqb, onesb], writes=[pb])
            rstd, rb = rstdpool.next()
            P.op("act", lambda e, ps=ps, rstd=rstd: e.activation(out=rstd[:], in_=ps[:], func=AF.Sqrt, bias=epsc[:],
                                                                 scale=1.0 / D), reads=[pb, epsb], writes=[rb])
            P.op("dve", lambda e, rstd=rstd: e.reciprocal(out=rstd[:], in_=rstd[:]), reads=[rb], writes=[rb])
            return rstd, rb

        def rmsnorm_to_h(g_t, gb, c):
            rstd, rb = rmsnorm_stats()
            for kt in range(NKT):
                P.op("dve", lambda e, kt=kt, rstd=rstd: e.scalar_tensor_tensor(
                    out=h_t[:, kt, c * TC:(c + 1) * TC], in0=x_t[:, kt, :], scalar=g_t[:, kt:kt + 1], in1=rstd[:],
                    op0=ALU.mult, op1=ALU.mult), reads=[xb[kt], rb, gb], writes=[hb[kt]])

        def residual_rmw(po, pob, d, c, scale, srcv, from_out):
            xs, xsb = f32pool.next()
            P.dma("sp", xs[:], srcv[:, d, c * TC:(c + 1) * TC], reads=([xob[d][c]] if from_out else []), writes=[xsb])
            P.op("dve", lambda e: e.scalar_tensor_tensor(out=xs[:], in0=po[:], scalar=scale, in1=xs[:],
                                                         op0=ALU.mult, op1=ALU.add), reads=[pob, xsb], writes=[xsb])
            P.dma("sp", xoTv[:, d, c * TC:(c + 1) * TC], xs[:], reads=[xsb], writes=[xob[d][c]], is_output=True)

        def ffn(srcv, from_out):
            for c in range(NC2):
                load_x_chunk(srcv, c, from_out)
                rmsnorm_to_h(gn_t, gnb, c)
            for half in range(2):
                for fc in range(HF // 2):
                    col0 = (half * HF + fc * 2) * 128
                    wgt, wgb = load_w(wgv, NKT, col0, 256)
                    wut, wub = load_w(wuv, NKT, col0, 256)
                    for j in range(2):
                        fl = fc * 2 + j
                        for c in range(NC2):
                            pg, pgb = psum.next()
                            pu, pub = psum.next()
                            for kt in range(NKT):
                                P.op("pe", lambda e, kt=kt, pg=pg, wgt=wgt, j=j, c=c: e.matmul(
                                    pg[:], lhsT=wgt[:, kt, j * 128:(j + 1) * 128], rhs=h_t[:, kt, c * TC:(c + 1) * TC],
                                    start=(kt == 0), stop=(kt == NKT - 1)), reads=[wgb, hb[kt]], writes=[pgb])
                            for kt in range(NKT):
                                P.op("pe", lambda e, kt=kt, pu=pu, wut=wut, j=j, c=c: e.matmul(
                                    pu[:], lhsT=wut[:, kt, j * 128:(j + 1) * 128], rhs=h_t[:, kt, c * TC:(c + 1) * TC],
                                    start=(kt == 0), stop=(kt == NKT - 1)), reads=[wub, hb[kt]], writes=[pub])
                            sg, sgb = f32pool.next()
                            P.op("act", lambda e, sg=sg, pg=pg: e.activation(out=sg[:], in_=pg[:], func=AF.Silu),
                                 reads=[pgb], writes=[sgb])
                            P.op("dve", lambda e, sg=sg, pu=pu, fl=fl, c=c: e.tensor_tensor(
                                out=hid[:, fl, c * TC:(c + 1) * TC], in0=sg[:], in1=pu[:], op=ALU.mult),
                                reads=[sgb, pub], writes=[hidb[fl]])
                for dc in range(D // 256):
                    wdt, wdb = wdpool.next()
                    P.dma("pool", wdt[:], wdv[:, half * HF:(half + 1) * HF, dc * 256:(dc + 1) * 256], writes=[wdb])
                    for j in range(2):
                        d = dc * 2 + j
                        for c in range(NC2):
                            po, pob = psum.next()
                            for fl in range(HF):
                                P.op("pe", lambda e, fl=fl, po=po, wdt=wdt, j=j, c=c: e.matmul(
                                    po[:], lhsT=wdt[:, fl, j * 128:(j + 1) * 128], rhs=hid[:, fl, c * TC:(c + 1) * TC],
                                    start=(fl == 0), stop=(fl == HF - 1)), reads=[wdb, hidb[fl]], writes=[pob])
                            if half == 0:
                                residual_rmw(po, pob, d, c, 0.5, srcv, from_out)
                            else:
                                residual_rmw(po, pob, d, c, 0.5, xoTv, True)

        if mode == "pre":
            ffn(xTv, False)
            for c in range(NC2):
                load_x_chunk(xoTv, c, True)
                rmsnorm_to_h(g2_t, g2b, c)
            winv = win.rearrange("(kt p) f -> p kt f", p=128)
            plan = []
            for i in range(4):
                plan.append((i * 256, 256, "feat", i * 256, 0.125))
            for i in range(4):
                plan.append((1024 + i * 256, 256, "feat", 1024 + i * 256, 1.0))
            for i in range(4):
                plan.append((2048 + i * 256, 256, "tokv", i * 256, 1.0))
            for i in range(4):
                plan.append((3072 + i * 256, 256, "feat", 2048 + i * 256, S128))
            plan.append((4096, 256, "feat", 3072, 1.0))
            plan.append((4352, 256, "feat", 3328, 1.0))
            plan.append((4608, 256, "feat", 3584, 1.0))
            plan.append((4864, 256, "tokv", 1024, 1.0))
            plan.append((5120, 256, "feat", 3840, 1.0))
            plan.append((5376, 256, "tokv", 1280, 1.0))
            plan.append((5632, 24, "gates", 0, 1.0))
            for i in range(2):
                plan.append((5656 + i * 256, 256, "feat", 4096 + i * 256, S128))
            for i in range(24):
                plan.append((6168 + i * 256, 256, "mg", i * 256, 1.0))
            for (c0, w, kind, dst, scale) in plan:
                wt, wb = load_w(winv, NKT, c0, w)
                if kind in ("feat", "mg"):
                    for j in range(w // 128):
                        for c in range(NC2):
                            t0 = c * TC
                            ps, pb = psum.next()
                            for kt in range(NKT):
                                P.op("pe", lambda e, kt=kt, ps=ps, wt=wt, j=j, t0=t0: e.matmul(
                                    ps[:], lhsT=wt[:, kt, j * 128:(j + 1) * 128], rhs=h_t[:, kt, t0:t0 + TC],
                                    start=(kt == 0), stop=(kt == NKT - 1)), reads=[wb, hb[kt]], writes=[pb])
                            r0 = dst + j * 128
                            if kind == "feat":
                                st, stb = bfpool.next()
                                P.op("act", lambda e, st=st, ps=ps, scale=scale: e.activation(
                                    out=st[:], in_=ps[:], func=AF.Copy, scale=scale), reads=[pb], writes=[stb])
                                P.dma("sp", featT[r0:r0 + 128, t0:t0 + TC], st[:], reads=[stb], is_output=True)
                            else:
                                st, stb = f32pool.next()
                                P.op("act", lambda e, st=st, ps=ps: e.activation(
                                    out=st[:], in_=ps[:], func=AF.Sigmoid), reads=[pb], writes=[stb])
                                P.dma("sp", gTo[r0:r0 + 128, t0:t0 + TC], st[:], reads=[stb], is_output=True)
                else:
                    for tt in range(TOK // 128):
                        ps, pb = psum.next()
                        for kt in range(NKT):
                            P.op("pe", lambda e, kt=kt, ps=ps, wt=wt, tt=tt, w=w: e.matmul(
                                ps[:, 0:w], lhsT=h_t[:, kt, tt * 128:(tt + 1) * 128], rhs=wt[:, kt, 0:w],
                                start=(kt == 0), stop=(kt == NKT - 1)), reads=[wb, hb[kt]], writes=[pb])
                        if kind == "tokv":
                            st, stb = bfpool.next()
                            P.op("act", lambda e, st=st, ps=ps, w=w: e.activation(
                                out=st[:, 0:w], in_=ps[:, 0:w], func=AF.Copy), reads=[pb], writes=[stb])
                            P.dma("sp", tokV[tt * 128:(tt + 1) * 128, dst:dst + w], st[:, 0:w],
                                  reads=[stb], is_output=True)
                        else:
                            st, stb = f32pool.next()
                            P.op("act", lambda e, st=st, ps=ps, w=w: e.activation(
                                out=st[:, 0:w], in_=ps[:, 0:w], func=AF.Sigmoid), reads=[pb], writes=[stb])
                            P.dma("sp", gates[tt * 128:(tt + 1) * 128, 0:w], st[:, 0:w],
                                  reads=[stb], is_output=True)
        else:
            oTv = oT.rearrange("(kt p) t -> p kt t", p=128)
            gTv = gTi.rearrange("(br dt p) t -> p br dt t", br=3, p=128)
            wudv = wud.rearrange("(kt p) f -> p kt f", p=128)
            wunv = wun.rearrange("(kt p) f -> p kt f", p=128)
            wumv = wum.rearrange("(kt p) f -> p kt f", p=128)
            wov = wo.rearrange("(kt p) f -> p kt f", p=128)
            for kt in range(20):
                P.dma("sp", hid[:, kt, :], oTv[:, kt, :], writes=[hidb[kt]])
            for dc in range(D // 256):
                w0, w0b = load_w(wudv, 8, dc * 256, 256)
                w1, w1b = load_w(wunv, 8, dc * 256, 256)
                w2, w2b = load_w(wumv, 4, dc * 256, 256)
                for j in range(2):
                    d = dc * 2 + j
                    for c in range(NC2):
                        t0 = c * TC
                        gt, gtb = gpool.next()
                        P.dma("sp", gt[:], gTv[:, :, d, t0:t0 + TC], writes=[gtb])
                        pss = []
                        for (wt_, wb_, base, nk) in ((w0, w0b, 0, 8), (w1, w1b, 8, 8), (w2, w2b, 16, 4)):
                            ps, pb = psum.next()
                            for kt in range(nk):
                                P.op("pe", lambda e, kt=kt, ps=ps, wt_=wt_, j=j, base=base, nk=nk, t0=t0: e.matmul(
                                    ps[:], lhsT=wt_[:, kt, j * 128:(j + 1) * 128], rhs=hid[:, base + kt, t0:t0 + TC],
                                    start=(kt == 0), stop=(kt == nk - 1)), reads=[wb_, hidb[base + kt]], writes=[pb])
                            pss.append((ps, pb))
                        m1, m1b = f32pool.next()
                        m2, m2b = f32pool.next()
                        P.op("dve", lambda e, m1=m1, ps=pss[0][0], gt=gt: e.tensor_tensor(
                            out=m1[:], in0=ps[:], in1=gt[:, 0, :], op=ALU.mult), reads=[pss[0][1], gtb], writes=[m1b])
                        P.op("dve", lambda e, m2=m2, ps=pss[1][0], gt=gt: e.tensor_tensor(
                            out=m2[:], in0=ps[:], in1=gt[:, 1, :], op=ALU.mult), reads=[pss[1][1], gtb], writes=[m2b])
                        P.op("dve", lambda e, m1=m1, m2=m2: e.tensor_tensor(
                            out=m1[:], in0=m1[:], in1=m2[:], op=ALU.add), reads=[m1b, m2b], writes=[m1b])
                        P.op("dve", lambda e, m2=m2, ps=pss[2][0], gt=gt: e.tensor_tensor(
                            out=m2[:], in0=ps[:], in1=gt[:, 2, :], op=ALU.mult), reads=[pss[2][1], gtb], writes=[m2b])
                        P.op("dve", lambda e, m1=m1, m2=m2, d=d, t0=t0: e.tensor_tensor(
                            out=h_t[:, d, t0:t0 + TC], in0=m1[:], in1=m2[:], op=ALU.add), reads=[m1b, m2b], writes=[hb[d]])
            for dc in range(D // 256):
                wt, wb = load_w(wov, NKT, dc * 256, 256)
                for j in range(2):
                    d = dc * 2 + j
                    for c in range(NC2):
                        po, pob = psum.next()
                        for kt in range(NKT):
                            P.op("pe", lambda e, kt=kt, po=po, wt=wt, j=j, c=c: e.matmul(
                                po[:], lhsT=wt[:, kt, j * 128:(j + 1) * 128], rhs=h_t[:, kt, c * TC:(c + 1) * TC],
                                start=(kt == 0), stop=(kt == NKT - 1)), reads=[wb, hb[kt]], writes=[pob])
                        residual_rmw(po, pob, d, c, 1.0, xTv, False)
            ffn(xoTv, True)
            yTv = yT.rearrange("(kt p) t -> p kt t", p=128)
            for c in range(NC2):
                load_x_chunk(xoTv, c, True)
                rstd, rb = rmsnorm_stats()
                for kt in range(NKT):
                    st, stb = f32pool.next()
                    P.op("dve", lambda e, kt=kt, rstd=rstd, st=st: e.scalar_tensor_tensor(
                        out=st[:], in0=x_t[:, kt, :], scalar=g2_t[:, kt:kt + 1], in1=rstd[:],
                        op0=ALU.mult, op1=ALU.mult), reads=[xb[kt], rb, g2b], writes=[stb])
                    P.dma("sp", yTv[:, kt, c * TC:(c + 1) * TC], st[:], reads=[stb], is_output=True)
        P.emit()
    return nc


S = 4096
NQT = S // 128
NCH = S // 512
NCMP = 255
NEG = -30000.0
BIGSEL = 1.0e9


def build_attn():
    nc = bass.Bass("TRN2", target_bir_lowering=False)

    def din(name, shape, dt=F32):
        return nc.dram_tensor(name, shape, dt, kind="ExternalInput").ap()

    def dout(name, shape, dt=F32):
        return nc.dram_tensor(name, shape, dt, kind="ExternalOutput").ap()

    dqa = din("dqa", [2, 2, 66, S], BF16)
    dka = din("dka", [2, 2, 66, S], BF16)
    dva = din("dva", [2, 128, NQT, 129], BF16)
    lamp = din("lamp", [128, 4, 64])
    subg = din("subg", [128, 128])
    lami = din("lami", [128, 1])
    nq = din("nq", [4, 128, S], BF16)
    kcs = din("kcs", [2, 128, S], BF16)
    kslc = din("kslc", [128, S], BF16)
    kwin = din("kwin", [128, S], BF16)
    vslc = din("vslc", [128, NQT, 129], BF16)
    vwin = din("vwin", [128, NQT, 129], BF16)
    ngate = din("ngate", [128, NQT, 6])
    w1 = din("w1", [2, 4096, 128])
    w2 = din("w2", [2, 128, 128])
    posT = din("posT", [2, 128, 32])
    mq = din("mq", [128, S], BF16)
    memT = din("memT", [D, 256])
    gmem = din("gmem", [128, NKT])
    wmk = din("wmk", [D, 128])
    wmv = din("wmv", [D, 128])
    c_ident = din("c_ident", [128, 128], BF16)
    c_causal = din("c_causal", [128, 128], BF16)
    c_winneg = din("c_winneg", [128, 128], BF16)
    c_tneg = din("c_tneg", [128, 2560], BF16)
    c_ufix = din("c_ufix", [128, NQT, 64])
    c_eaug = din("c_eaug", [2, 66, S], BF16)
    c_selaug = din("c_selaug", [2, S], BF16)
    c_slrow = din("c_slrow", [2, 2, 128], BF16)
    c_bias = din("c_bias", [128, 4, 32])
    c_ovl = din("c_ovl", [2, 128, 65], BF16)
    o_diff = dout("o_diff", [S, 2, 128], BF16)
    o_nsa = dout("o_nsa", [S, 2, 128], BF16)
    o_mem = dout("o_mem", [S, 128], BF16)

    with ExitStack() as es:
        P = Prog(nc)
        sb = lambda name, shape, dt: es.enter_context(nc.sbuf_tensor(name, shape, dt))
        cnt = [0]

        def const(name, src, shape, dt, eng="sp"):
            t = sb(name, shape, dt)
            b = Buf(name)
            P.dma(eng, t[:], src, writes=[b])
            return t, b

        ident, identb = const("ident", c_ident, [128, 128], BF16)
        causal, causalb = const("causal", c_causal, [128, 128], BF16)
        winneg, winnegb = const("winneg", c_winneg, [128, 128], BF16)
        tneg, tnegb = const("tneg", c_tneg, [128, 2560], BF16)
        ufix, ufixb = const("ufix", c_ufix, [128, NQT, 64], F32)
        biasT, biasb = const("biasT", c_bias, [128, 4, 32], F32)
        lamp_t, lampb = const("lamp_t", lamp, [128, 4, 64], F32)
        subg_t, subgb = const("subg_t", subg, [128, 128], F32)
        lami_t, lamib = const("lami_t", lami, [128, 1], F32)
        epsc = sb("epsc", [128, 1], F32)
        epsb = Buf("eps")
        P.op("dve", lambda e: e.memset(epsc[:], EPS), writes=[epsb])

        kpool = RPool(es, nc, "kp", [128, S], BF16, 5)
        vpool = RPool(es, nc, "vp", [128, NQT, 129], BF16, 3)
        qpool = RPool(es, nc, "qp", [128, 4, 512], BF16, 3)
        ptpool = RPool(es, nc, "pt", [128, 512], BF16, 4)
        ps_s = RPool(es, nc, "pss", [128, 512], F32, 3, psum=True)
        ps_a = RPool(es, nc, "psa", [128, 512], F32, 4, psum=True)
        pst = es.enter_context(nc.psum_tensor("pst", [128, 1024], BF16))
        pstb = Buf("pst")
        smallp = RPool(es, nc, "sm", [128, 8], F32, 12)
        o32p = RPool(es, nc, "o32", [128, 128], F32, 10)
        sqjp = RPool(es, nc, "sqj", [128, 128], F32, 2)
        obfp = RPool(es, nc, "obf", [128, 128], BF16, 4)

        def exp_act(pt, ps, lo, hi, rows, bias_ap, reads, writes):
            if bias_ap is None:
                P.op("act", lambda e: e.activation(out=pt[0:rows, lo:hi], in_=ps[0:rows, lo:hi], func=AF.Exp),
                     reads=reads, writes=writes)
            else:
                P.op("act", lambda e: e.activation(out=pt[0:rows, lo:hi], in_=ps[0:rows, lo:hi], func=AF.Exp,
                                                   bias=bias_ap), reads=reads + [biasb], writes=writes)

        mem_t = sb("mem_t", [128, NKT, 256], F32)
        memb = Buf("mem")
        P.dma("sp", mem_t[:], memT.rearrange("(kt p) m -> p kt m", p=128), writes=[memb])
        gmem_t, gmemb = const("gmem_t", gmem, [128, NKT], F32)
        ones = sb("ones", [128, 128], BF16)
        onesb = Buf("ones")
        P.op("dve", lambda e: e.memset(ones[:], 1.0), writes=[onesb])
        memn = sb("memn", [128, NKT, 256], BF16)
        memnb = Buf("memn")
        wmk_t = sb("wmk_t", [128, NKT, 128], BF16)
        wmkb = Buf("wmk")
        wmv_t = sb("wmv_t", [128, NKT, 128], BF16)
        wmvb = Buf("wmv")
        P.dma("pool", wmk_t[:], wmk.rearrange("(kt p) f -> p kt f", p=128), writes=[wmkb])
        P.dma("pool", wmv_t[:], wmv.rearrange("(kt p) f -> p kt f", p=128), writes=[wmvb])
        ps, pb = ps_s.next()
        for kt in range(NKT):
            sq, sqb = ptpool.next()
            P.op("act", lambda e, kt=kt, sq=sq: e.activation(out=sq[:, 0:256], in_=mem_t[:, kt, :], func=AF.Square),
                 reads=[memb], writes=[sqb])
            P.op("pe", lambda e, kt=kt, sq=sq, ps=ps: e.matmul(ps[:, 0:256], lhsT=ones[:], rhs=sq[:, 0:256],
                                                                start=(kt == 0), stop=(kt == NKT - 1)),
                 reads=[sqb, onesb], writes=[pb])
        rstd_m = sb("rstd_m", [128, 256], F32)
        rstdmb = Buf("rstd_m")
        P.op("act", lambda e, ps=ps: e.activation(out=rstd_m[:], in_=ps[:, 0:256], func=AF.Sqrt, bias=epsc[:],
                                                  scale=1.0 / D), reads=[pb, epsb], writes=[rstdmb])
        P.op("dve", lambda e: e.reciprocal(out=rstd_m[:], in_=rstd_m[:]), reads=[rstdmb], writes=[rstdmb])
        for kt in range(NKT):
            P.op("dve", lambda e, kt=kt: e.scalar_tensor_tensor(
                out=memn[:, kt, :], in0=mem_t[:, kt, :], scalar=gmem_t[:, kt:kt + 1], in1=rstd_m[:],
                op0=ALU.mult, op1=ALU.mult), reads=[memb, rstdmb, gmemb], writes=[memnb])
        kmT = sb("kmT", [128, 256], BF16)
        kmTb = Buf("kmT")
        vm = sb("vm", [128, 2, 129], BF16)
        vmb = Buf("vm")
        ps, pb = ps_s.next()
        for kt in range(NKT):
            P.op("pe", lambda e, kt=kt, ps=ps: e.matmul(ps[:, 0:256], lhsT=wmk_t[:, kt, :], rhs=memn[:, kt, :],
                                                        start=(kt == 0), stop=(kt == NKT - 1)),
                 reads=[wmkb, memnb], writes=[pb])
        P.op("act", lambda e, ps=ps: e.activation(out=kmT[:], in_=ps[:, 0:256], func=AF.Copy), reads=[pb], writes=[kmTb])
        P.op("dve", lambda e: e.memset(vm[:, :, 128:129], 1.0), writes=[vmb])
        for mt in range(2):
            ps, pb = ps_s.next()
            for kt in range(NKT):
                P.op("pe", lambda e, kt=kt, ps=ps, mt=mt: e.matmul(
                    ps[:, 0:128], lhsT=memn[:, kt, mt * 128:(mt + 1) * 128], rhs=wmv_t[:, kt, :],
                    start=(kt == 0), stop=(kt == NKT - 1)), reads=[wmvb, memnb], writes=[pb])
            P.op("act", lambda e, ps=ps, mt=mt: e.activation(out=vm[:, mt, 0:128], in_=ps[:, 0:128], func=AF.Copy),
                 reads=[pb], writes=[vmb])
        class Pipe:
            def __init__(self):
                self.pv = None
                self.epi = None

            def step(self, qk, exp, pv):
                qk()
                if self.pv is not None:
                    self.pv()
                    self.pv = None
                if self.epi is not None:
                    self.epi()
                    self.epi = None
                exp()
                self.pv = pv

            def end_pass(self, epi):
                last_pv = self.pv
                self.pv = None

                def both():
                    if last_pv is not None:
                        last_pv()
                    epi()
                self.epi = both

            def flush(self):
                if self.pv is not None:
                    self.pv()
                    self.pv = None
                if self.epi is not None:
                    self.epi()
                    self.epi = None

        pipe = Pipe()

        def run_pass(steps, epi):
            for (qk, ex, pv) in steps:
                pipe.step(qk, ex, pv)
            pipe.end_pass(epi)

        for c in range(NCH):
            qt_, qb_ = qpool.next()
            P.dma("sp", qt_[:, 0, :], mq[:, c * 512:(c + 1) * 512], writes=[qb_])
            accs = [ps_a.next() for _ in range(4)]
            steps = []
            for mt in range(2):
                st = {}

                def qk(st=st, mt=mt, qt_=qt_, qb_=qb_):
                    ps, pb = ps_s.next()
                    st["ps"], st["pb"] = ps, pb
                    P.op("pe", lambda e: e.matmul(ps[:], lhsT=kmT[:, mt * 128:(mt + 1) * 128], rhs=qt_[:, 0, :],
                                                  start=True, stop=True), reads=[kmTb, qb_], writes=[pb])

                def ex(st=st):
                    pt, ptb = ptpool.next()
                    st["pt"], st["ptb"] = pt, ptb
                    exp_act(pt, st["ps"], 0, 512, 128, None, [st["pb"]], [ptb])

                def pv(st=st, mt=mt, accs=accs):
                    pt, ptb = st["pt"], st["ptb"]
                    for q4 in range(4):
                        acc, accb = accs[q4]
                        P.op("pe", lambda e, acc=acc, q4=q4: e.matmul(
                            acc[:, 0:129], lhsT=pt[:, q4 * 128:(q4 + 1) * 128], rhs=vm[:, mt, :],
                            start=(mt == 0), stop=(mt == 1)), reads=[ptb, vmb], writes=[accb])
                steps.append((qk, ex, pv))

            def epi(accs=accs, c=c):
                for q4 in range(4):
                    acc, accb = accs[q4]
                    rinv, rinvb = smallp.next()
                    P.op("dve", lambda e, acc=acc, rinv=rinv: e.reciprocal(out=rinv[:, 0:1], in_=acc[:, 128:129]),
                         reads=[accb], writes=[rinvb])
                    ob, obb = obfp.next()
                    P.op("dve", lambda e, acc=acc, rinv=rinv, ob=ob: e.tensor_scalar(
                        out=ob[:], in0=acc[:, 0:128], scalar1=rinv[:, 0:1], scalar2=None, op0=ALU.mult),
                        reads=[accb, rinvb], writes=[obb])
                    q0 = c * 512 + q4 * 128
                    P.dma("sp", o_mem[q0:q0 + 128, :], ob[:], reads=[obb], is_output=True)
            run_pass(steps, epi)
        pipe.flush()
        lprod = sb("lprod", [128, 2, 64], F32)
        lprodb = Buf("lprod")
        lam4 = sb("lam4", [128, 4], F32)
        lam4b = Buf("lam4")
        gsub = sb("gsub", [128, 128], F32)
        gsubb = Buf("gsub")
        P.op("dve", lambda e: e.tensor_tensor(out=lprod[:], in0=lamp_t[:, 0:4:2, :], in1=lamp_t[:, 1:4:2, :], op=ALU.mult),
             reads=[lampb], writes=[lprodb])
        P.op("dve", lambda e: e.reduce_sum(out=lam4[:, 0:2], in_=lprod[:], axis=AX.X), reads=[lprodb], writes=[lam4b])
        P.op("act", lambda e: e.activation(out=lam4[:, 0:2], in_=lam4[:, 0:2], func=AF.Exp), reads=[lam4b], writes=[lam4b])
        P.op("dve", lambda e: e.tensor_tensor(out=lam4[:, 2:3], in0=lam4[:, 0:1], in1=lam4[:, 1:2], op=ALU.subtract),
             reads=[lam4b], writes=[lam4b])
        P.op("dve", lambda e: e.tensor_tensor(out=lam4[:, 2:3], in0=lam4[:, 2:3], in1=lami_t[:, 0:1], op=ALU.add),
             reads=[lam4b, lamib], writes=[lam4b])
        P.op("dve", lambda e: e.tensor_scalar(out=lam4[:, 3:4], in0=lam4[:, 2:3], scalar1=-1.0, scalar2=None, op0=ALU.mult),
             reads=[lam4b], writes=[lam4b])
        oml = sb("oml", [128, 1], F32)
        omlb = Buf("oml")
        P.op("dve", lambda e: e.tensor_scalar(out=oml[:], in0=lami_t[:], scalar1=-1.0, scalar2=1.0, op0=ALU.mult, op1=ALU.add),
             reads=[lamib], writes=[omlb])
        P.op("dve", lambda e: e.tensor_scalar(out=gsub[:], in0=subg_t[:], scalar1=oml[:, 0:1], scalar2=None, op0=ALU.mult),
             reads=[subgb, omlb], writes=[gsubb])

        def causal_steps(ka, kab, krows, qab, qrows_sel, va, vab, bias_idx, c, accs, extra_mm=None):
            steps = []
            nk = 4 * c + 4
            for kt in range(nk):
                qlo = max(0, kt - 4 * c)
                lo = qlo * 128
                diag = kt >= 4 * c
                nmm = 1 + (1 if extra_mm is not None else 0) + (1 if diag else 0)
                st = {}

                def qk(st=st, kt=kt, lo=lo, diag=diag, nmm=nmm):
                    ps, pb = ps_s.next()
                    st["ps"], st["pb"] = ps, pb
                    P.op("pe", lambda e: e.matmul(
                        ps[:, lo:512], lhsT=ka[0:krows, kt * 128:(kt + 1) * 128], rhs=qrows_sel(lo),
                        start=True, stop=(nmm == 1)), reads=[kab, qab], writes=[pb])
                    k_done = 1
                    if extra_mm is not None:
                        k_done += 1
                        extra_mm(ps, pb, kt, lo, k_done == nmm)
                    if diag:
                        P.op("pe", lambda e: e.matmul(
                            ps[:, lo:lo + 128], lhsT=ident[:], rhs=causal[:], start=False, stop=True),
                            reads=[identb, causalb], writes=[pb])

                def ex(st=st, kt=kt, lo=lo):
                    pt, ptb = ptpool.next()
                    st["pt"], st["ptb"] = pt, ptb
                    exp_act(pt, st["ps"], lo, 512, 128, biasT[:, bias_idx, kt - 4 * c + 28:kt - 4 * c + 29],
                            [st["pb"]], [ptb])

                def pv(st=st, kt=kt, qlo=qlo):
                    pt, ptb = st["pt"], st["ptb"]
                    for q4 in range(qlo, 4):
                        acc, accb = accs[q4]
                        P.op("pe", lambda e, acc=acc, q4=q4: e.matmul(
                            acc[:, 0:129], lhsT=pt[:, q4 * 128:(q4 + 1) * 128], rhs=va[:, kt, :],
                            start=(kt == 0), stop=(kt == 4 * c + q4)), reads=[ptb, vab], writes=[accb])
                steps.append((qk, ex, pv))
            return steps

        for hl in range(2):
            va, vab = vpool.next()
            P.dma("sp", va[:], dva[hl], writes=[vab])
            kas = []
            for m in range(2):
                ka, kab = kpool.next()
                P.dma("sp", ka[0:66, :], dka[hl, m], writes=[kab])
                kas.append((ka, kab))
            for c in range(NCH):
                qa, qab = qpool.next()
                for m in range(2):
                    P.dma("sp", qa[0:66, m, :], dqa[hl, m, :, c * 512:(c + 1) * 512], writes=[qab])
                o0s = [o32p.next() for _ in range(4)]
                for m in range(2):
                    ka, kab = kas[m]
                    accs = [ps_a.next() for _ in range(4)]
                    steps = causal_steps(ka, kab, 66, qab, (lambda lo, qa=qa, m=m: qa[0:66, m, lo:512]),
                                         va, vab, hl, c, accs)

                    def epi(accs=accs, m=m, o0s=o0s, c=c, hl=hl):
                        for q4 in range(4):
                            acc, accb = accs[q4]
                            rinv, rinvb = smallp.next()
                            P.op("dve", lambda e, acc=acc, rinv=rinv: e.reciprocal(out=rinv[:, 0:1], in_=acc[:, 128:129]),
                                 reads=[accb], writes=[rinvb])
                            o0, o0b = o0s[q4]
                            if m == 0:
                                P.op("dve", lambda e, acc=acc, rinv=rinv, o0=o0: e.tensor_scalar(
                                    out=o0[:], in0=acc[:, 0:128], scalar1=rinv[:, 0:1], scalar2=None, op0=ALU.mult),
                                    reads=[accb, rinvb], writes=[o0b])
                            else:
                                P.op("dve", lambda e, rinv=rinv: e.tensor_tensor(
                                    out=rinv[:, 0:1], in0=rinv[:, 0:1], in1=lam4[:, 3:4], op=ALU.mult),
                                    reads=[rinvb, lam4b], writes=[rinvb])
                                P.op("dve", lambda e, acc=acc, rinv=rinv, o0=o0: e.scalar_tensor_tensor(
                                    out=o0[:], in0=acc[:, 0:128], scalar=rinv[:, 0:1], in1=o0[:], op0=ALU.mult, op1=ALU.add),
                                    reads=[accb, rinvb, o0b], writes=[o0b])
                                sqj, sqjb = sqjp.next()
                                ss, ssb = smallp.next()
                                P.op("dve", lambda e, sqj=sqj, o0=o0: e.tensor_tensor(
                                    out=sqj[:], in0=o0[:], in1=o0[:], op=ALU.mult), reads=[o0b], writes=[sqjb])
                                P.op("dve", lambda e, sqj=sqj, ss=ss: e.reduce_sum(
                                    out=ss[:, 0:1], in_=sqj[:], axis=AX.X), reads=[sqjb], writes=[ssb])
                                P.op("act", lambda e, ss=ss: e.activation(
                                    out=ss[:, 1:2], in_=ss[:, 0:1], func=AF.Sqrt, bias=epsc[:], scale=1.0 / 128.0),
                                    reads=[ssb, epsb], writes=[ssb])
                                P.op("dve", lambda e, ss=ss: e.reciprocal(out=ss[:, 2:3], in_=ss[:, 1:2]),
                                     reads=[ssb], writes=[ssb])
                                ob, obb = obfp.next()
                                P.op("dve", lambda e, o0=o0, ss=ss, ob=ob: e.scalar_tensor_tensor(
                                    out=ob[:], in0=o0[:], scalar=ss[:, 2:3], in1=gsub[:], op0=ALU.mult, op1=ALU.mult),
                                    reads=[o0b, ssb, gsubb], writes=[obb])
                                q0 = c * 512 + q4 * 128
                                P.dma("sp", o_diff[q0:q0 + 128, hl, :], ob[:], reads=[obb], is_output=True)
                    run_pass(steps, epi)
        pipe.flush()
        f256 = RPool(es, nc, "f256", [128, 256], F32, 4)
        impp = RPool(es, nc, "imp", [128, 64], F32, 10)
        onsa = RPool(es, nc, "onsa", [128, 128], F32, 18)
        w1_t, w1b, w2_t, w2b = [], [], [], []
        w1_shared = sb("w1_s", [128, 32, 128], BF16)
        w1_sb = Buf("w1_s")
        for i in range(2):
            w1_t.append(w1_shared)
            w1b.append(w1_sb)
            t = sb("w2_%d" % i, [128, 128], BF16)
            b = Buf("w2_%d" % i)
            P.dma("pool", t[:], w2[i], writes=[b])
            w2_t.append(t)
            w2b.append(b)
        posT_t = sb("posT_t", [128, 2, 32], BF16)
        posTb = Buf("posT")
        P.dma("pool", posT_t[:], posT.rearrange("k d l -> d k l"), writes=[posTb])
        kcmpT = sb("kcmpT", [128, 256], BF16)
        kcmpTb = Buf("kcmpT")
        vaug2 = sb("vaug2", [128, 2, 193], BF16)
        vaug2b = Buf("vaug2")
        P.dma("sp", vaug2[:, :, 128:193], c_ovl.rearrange("ct p e -> p ct e"), writes=[vaug2b])
        selT = sb("selT", [66, S], BF16)
        selTb = Buf("selT")
        P.dma("sp", selT[64:66, :], c_selaug, writes=[selTb])
        hilo = sb("hilo", [2, 512], BF16)
        hilob = Buf("hilo")
        P.dma("sp", hilo[:], c_selaug[:, 0:512], writes=[hilob])
        slrow = sb("slrow", [2, 2, 128], BF16)
        slrowb = Buf("slrow")
        P.dma("sp", slrow[:], c_slrow.rearrange("h r k -> r h k"), writes=[slrowb])
        eaug, eaugb = [], []
        for hl in range(2):
            t = sb("eaug%d" % hl, [66, S], BF16)
            b = Buf("eaug%d" % hl)
            P.dma("sp", t[:], c_eaug[hl], writes=[b])
            eaug.append(t)
            eaugb.append(b)
        gate_t = sb("gate_t", [128, NQT, 6], F32)
        gateb = Buf("gate")
        P.dma("sp", gate_t[:], ngate, writes=[gateb])

        for i in range(2):
            src, srcb = kpool.next()
            P.dma("sp", src[:], kcs[i], writes=[srcb])
            P.dma("pool", w1_shared[:], w1[i].rearrange("(l d) j -> d l j", d=128), writes=[w1_sb])
            ps, pb = ps_s.next()
            for l in range(32):
                P.op("pe", lambda e, ps=ps, l=l, i=i: e.matmul(ps[:, 0:1], lhsT=w1_t[i][:, l, :], rhs=posT_t[:, i, l:l + 1],
                                                                start=(l == 0), stop=(l == 31)),
                     reads=[w1b[i], posTb], writes=[pb])
            pbias, pbiasb = smallp.next()
            P.op("dve", lambda e, ps=ps, pbias=pbias: e.tensor_copy(out=pbias[:, 0:1], in_=ps[:, 0:1]),
                 reads=[pb], writes=[pbiasb])
            ps, pb = ps_s.next()
            for l in range(32):
                P.op("pe", lambda e, ps=ps, l=l, i=i, src=src: e.matmul(
                    ps[:, 0:NCMP], lhsT=w1_t[i][:, l, :], rhs=src[:, l:l + 16 * (NCMP - 1) + 1:16],
                    start=(l == 0), stop=(l == 31)), reads=[w1b[i], srcb], writes=[pb])
            pre, preb = f256.next()
            tq, tqb = f256.next()
            P.op("dve", lambda e, ps=ps, pre=pre, pbias=pbias: e.tensor_scalar(
                out=pre[:, 0:NCMP], in0=ps[:, 0:NCMP], scalar1=pbias[:, 0:1], scalar2=None, op0=ALU.add),
                reads=[pb, pbiasb], writes=[preb])
            P.op("dve", lambda e, pre=pre, tq=tq: e.tensor_tensor(out=tq[:, 0:NCMP], in0=pre[:, 0:NCMP], in1=pre[:, 0:NCMP],
                                                                  op=ALU.mult), reads=[preb], writes=[tqb])
            P.op("dve", lambda e, tq=tq: e.tensor_scalar(out=tq[:, 0:NCMP], in0=tq[:, 0:NCMP], scalar1=0.044715, scalar2=1.0,
                                                         op0=ALU.mult, op1=ALU.add), reads=[tqb], writes=[tqb])
            P.op("dve", lambda e, pre=pre, tq=tq: e.tensor_tensor(out=tq[:, 0:NCMP], in0=tq[:, 0:NCMP], in1=pre[:, 0:NCMP],
                                                                  op=ALU.mult), reads=[preb, tqb], writes=[tqb])
            P.op("act", lambda e, tq=tq: e.activation(out=tq[:, 0:NCMP], in_=tq[:, 0:NCMP], func=AF.Sigmoid,
                                                      scale=1.5957691216057308), reads=[tqb], writes=[tqb])
            gl, glb = ptpool.next()
            P.op("dve", lambda e, pre=pre, tq=tq, gl=gl: e.tensor_tensor(out=gl[:, 0:NCMP], in0=tq[:, 0:NCMP],
                                                                          in1=pre[:, 0:NCMP], op=ALU.mult),
                 reads=[preb, tqb], writes=[glb])
            if i == 0:
                ps2, pb2 = ps_s.next()
                P.op("pe", lambda e, ps2=ps2, gl=gl: e.matmul(ps2[:, 0:NCMP], lhsT=w2_t[0][:], rhs=gl[:, 0:NCMP],
                                                              start=True, stop=True), reads=[w2b[0], glb], writes=[pb2])
                P.op("act", lambda e, ps2=ps2: e.activation(out=kcmpT[:, 0:NCMP], in_=ps2[:, 0:NCMP], func=AF.Copy),
                     reads=[pb2], writes=[kcmpTb])
            else:
                for ct, n in ((0, 128), (1, 127)):
                    ps2, pb2 = ps_s.next()
                    P.op("pe", lambda e, ps2=ps2, gl=gl, ct=ct, n=n: e.matmul(
                        ps2[0:n, 0:128], lhsT=gl[:, ct * 128:ct * 128 + n], rhs=w2_t[1][:], start=True, stop=True),
                        reads=[w2b[1], glb], writes=[pb2])
                    P.op("act", lambda e, ps2=ps2, ct=ct, n=n: e.activation(
                        out=vaug2[0:n, ct, 0:128], in_=ps2[0:n, 0:128], func=AF.Copy), reads=[pb2], writes=[vaug2b])

        kslc_t, kslcb = kpool.next()
        P.dma("sp", kslc_t[:], kslc, writes=[kslcb])
        kwin_t, kwinb = kpool.next()
        P.dma("sp", kwin_t[:], kwin, writes=[kwinb])
        vslc_t, vslcb = vpool.next()
        P.dma("sp", vslc_t[:], vslc, writes=[vslcb])
        vwin_t, vwinb = vpool.next()
        P.dma("sp", vwin_t[:], vwin, writes=[vwinb])

        for c in range(NCH):
            qn, qnb = qpool.next()
            P.dma("sp", qn[:], nq[:, :, c * 512:(c + 1) * 512].rearrange("h p t -> p h t"), writes=[qnb])
            imps = [impp.next() for _ in range(4)]
            oaccs = [[onsa.next() for _ in range(4)] for _ in range(2)]
            cts = [0] if c < 4 else [0, 1]
            for hh in range(4):
                accs = [ps_a.next() for _ in range(4)]
                steps = []
                for ct in cts:
                    n = 128 if ct == 0 else 127
                    need_mask = (ct == 1) or (c <= 4)
                    off = 512 * c - 2048 * ct
                    st = {}

                    def qk(st=st, ct=ct, n=n, need_mask=need_mask, off=off, hh=hh, qn=qn, qnb=qnb):
                        ps, pb = ps_s.next()
                        st["ps"], st["pb"] = ps, pb
                        P.op("pe", lambda e: e.matmul(
                            ps[0:n, :], lhsT=kcmpT[:, ct * 128:ct * 128 + n], rhs=qn[:, hh, :], start=True,
                            stop=(not need_mask)), reads=[kcmpTb, qnb], writes=[pb])
                        if need_mask:
                            P.op("pe", lambda e: e.matmul(
                                ps[0:n, :], lhsT=ident[:, 0:n], rhs=tneg[:, off:off + 512], start=False, stop=True),
                                reads=[identb, tnegb], writes=[pb])

                    def ex(st=st, n=n):
                        pt, ptb = ptpool.next()
                        st["pt"], st["ptb"] = pt, ptb
                        exp_act(pt, st["ps"], 0, 512, n, None, [st["pb"]], [ptb])

                    def pv(st=st, ct=ct, n=n, accs=accs, cts=cts):
                        pt, ptb = st["pt"], st["ptb"]
                        for q4 in range(4):
                            acc, accb = accs[q4]
                            P.op("pe", lambda e, acc=acc, q4=q4: e.matmul(
                                acc[:, 0:193], lhsT=pt[0:n, q4 * 128:(q4 + 1) * 128], rhs=vaug2[0:n, ct, :],
                                start=(ct == cts[0]), stop=(ct == cts[-1])), reads=[ptb, vaug2b], writes=[accb])
                    steps.append((qk, ex, pv))

                def epi(accs=accs, hh=hh, c=c, imps=imps, oaccs=oaccs):
                    for q4 in range(4):
                        acc, accb = accs[q4]
                        qg = 4 * c + q4
                        rs, rsb = smallp.next()
                        P.op("dve", lambda e, acc=acc, rs=rs: e.tensor_scalar(
                            out=rs[:, 0:1], in0=acc[:, 128:129], scalar1=1e-30, scalar2=None, op0=ALU.max),
                            reads=[accb], writes=[rsb])
                        P.op("dve", lambda e, rs=rs: e.reciprocal(out=rs[:, 1:2], in_=rs[:, 0:1]), reads=[rsb], writes=[rsb])
                        im, imb = imps[q4]
                        if hh == 0:
                            P.op("dve", lambda e, acc=acc, rs=rs, im=im: e.tensor_scalar(
                                out=im[:], in0=acc[:, 129:193], scalar1=rs[:, 1:2], scalar2=None, op0=ALU.mult),
                                reads=[accb, rsb], writes=[imb])
                        else:
                            P.op("dve", lambda e, acc=acc, rs=rs, im=im: e.scalar_tensor_tensor(
                                out=im[:], in0=acc[:, 129:193], scalar=rs[:, 1:2], in1=im[:], op0=ALU.mult, op1=ALU.add),
                                reads=[accb, rsb, imb], writes=[imb])
                        if hh < 2:
                            oa, oab = oaccs[hh][q4]
                            P.op("dve", lambda e, rs=rs, qg=qg: e.tensor_tensor(
                                out=rs[:, 2:3], in0=rs[:, 1:2], in1=gate_t[:, qg, hh * 3:hh * 3 + 1], op=ALU.mult),
                                reads=[rsb, gateb], writes=[rsb])
                            P.op("dve", lambda e, acc=acc, rs=rs, oa=oa: e.tensor_scalar(
                                out=oa[:], in0=acc[:, 0:128], scalar1=rs[:, 2:3], scalar2=None, op0=ALU.mult),
                                reads=[accb, rsb], writes=[oab])
                    if hh == 3:
                        for q4 in range(4):
                            qg = 4 * c + q4
                            im, imb = imps[q4]
                            adj, adjb = impp.next()
                            P.op("dve", lambda e, im=im, adj=adj, qg=qg: e.tensor_tensor(
                                out=adj[:], in0=im[:], in1=ufix[:, qg, :], op=ALU.add), reads=[imb, ufixb], writes=[adjb])
                            t8, t8b = smallp.next()
                            P.op("dve", lambda e, adj=adj, t8=t8: e.max(out=t8[:, 0:8], in_=adj[:]), reads=[adjb], writes=[t8b])
                            thr, thrb = smallp.next()
                            P.op("dve", lambda e, t8=t8, thr=thr: e.tensor_reduce(
                                out=thr[:, 0:1], in_=t8[:, 0:8], axis=AX.X, op=ALU.min), reads=[t8b], writes=[thrb])
                            selm, selmb = obfp.next()
                            P.op("dve", lambda e, adj=adj, thr=thr, selm=selm: e.tensor_scalar(
                                out=selm[:, 0:64], in0=adj[:], scalar1=thr[:, 0:1], scalar2=1.0, op0=ALU.is_ge,
                                op1=ALU.subtract), reads=[adjb, thrb], writes=[selmb])
                            P.op("pe", lambda e, selm=selm, q4=q4: e.transpose(
                                out=pst[0:64, q4 * 128:(q4 + 1) * 128], in_=selm[:, 0:64], identity=ident[:]),
                                reads=[selmb, identb], writes=[pstb])
                        P.op("act", lambda e: e.activation(out=selT[0:64, c * 512:(c + 1) * 512], in_=pst[0:64, 0:512],
                                                           func=AF.Copy), reads=[pstb], writes=[selTb])
                run_pass(steps, epi)
            for hl in range(2):
                accs = [ps_a.next() for _ in range(4)]
                steps = []
                for kt in range(max(0, 4 * c - 4), 4 * c + 4):
                    qlo = max(kt, 4 * c) - 4 * c
                    qhi = min(kt + 4, 4 * c + 3) - 4 * c
                    lo, hi = qlo * 128, (qhi + 1) * 128
                    diag = kt >= 4 * c
                    far = (kt + 4 >= 4 * c) and (kt + 4 <= 4 * c + 3)
                    st = {}

                    def qk(st=st, kt=kt, lo=lo, hi=hi, diag=diag, far=far, hl=hl, qn=qn, qnb=qnb, c=c):
                        ps, pb = ps_s.next()
                        st["ps"], st["pb"] = ps, pb
                        P.op("pe", lambda e: e.matmul(
                            ps[:, lo:hi], lhsT=kwin_t[:, kt * 128:(kt + 1) * 128], rhs=qn[:, hl, lo:hi],
                            start=True, stop=False), reads=[kwinb, qnb], writes=[pb])
                        P.op("pe", lambda e: e.matmul(
                            ps[:, lo:hi], lhsT=slrow[0:2, hl, :], rhs=hilo[0:2, lo:hi],
                            start=False, stop=(not diag and not far)), reads=[slrowb, hilob], writes=[pb])
                        if diag:
                            q4d = kt - 4 * c
                            P.op("pe", lambda e: e.matmul(
                                ps[:, q4d * 128:(q4d + 1) * 128], lhsT=ident[:], rhs=causal[:], start=False, stop=(not far)),
                                reads=[identb, causalb], writes=[pb])
                        if far:
                            q4f = kt + 4 - 4 * c
                            P.op("pe", lambda e: e.matmul(
                                ps[:, q4f * 128:(q4f + 1) * 128], lhsT=ident[:], rhs=winneg[:], start=False, stop=True),
                                reads=[identb, winnegb], writes=[pb])

                    def ex(st=st, kt=kt, lo=lo, hi=hi, hl=hl, c=c):
                        pt, ptb = ptpool.next()
                        st["pt"], st["ptb"] = pt, ptb
                        exp_act(pt, st["ps"], lo, hi, 128, biasT[:, 2 + hl, kt - 4 * c + 28:kt - 4 * c + 29],
                                [st["pb"]], [ptb])

                    def pv(st=st, kt=kt, qlo=qlo, qhi=qhi, accs=accs, c=c):
                        pt, ptb = st["pt"], st["ptb"]
                        for q4 in range(qlo, qhi + 1):
                            acc, accb = accs[q4]
                            qg = 4 * c + q4
                            P.op("pe", lambda e, acc=acc, q4=q4, qg=qg: e.matmul(
                                acc[:, 0:129], lhsT=pt[:, q4 * 128:(q4 + 1) * 128], rhs=vwin_t[:, kt, :],
                                start=(kt == max(0, qg - 4)), stop=(kt == qg)), reads=[ptb, vwinb], writes=[accb])
                    steps.append((qk, ex, pv))

                def epi(accs=accs, hl=hl, c=c, oaccs=oaccs):
                    for q4 in range(4):
                        acc, accb = accs[q4]
                        qg = 4 * c + q4
                        rs, rsb = smallp.next()
                        P.op("dve", lambda e, acc=acc, rs=rs: e.reciprocal(out=rs[:, 0:1], in_=acc[:, 128:129]),
                             reads=[accb], writes=[rsb])
                        P.op("dve", lambda e, rs=rs, qg=qg: e.tensor_tensor(
                            out=rs[:, 1:2], in0=rs[:, 0:1], in1=gate_t[:, qg, hl * 3 + 2:hl * 3 + 3], op=ALU.mult),
                            reads=[rsb, gateb], writes=[rsb])
                        oa, oab = oaccs[hl][q4]
                        P.op("dve", lambda e, acc=acc, rs=rs, oa=oa: e.scalar_tensor_tensor(
                            out=oa[:], in0=acc[:, 0:128], scalar=rs[:, 1:2], in1=oa[:], op0=ALU.mult, op1=ALU.add),
                            reads=[accb, rsb, oab], writes=[oab])
                run_pass(steps, epi)
            for hl in range(2):
                def mask_mm(ps, pb, kt, lo, last, hl=hl, c=c):
                    P.op("pe", lambda e: e.matmul(
                        ps[:, lo:512], lhsT=eaug[hl][0:66, kt * 128:(kt + 1) * 128],
                        rhs=selT[0:66, c * 512 + lo:(c + 1) * 512], start=False, stop=last),
                        reads=[eaugb[hl], selTb], writes=[pb])
                accs = [ps_a.next() for _ in range(4)]
                steps = causal_steps(kslc_t, kslcb, 128, qnb, (lambda lo, qn=qn, hl=hl: qn[:, hl, lo:512]),
                                     vslc_t, vslcb, 2 + hl, c, accs, extra_mm=mask_mm)

                def epi(accs=accs, hl=hl, c=c, oaccs=oaccs):
                    for q4 in range(4):
                        acc, accb = accs[q4]
                        qg = 4 * c + q4
                        rs, rsb = smallp.next()
                        P.op("dve", lambda e, acc=acc, rs=rs: e.reciprocal(out=rs[:, 0:1], in_=acc[:, 128:129]),
                             reads=[accb], writes=[rsb])
                        P.op("dve", lambda e, rs=rs, qg=qg: e.tensor_tensor(
                            out=rs[:, 1:2], in0=rs[:, 0:1], in1=gate_t[:, qg, hl * 3 + 1:hl * 3 + 2], op=ALU.mult),
                            reads=[rsb, gateb], writes=[rsb])
                        oa, oab = oaccs[hl][q4]
                        ob, obb = obfp.next()
                        P.op("dve", lambda e, acc=acc, rs=rs, oa=oa, ob=ob: e.scalar_tensor_tensor(
                            out=ob[:], in0=acc[:, 0:128], scalar=rs[:, 1:2], in1=oa[:], op0=ALU.mult, op1=ALU.add),
                            reads=[accb, rsb, oab], writes=[obb])
                        q0 = qg * 128
                        P.dma("sp", o_nsa[q0:q0 + 128, hl, :], ob[:], reads=[obb], is_output=True)
                run_pass(steps, epi)
        pipe.flush()
        P.emit()
    return nc


import ml_dtypes

BF = ml_dtypes.bfloat16
_NC = {}
DEBUG = {}


def _prog(name):
    if name not in _NC:
        _NC[name] = build_token(name) if name in ("pre", "post") else build_attn()
    return _NC[name]


def _run(name, in_maps):
    res = run_bass_kernel_spmd(_prog(name), in_maps, core_ids=list(range(8)))
    return res.results


def _lay(g):
    return np.ascontiguousarray(np.asarray(g, np.float32).reshape(NKT, 128).T)


def _tile_tok(a):
    e = a.shape[1]
    return np.ascontiguousarray(a.reshape(NQT, 128, e).transpose(1, 0, 2))


_CONST = {}


def _consts():
    if _CONST:
        return _CONST
    kk = np.arange(128)[:, None]
    qq = np.arange(128)[None, :]
    _CONST["c_ident"] = np.eye(128, dtype=np.float32).astype(BF)
    _CONST["c_causal"] = np.where(kk > qq, NEG, 0.0).astype(np.float32).astype(BF)
    _CONST["c_winneg"] = np.where(kk <= qq, NEG, 0.0).astype(np.float32).astype(BF)
    z = np.arange(2560)[None, :]
    _CONST["c_tneg"] = np.where(16 * kk + 31 <= z, 0.0, NEG).astype(np.float32).astype(BF)
    p = np.arange(128)[:, None, None]
    qg = np.arange(NQT)[None, :, None]
    j = np.arange(64)[None, None, :]
    cur = 2 * qg + (p >= 64)
    forced = (j == 0) | (j == cur) | (j == cur - 1)
    future = j > cur
    _CONST["c_ufix"] = np.where(forced, BIGSEL, np.where(future, -BIGSEL, 0.0)).astype(np.float32)
    q512 = np.arange(S) % 512
    hi = (q512 // 2) * 2
    lo = q512 % 2
    _CONST["c_selaug"] = (-np.stack([hi, lo]).astype(np.float32)).astype(BF)
    c = np.arange(256)[:, None]
    jj = np.arange(64)[None, :]
    ovl = ((16 * c < 64 * jj + 64) & (16 * c + 32 > 64 * jj) & (c < NCMP)).astype(np.float32)
    o65 = np.concatenate([(c < NCMP).astype(np.float32), ovl], axis=1)
    _CONST["c_ovl"] = o65.reshape(2, 128, 65).astype(BF)
    col = np.arange(S)[None, :]
    _CONST["erows"] = np.where(col // 64 == np.arange(64)[:, None], 30000.0, 0.0).astype(np.float32)
    return _CONST


def _slope(h):
    return 2.0 ** (-(h + 1.0))


def kernel(**inputs):
    inp = {k: np.asarray(v) for k, v in inputs.items()}
    x = inp["x"].astype(np.float32).reshape(8 * TOK, D)
    mem = inp["mem"].astype(np.float32)
    cst = _consts()
    xT = [np.ascontiguousarray(x[c * TOK:(c + 1) * TOK].T) for c in range(8)]
    depth = inp["w_in"].shape[0]
    yT = None
    for l in range(depth):
        maps = [{"xT": xT[c], "gn": _lay(inp["ffn1_norm"][l]), "gm": _lay(inp["mix_norm"][l]),
                 "wg": inp["ffn1_w_gate"][l], "wu": inp["ffn1_w_up"][l], "wd": inp["ffn1_w_down"][l],
                 "win": inp["w_in"][l]} for c in range(8)]
        r1 = _run("pre", maps)
        x1T = [r["xoT"] for r in r1]
        gT = [r["gT"] for r in r1]
        lam_init = 0.8 - 0.6 * math.exp(-0.3 * l)
        maps = []
        for b in range(2):
            featB = np.concatenate([np.asarray(r1[4 * b + i]["featT"]) for i in range(4)], axis=1)
            tokVB = np.concatenate([np.asarray(r1[4 * b + i]["tokV"]) for i in range(4)], axis=0)
            gatesB = np.concatenate([np.asarray(r1[4 * b + i]["gates"]) for i in range(4)], axis=0)
            onesc = np.ones((S, 1), BF)
            for r in range(4):
                g = r // 2
                m = {}
                dqa = np.zeros((2, 2, 66, S), BF)
                dka = np.zeros((2, 2, 66, S), BF)
                dva = np.zeros((2, 128, NQT, 129), BF)
                for hl in range(2):
                    h = 2 * r + hl
                    for mm in range(2):
                        dqa[hl, mm, 0:64] = featB[h * 128 + mm * 64:h * 128 + mm * 64 + 64]
                        dqa[hl, mm, 64:66] = cst["c_selaug"]
                        dka[hl, mm, 0:64] = featB[1024 + h * 128 + mm * 64:1024 + h * 128 + mm * 64 + 64]
                        dka[hl, mm, 64:66] = np.float32(_slope(h))
                    dva[hl] = _tile_tok(np.concatenate([tokVB[:, h * 128:(h + 1) * 128], onesc], axis=1))
                m["dqa"], m["dka"], m["dva"] = dqa, dka, dva
                m["lamp"] = np.ascontiguousarray(np.broadcast_to(inp["diff_lambda"][l].astype(np.float32)[None], (128, 4, 64)))
                m["subg"] = np.ascontiguousarray(np.broadcast_to(inp["diff_subln"][l].astype(np.float32)[None], (128, 128)))
                m["lami"] = np.full((128, 1), lam_init, np.float32)
                outh = [2 * r, 2 * r + 1]
                hh_list = outh + [hh for hh in range(4 * g, 4 * g + 4) if hh not in outh]
                m["nq"] = np.stack([featB[2048 + hh * 128:2048 + (hh + 1) * 128] for hh in hh_list])
                m["kcs"] = np.stack([featB[3072 + g * 128:3072 + (g + 1) * 128], featB[3328 + g * 128:3328 + (g + 1) * 128]])
                m["kslc"] = np.ascontiguousarray(featB[3584 + g * 128:3584 + (g + 1) * 128])
                m["kwin"] = np.ascontiguousarray(featB[3840 + g * 128:3840 + (g + 1) * 128])
                m["vslc"] = _tile_tok(np.concatenate([tokVB[:, 1024 + g * 128:1024 + (g + 1) * 128], onesc], axis=1))
                m["vwin"] = _tile_tok(np.concatenate([tokVB[:, 1280 + g * 128:1280 + (g + 1) * 128], onesc], axis=1))
                m["ngate"] = _tile_tok(np.concatenate([gatesB[:, hh * 3:hh * 3 + 3] for hh in outh], axis=1))
                m["w1"] = inp["nsa_cmp_w1"][l].astype(np.float32)
                m["w2"] = inp["nsa_cmp_w2"][l].astype(np.float32)
                m["posT"] = np.ascontiguousarray(inp["nsa_cmp_pos"][l].astype(np.float32).transpose(0, 2, 1))
                m["mq"] = np.ascontiguousarray(featB[4096 + r * 128:4096 + (r + 1) * 128])
                m["memT"] = np.ascontiguousarray(mem[b].T)
                m["gmem"] = _lay(inp["mem_norm"][l])
                m["wmk"] = np.ascontiguousarray(inp["w_mem_kv"][l][:, r * 128:(r + 1) * 128].astype(np.float32))
                m["wmv"] = np.ascontiguousarray(inp["w_mem_kv"][l][:, 512 + r * 128:512 + (r + 1) * 128].astype(np.float32))
                for k in ("c_ident", "c_causal", "c_winneg", "c_tneg", "c_ufix", "c_selaug", "c_ovl"):
                    m[k] = cst[k]
                eaug = np.zeros((2, 66, S), np.float32)
                slrow = np.zeros((2, 2, 128), np.float32)
                bias = np.zeros((128, 4, 32), np.float32)
                pp = np.arange(128, dtype=np.float32)[:, None]
                idx = np.arange(32, dtype=np.float32)[None, :]
                for hl in range(2):
                    eaug[hl, 0:64] = cst["erows"]
                    eaug[hl, 64:66] = _slope(outh[hl])
                    slrow[hl] = _slope(outh[hl])
                    bias[:, hl, :] = _slope(2 * r + hl) * (pp + 128.0 * (idx - 28.0))
                    bias[:, 2 + hl, :] = _slope(outh[hl]) * (pp + 128.0 * (idx - 28.0))
                m["c_eaug"] = eaug.astype(BF)
                m["c_slrow"] = slrow.astype(BF)
                m["c_bias"] = bias
                maps.append(m)
        r2 = _run("attn", maps)
        if DEBUG.get("on"):
            DEBUG["r1_%d" % l] = r1
            DEBUG["r2_%d" % l] = r2
        maps = []
        for c in range(8):
            b, tq = c // 4, c % 4
            sl = slice(tq * TOK, (tq + 1) * TOK)
            parts = [np.asarray(r2[4 * b + r]["o_diff"])[sl].reshape(TOK, 256) for r in range(4)]
            parts += [np.asarray(r2[4 * b + r]["o_nsa"])[sl].reshape(TOK, 256) for r in range(4)]
            parts += [np.asarray(r2[4 * b + r]["o_mem"])[sl] for r in range(4)]
            oT = np.ascontiguousarray(np.concatenate(parts, axis=1).T)
            maps.append({"xT": x1T[c], "gn": _lay(inp["ffn2_norm"][l]), "wg": inp["ffn2_w_gate"][l],
                         "wu": inp["ffn2_w_up"][l], "wd": inp["ffn2_w_down"][l], "oT": oT, "gT": gT[c],
                         "wud": inp["w_up_diff"][l], "wun": inp["w_up_nsa"][l], "wum": inp["w_up_mem"][l],
                         "wo": inp["w_out"][l], "gf": _lay(inp["final_norm"])})
        r3 = _run("post", maps)
        xT = [r["xoT"] for r in r3]
        yT = [r["yT"] for r in r3]
        if DEBUG.get("on"):
            DEBUG["r3_%d" % l] = r3
            if DEBUG.get("stop_after") == l:
                break
    out = np.concatenate([np.asarray(y).T for y in yT], axis=0).reshape(2, S, D).astype(np.float32)
    return out
```

```python
import math
from contextlib import ExitStack
import numpy as np
import concourse.bass as bass
import concourse.mybir as mybir
from concourse.bass_utils import run_bass_kernel_spmd

F32 = mybir.dt.float32
BF16 = mybir.dt.bfloat16
AF = mybir.ActivationFunctionType
ALU = mybir.AluOpType
AX = mybir.AxisListType

ENGS = ("pe", "act", "dve", "pool", "sp")
N_DMA_SLOTS = 8


class Buf:
    __slots__ = ("name", "w", "r")

    def __init__(self, name):
        self.name = name
        self.w = None
        self.r = []


class Op:
    __slots__ = ("id", "eng", "fn", "deps", "is_dma", "slot", "semval", "prev_slot_op",
                 "signaled", "sigval")

    def __init__(self, id, eng, fn, is_dma):
        self.id = id
        self.eng = eng
        self.fn = fn
        self.deps = []
        self.is_dma = is_dma
        self.slot = None
        self.semval = 0
        self.prev_slot_op = None
        self.signaled = False
        self.sigval = 0


class Prog:
    def __init__(self, nc):
        self.nc = nc
        self.ops = []
        self.by_eng = {e: [] for e in ENGS}
        self.slot_last = {e: [None] * N_DMA_SLOTS for e in ENGS}
        self.slot_cnt = {e: [0] * N_DMA_SLOTS for e in ENGS}
        self.slot_rr = {e: 0 for e in ENGS}
        self.out_dmas = []

    def _add(self, eng, fn, reads, writes, is_dma):
        op = Op(len(self.ops), eng, fn, is_dma)
        deps = set()
        for b in reads:
            if b.w is not None:
                deps.add(b.w)
        for b in writes:
            if b.w is not None:
                deps.add(b.w)
            for r in b.r:
                deps.add(r)
        for b in writes:
            b.w = op.id
            b.r = []
        for b in reads:
            if b.w != op.id:
                b.r.append(op.id)
        deps.discard(op.id)
        for d in sorted(deps):
            t = self.ops[d]
            if (not is_dma) and eng == "pe" and t.eng == "pe" and not t.is_dma:
                continue
            op.deps.append(d)
            if not t.is_dma:
                t.signaled = True
        if is_dma:
            s = self.slot_rr[eng]
            self.slot_rr[eng] = (s + 1) % N_DMA_SLOTS
            op.slot = s
            op.prev_slot_op = self.slot_last[eng][s]
            self.slot_cnt[eng][s] += 16
            op.semval = self.slot_cnt[eng][s]
            self.slot_last[eng][s] = op.id
        self.ops.append(op)
        self.by_eng[eng].append(op)
        return op

    def op(self, eng, fn, reads=(), writes=()):
        return self._add(eng, fn, reads, writes, False)

    def dma(self, eng, out, in_, reads=(), writes=(), is_output=False):
        op = self._add(eng, lambda e: e.dma_start(out=out, in_=in_), reads, writes, True)
        if is_output:
            self.out_dmas.append(op.id)
        return op

    def emit(self):
        nc = self.nc
        with ExitStack() as es:
            es.enter_context(nc.cleanup_on_exit())
            esem = {e: nc.alloc_semaphore(name="s_" + e) for e in ENGS}
            dsem = {e: [nc.alloc_semaphore(name="d_%s%d" % (e, i)) for i in range(N_DMA_SLOTS)]
                    for e in ("sp", "pool")}
            for e in ENGS:
                c = 0
                for op in self.by_eng[e]:
                    if op.signaled and not op.is_dma:
                        c += 1
                        op.sigval = c
            block = nc.Block()
            block.__enter__()
            ops = self.ops
            out_dmas = self.out_dmas

            def stream(eng_name, h):
                seen = {}

                def wait(sem, key, val):
                    if seen.get(key, 0) >= val:
                        return
                    h.wait_ge(sem, val)
                    seen[key] = val

                for op in self.by_eng[eng_name]:
                    for d in op.deps:
                        t = ops[d]
                        if t.is_dma:
                            wait(dsem[t.eng][t.slot], (t.eng, t.slot), t.semval)
                        else:
                            wait(esem[t.eng], t.eng, t.sigval)
                    if op.is_dma:
                        if op.prev_slot_op is not None:
                            p = ops[op.prev_slot_op]
                            wait(dsem[eng_name][op.slot], (eng_name, op.slot), p.semval)
                        ins = op.fn(h)
                        ins.then_inc(dsem[eng_name][op.slot], 16)
                    else:
                        ins = op.fn(h)
                        if op.signaled:
                            ins.then_inc(esem[eng_name], 1)
                if eng_name == "sp":
                    for d in out_dmas:
                        t = ops[d]
                        wait(dsem[t.eng][t.slot], (t.eng, t.slot), t.semval)
                    for q in ("sp", "pool"):
                        for s in range(N_DMA_SLOTS):
                            if self.slot_cnt[q][s] > 0:
                                wait(dsem[q][s], (q, s), self.slot_cnt[q][s])

            @block.tensor
            def _(e):
                stream("pe", e)

            @block.scalar
            def _(e):
                stream("act", e)

            @block.vector
            def _(e):
                stream("dve", e)

            @block.gpsimd
            def _(e):
                stream("pool", e)

            @block.sync
            def _(e):
                stream("sp", e)

            block.__exit__(None, None, None)
            nc.all_engine_barrier()


D = 2048
DFF = 5632
NKT = D // 128
NFT = DFF // 128
TOK = 1024
TC = 512
IN_COLS = 12312
EPS = 1e-6
S128 = 128.0 ** -0.5


class RPool:
    def __init__(self, es, nc, name, shape, dtype, n, psum=False):
        mk = nc.psum_tensor if psum else nc.sbuf_tensor
        self.t = [es.enter_context(mk("%s%d" % (name, i), shape, dtype)) for i in range(n)]
        self.b = [Buf("%s%d" % (name, i)) for i in range(n)]
        self.i = 0
        self.n = n

    def next(self):
        i = self.i
        self.i = (i + 1) % self.n
        return self.t[i], self.b[i]


def build_token(mode):
    nc = bass.Bass("TRN2", target_bir_lowering=False)

    def din(name, shape, dt=F32):
        return nc.dram_tensor(name, shape, dt, kind="ExternalInput").ap()

    def dout(name, shape, dt=F32):
        return nc.dram_tensor(name, shape, dt, kind="ExternalOutput").ap()

    xT = din("xT", [D, TOK])
    gn = din("gn", [128, NKT])
    wg = din("wg", [D, DFF])
    wu = din("wu", [D, DFF])
    wd = din("wd", [DFF, D])
    if mode == "pre":
        gm = din("gm", [128, NKT])
        win = din("win", [D, IN_COLS])
        xoT = dout("xoT", [D, TOK])
        featT = dout("featT", [36 * 128, TOK], BF16)
        gTo = dout("gT", [6144, TOK])
        tokV = dout("tokV", [TOK, 1536], BF16)
        gates = dout("gates", [TOK, 24])
    else:
        oT = din("oT", [2560, TOK], BF16)
        gTi = din("gT", [6144, TOK])
        wud = din("wud", [1024, D])
        wun = din("wun", [1024, D])
        wum = din("wum", [512, D])
        wo = din("wo", [D, D])
        gf = din("gf", [128, NKT])
        xoT = dout("xoT", [D, TOK])
        yT = dout("yT", [D, TOK])

    NC2 = TOK // TC
    HF = NFT // 2

    with ExitStack() as es:
        P = Prog(nc)
        sb = lambda name, shape, dt: es.enter_context(nc.sbuf_tensor(name, shape, dt))
        x_t = sb("x_t", [128, NKT, TC], F32)
        xb = [Buf("x%d" % i) for i in range(NKT)]
        h_t = sb("h_t", [128, NKT, TOK], BF16)
        hb = [Buf("h%d" % i) for i in range(NKT)]
        hid = sb("hid", [128, HF, TOK], BF16)
        hidb = [Buf("hid%d" % i) for i in range(HF)]
        ones = sb("ones", [128, 128], BF16)
        onesb = Buf("ones")
        gn_t = sb("gn_t", [128, NKT], F32)
        gnb = Buf("gn")
        g2_t = sb("g2_t", [128, NKT], F32)
        g2b = Buf("g2")
        wpool = RPool(es, nc, "wp", [128, NKT, 256], BF16, 4)
        wdpool = RPool(es, nc, "wdp", [128, HF, 256], BF16, 2)
        psum = RPool(es, nc, "ps", [128, 512], F32, 8, psum=True)
        sqpool = RPool(es, nc, "sq", [128, TC], BF16, 3)
        f32pool = RPool(es, nc, "f32p", [128, TC], F32, 8)
        rstdpool = RPool(es, nc, "rstd", [128, TC], F32, 2)
        bfpool = RPool(es, nc, "bfp", [128, TC], BF16, 4)
        if mode == "post":
            gpool = RPool(es, nc, "gp", [128, 3, TC], F32, 2)

        epsc = sb("epsc", [128, 1], F32)
        epsb = Buf("eps")
        P.op("dve", lambda e: e.memset(ones[:], 1.0), writes=[onesb])
        P.op("dve", lambda e: e.memset(epsc[:], EPS), writes=[epsb])
        P.dma("sp", gn_t[:], gn, writes=[gnb])
        P.dma("sp", g2_t[:], gm if mode == "pre" else gf, writes=[g2b])

        xTv = xT.rearrange("(kt p) t -> p kt t", p=128)
        xoTv = xoT.rearrange("(kt p) t -> p kt t", p=128)
        xob = [[Buf("xo%d_%d" % (d, c)) for c in range(NC2)] for d in range(NKT)]
        wgv = wg.rearrange("(kt p) f -> p kt f", p=128)
        wuv = wu.rearrange("(kt p) f -> p kt f", p=128)
        wdv = wd.rearrange("(ft p) d -> p ft d", p=128)

        def load_w(view, nk, c0, w):
            wt, wb = wpool.next()
            P.dma("pool", wt[:, 0:nk, 0:w], view[:, :, c0:c0 + w], writes=[wb])
            return wt, wb

        def load_x_chunk(srcv, c, from_out):
            for kt in range(NKT):
                P.dma("sp", x_t[:, kt, :], srcv[:, kt, c * TC:(c + 1) * TC],
                      reads=([xob[kt][c]] if from_out else []), writes=[xb[kt]])

        def rmsnorm_stats():
            ps, pb = psum.next()
            for kt in range(NKT):
                sq, sqb = sqpool.next()
                P.op("act", lambda e, kt=kt, sq=sq: e.activation(out=sq[:], in_=x_t[:, kt, :], func=AF.Square),
                     reads=[xb[kt]], writes=[sqb])
                P.op("pe", lambda e, kt=kt, sq=sq, ps=ps: e.matmul(ps[:], lhsT=ones[:], rhs=sq[:],
                                                                    start=(kt == 0), stop=(kt == NKT - 1)),
                     reads=[sqb, onesb], writes=[pb])
            rstd, rb = rstdpool.next()
            P.op("act", lambda e, ps=ps, rstd=rstd: e.activation(out=rstd[:], in_=ps[:], func=AF.Sqrt, bias=epsc[:],
                                                                 scale=1.0 / D), reads=[pb, epsb], writes=[rb])
            P.op("dve", lambda e, rstd=rstd: e.reciprocal(out=rstd[:], in_=rstd[:]), reads=[rb], writes=[rb])
            return rstd, rb

        def rmsnorm_to_h(g_t, gb, c):
            rstd, rb = rmsnorm_stats()
            for kt in range(NKT):
                P.op("dve", lambda e, kt=kt, rstd=rstd: e.scalar_tensor_tensor(
                    out=h_t[:, kt, c * TC:(c + 1) * TC], in0=x_t[:, kt, :], scalar=g_t[:, kt:kt + 1], in1=rstd[:],
                    op0=ALU.mult, op1=ALU.mult), reads=[xb[kt], rb, gb], writes=[hb[kt]])

        def residual_rmw(po, pob, d, c, scale, srcv, from_out):
            xs, xsb = f32pool.next()
            P.dma("sp", xs[:], srcv[:, d, c * TC:(c + 1) * TC], reads=([xob[d][c]] if from_out else []), writes=[xsb])
            P.op("dve", lambda e: e.scalar_tensor_tensor(out=xs[:], in0=po[:], scalar=scale, in1=xs[:],
                                                         op0=ALU.mult, op1=ALU.add), reads=[pob, xsb], writes=[xsb])
            P.dma("sp", xoTv[:, d, c * TC:(c + 1) * TC], xs[:], reads=[xsb], writes=[xob[d][c]], is_output=True)

        def ffn(srcv, from_out):
            for c in range(NC2):
                load_x_chunk(srcv, c, from_out)
                rmsnorm_to_h(gn_t, gnb, c)
            for half in range(2):
                for fc in range(HF // 2):
                    col0 = (half * HF + fc * 2) * 128
                    wgt, wgb = load_w(wgv, NKT, col0, 256)
                    wut, wub = load_w(wuv, NKT, col0, 256)
                    for j in range(2):
                        fl = fc * 2 + j
                        for c in range(NC2):
                            pg, pgb = psum.next()
                            pu, pub = psum.next()
                            for kt in range(NKT):
                                P.op("pe", lambda e, kt=kt, pg=pg, wgt=wgt, j=j, c=c: e.matmul(
                                    pg[:], lhsT=wgt[:, kt, j * 128:(j + 1) * 128], rhs=h_t[:, kt, c * TC:(c + 1) * TC],
                                    start=(kt == 0), stop=(kt == NKT - 1)), reads=[wgb, hb[kt]], writes=[pgb])
                            for kt in range(NKT):
                                P.op("pe", lambda e, kt=kt, pu=pu, wut=wut, j=j, c=c: e.matmul(
                                    pu[:], lhsT=wut[:, kt, j * 128:(j + 1) * 128], rhs=h_t[:, kt, c * TC:(c + 1) * TC],
                                    start=(kt == 0), stop=(kt == NKT - 1)), reads=[wub, hb[kt]], writes=[pub])
                            sg, sgb = f32pool.next()
                            P.op("act", lambda e, sg=sg, pg=pg: e.activation(out=sg[:], in_=pg[:], func=AF.Silu),
                                 reads=[pgb], writes=[sgb])
                            P.op("dve", lambda e, sg=sg, pu=pu, fl=fl, c=c: e.tensor_tensor(
                                out=hid[:, fl, c * TC:(c + 1) * TC], in0=sg[:], in1=pu[:], op=ALU.mult),
                                reads=[sgb, pub], writes=[hidb[fl]])
                for dc in range(D // 256):
                    wdt, wdb = wdpool.next()
                    P.dma("pool", wdt[:], wdv[:, half * HF:(half + 1) * HF, dc * 256:(dc + 1) * 256], writes=[wdb])
                    for j in range(2):
                        d = dc * 2 + j
                        for c in range(NC2):
                            po, pob = psum.next()
                            for fl in range(HF):
                                P.op("pe", lambda e, fl=fl, po=po, wdt=wdt, j=j, c=c: e.matmul(
                                    po[:], lhsT=wdt[:, fl, j * 128:(j + 1) * 128], rhs=hid[:, fl, c * TC:(c + 1) * TC],
                                    start=(fl == 0), stop=(fl == HF - 1)), reads=[wdb, hidb[fl]], writes=[pob])
                            if half == 0:
                                residual_rmw(po, pob, d, c, 0.5, srcv, from_out)
                            else:
                                residual_rmw(po, pob, d, c, 0.5, xoTv, True)

        if mode == "pre":
            ffn(xTv, False)
            for c in range(NC2):
                load_x_chunk(xoTv, c, True)
                rmsnorm_to_h(g2_t, g2b, c)
            winv = win.rearrange("(kt p) f -> p kt f", p=128)
            plan = []
            for i in range(4):
                plan.append((i * 256, 256, "feat", i * 256, 0.125))
            for i in range(4):
                plan.append((1024 + i * 256, 256, "feat", 1024 + i * 256, 1.0))
            for i in range(4):
                plan.append((2048 + i * 256, 256, "tokv", i * 256, 1.0))
            for i in range(4):
                plan.append((3072 + i * 256, 256, "feat", 2048 + i * 256, S128))
            plan.append((4096, 256, "feat", 3072, 1.0))
            plan.append((4352, 256, "feat", 3328, 1.0))
            plan.append((4608, 256, "feat", 3584, 1.0))
            plan.append((4864, 256, "tokv", 1024, 1.0))
            plan.append((5120, 256, "feat", 3840, 1.0))
            plan.append((5376, 256, "tokv", 1280, 1.0))
            plan.append((5632, 24, "gates", 0, 1.0))
            for i in range(2):
                plan.append((5656 + i * 256, 256, "feat", 4096 + i * 256, S128))
            for i in range(24):
                plan.append((6168 + i * 256, 256, "mg", i * 256, 1.0))
            for (c0, w, kind, dst, scale) in plan:
                wt, wb = load_w(winv, NKT, c0, w)
                if kind in ("feat", "mg"):
                    for j in range(w // 128):
                        for c in range(NC2):
                            t0 = c * TC
                            ps, pb = psum.next()
                            for kt in range(NKT):
                                P.op("pe", lambda e, kt=kt, ps=ps, wt=wt, j=j, t0=t0: e.matmul(
                                    ps[:], lhsT=wt[:, kt, j * 128:(j + 1) * 128], rhs=h_t[:, kt, t0:t0 + TC],
                                    start=(kt == 0), stop=(kt == NKT - 1)), reads=[wb, hb[kt]], writes=[pb])
                            r0 = dst + j * 128
                            if kind == "feat":
                                st, stb = bfpool.next()
                                P.op("act", lambda e, st=st, ps=ps, scale=scale: e.activation(
                                    out=st[:], in_=ps[:], func=AF.Copy, scale=scale), reads=[pb], writes=[stb])
                                P.dma("sp", featT[r0:r0 + 128, t0:t0 + TC], st[:], reads=[stb], is_output=True)
                            else:
                                st, stb = f32pool.next()
                                P.op("act", lambda e, st=st, ps=ps: e.activation(
                                    out=st[:], in_=ps[:], func=AF.Sigmoid), reads=[pb], writes=[stb])
                                P.dma("sp", gTo[r0:r0 + 128, t0:t0 + TC], st[:], reads=[stb], is_output=True)
                else:
                    for tt in range(TOK // 128):
                        ps, pb = psum.next()
                        for kt in range(NKT):
                            P.op("pe", lambda e, kt=kt, ps=ps, wt=wt, tt=tt, w=w: e.matmul(
                                ps[:, 0:w], lhsT=h_t[:, kt, tt * 128:(tt + 1) * 128], rhs=wt[:, kt, 0:w],
                                start=(kt == 0), stop=(kt == NKT - 1)), reads=[wb, hb[kt]], writes=[pb])
                        if kind == "tokv":
                            st, stb = bfpool.next()
                            P.op("act", lambda e, st=st, ps=ps, w=w: e.activation(
                                out=st[:, 0:w], in_=ps[:, 0:w], func=AF.Copy), reads=[pb], writes=[stb])
                            P.dma("sp", tokV[tt * 128:(tt + 1) * 128, dst:dst + w], st[:, 0:w],
                                  reads=[stb], is_output=True)
                        else:
                            st, stb = f32pool.next()
                            P.op("act", lambda e, st=st, ps=ps, w=w: e.activation(
                                out=st[:, 0:w], in_=ps[:, 0:w], func=AF.Sigmoid), reads=[pb], writes=[stb])
                            P.dma("sp", gates[tt * 128:(tt + 1) * 128, 0:w], st[:, 0:w],
                                  reads=[stb], is_output=True)
        else:
            oTv = oT.rearrange("(kt p) t -> p kt t", p=128)
            gTv = gTi.rearrange("(br dt p) t -> p br dt t", br=3, p=128)
            wudv = wud.rearrange("(kt p) f -> p kt f", p=128)
            wunv = wun.rearrange("(kt p) f -> p kt f", p=128)
            wumv = wum.rearrange("(kt p) f -> p kt f", p=128)
            wov = wo.rearrange("(kt p) f -> p kt f", p=128)
            for kt in range(20):
                P.dma("sp", hid[:, kt, :], oTv[:, kt, :], writes=[hidb[kt]])
            for dc in range(D // 256):
                w0, w0b = load_w(wudv, 8, dc * 256, 256)
                w1, w1b = load_w(wunv, 8, dc * 256, 256)
                w2, w2b = load_w(wumv, 4, dc * 256, 256)
                for j in range(2):
                    d = dc * 2 + j
                    for c in range(NC2):
                        t0 = c * TC
                        gt, gtb = gpool.next()
                        P.dma("sp", gt[:], gTv[:, :, d, t0:t0 + TC], writes=[gtb])
                        pss = []
                        for (wt_, wb_, base, nk) in ((w0, w0b, 0, 8), (w1, w1b, 8, 8), (w2, w2b, 16, 4)):
                            ps, pb = psum.next()
                            for kt in range(nk):
                                P.op("pe", lambda e, kt=kt, ps=ps, wt_=wt_, j=j, base=base, nk=nk, t0=t0: e.matmul(
                                    ps[:], lhsT=wt_[:, kt, j * 128:(j + 1) * 128], rhs=hid[:, base + kt, t0:t0 + TC],
                                    start=(kt == 0), stop=(kt == nk - 1)), reads=[wb_, hidb[base + kt]], writes=[pb])
                            pss.append((ps, pb))
                        m1, m1b = f32pool.next()
                        m2, m2b = f32pool.next()
                        P.op("dve", lambda e, m1=m1, ps=pss[0][0], gt=gt: e.tensor_tensor(
                            out=m1[:], in0=ps[:], in1=gt[:, 0, :], op=ALU.mult), reads=[pss[0][1], gtb], writes=[m1b])
                        P.op("dve", lambda e, m2=m2, ps=pss[1][0], gt=gt: e.tensor_tensor(
                            out=m2[:], in0=ps[:], in1=gt[:, 1, :], op=ALU.mult), reads=[pss[1][1], gtb], writes=[m2b])
                        P.op("dve", lambda e, m1=m1, m2=m2: e.tensor_tensor(
                            out=m1[:], in0=m1[:], in1=m2[:], op=ALU.add), reads=[m1b, m2b], writes=[m1b])
                        P.op("dve", lambda e, m2=m2, ps=pss[2][0], gt=gt: e.tensor_tensor(
                            out=m2[:], in0=ps[:], in1=gt[:, 2, :], op=ALU.mult), reads=[pss[2][1], gtb], writes=[m2b])
                        P.op("dve", lambda e, m1=m1, m2=m2, d=d, t0=t0: e.tensor_tensor(
                            out=h_t[:, d, t0:t0 + TC], in0=m1[:], in1=m2[:], op=ALU.add), reads=[m1b, m2b], writes=[hb[d]])
            for dc in range(D // 256):
                wt, wb = load_w(wov, NKT, dc * 256, 256)
                for j in range(2):
                    d = dc * 2 + j
                    for c in range(NC2):
                        po, pob = psum.next()
                        for kt in range(NKT):
                            P.op("pe", lambda e, kt=kt, po=po, wt=wt, j=j, c=c: e.matmul(
                                po[:], lhsT=wt[:, kt, j * 128:(j + 1) * 128], rhs=h_t[:, kt, c * TC:(c + 1) * TC],
                                start=(kt == 0), stop=(kt == NKT - 1)), reads=[wb, hb[kt]], writes=[pob])
                        residual_rmw(po, pob, d, c, 1.0, xTv, False)
            ffn(xoTv, True)
            yTv = yT.rearrange("(kt p) t -> p kt t", p=128)
            for c in range(NC2):
                load_x_chunk(xoTv, c, True)
                rstd, rb = rmsnorm_stats()
                for kt in range(NKT):
                    st, stb = f32pool.next()
                    P.op("dve", lambda e, kt=kt, rstd=rstd, st=st: e.scalar_tensor_tensor(
                        out=st[:], in0=x_t[:, kt, :], scalar=g2_t[:, kt:kt + 1], in1=rstd[:],
                        op0=ALU.mult, op1=ALU.mult), reads=[xb[kt], rb, g2b], writes=[stb])
                    P.dma("sp", yTv[:, kt, c * TC:(c + 1) * TC], st[:], reads=[stb], is_output=True)
        P.emit()
    return nc


S = 4096
NQT = S // 128
NCH = S // 512
NCMP = 255
NEG = -30000.0
BIGSEL = 1.0e9


def build_attn():
    nc = bass.Bass("TRN2", target_bir_lowering=False)

    def din(name, shape, dt=F32):
        return nc.dram_tensor(name, shape, dt, kind="ExternalInput").ap()

    def dout(name, shape, dt=F32):
        return nc.dram_tensor(name, shape, dt, kind="ExternalOutput").ap()

    dqa = din("dqa", [2, 2, 66, S], BF16)
    dka = din("dka", [2, 2, 66, S], BF16)
    dva = din("dva", [2, 128, NQT, 129], BF16)
    lamp = din("lamp", [128, 4, 64])
    subg = din("subg", [128, 128])
    lami = din("lami", [128, 1])
    nq = din("nq", [4, 128, S], BF16)
    kcs = din("kcs", [2, 128, S], BF16)
    kslc = din("kslc", [128, S], BF16)
    kwin = din("kwin", [128, S], BF16)
    vslc = din("vslc", [128, NQT, 129], BF16)
    vwin = din("vwin", [128, NQT, 129], BF16)
    ngate = din("ngate", [128, NQT, 6])
    w1 = din("w1", [2, 4096, 128])
    w2 = din("w2", [2, 128, 128])
    posT = din("posT", [2, 128, 32])
    mq = din("mq", [128, S], BF16)
    memT = din("memT", [D, 256])
    gmem = din("gmem", [128, NKT])
    wmk = din("wmk", [D, 128])
    wmv = din("wmv", [D, 128])
    c_ident = din("c_ident", [128, 128], BF16)
    c_causal = din("c_causal", [128, 128], BF16)
    c_winneg = din("c_winneg", [128, 128], BF16)
    c_tneg = din("c_tneg", [128, 2560], BF16)
    c_ufix = din("c_ufix", [128, NQT, 64])
    c_eaug = din("c_eaug", [2, 66, S], BF16)
    c_selaug = din("c_selaug", [2, S], BF16)
    c_slrow = din("c_slrow", [2, 2, 128], BF16)
    c_bias = din("c_bias", [128, 4, 32])
    c_ovl = din("c_ovl", [2, 128, 65], BF16)
    o_diff = dout("o_diff", [S, 2, 128], BF16)
    o_nsa = dout("o_nsa", [S, 2, 128], BF16)
    o_mem = dout("o_mem", [S, 128], BF16)

    with ExitStack() as es:
        P = Prog(nc)
        sb = lambda name, shape, dt: es.enter_context(nc.sbuf_tensor(name, shape, dt))
        cnt = [0]

        def const(name, src, shape, dt, eng="sp"):
            t = sb(name, shape, dt)
            b = Buf(name)
            P.dma(eng, t[:], src, writes=[b])
            return t, b

        ident, identb = const("ident", c_ident, [128, 128], BF16)
        causal, causalb = const("causal", c_causal, [128, 128], BF16)
        winneg, winnegb = const("winneg", c_winneg, [128, 128], BF16)
        tneg, tnegb = const("tneg", c_tneg, [128, 2560], BF16)
        ufix, ufixb = const("ufix", c_ufix, [128, NQT, 64], F32)
        biasT, biasb = const("biasT", c_bias, [128, 4, 32], F32)
        lamp_t, lampb = const("lamp_t", lamp, [128, 4, 64], F32)
        subg_t, subgb = const("subg_t", subg, [128, 128], F32)
        lami_t, lamib = const("lami_t", lami, [128, 1], F32)
        epsc = sb("epsc", [128, 1], F32)
        epsb = Buf("eps")
        P.op("dve", lambda e: e.memset(epsc[:], EPS), writes=[epsb])

        kpool = RPool(es, nc, "kp", [128, S], BF16, 5)
        vpool = RPool(es, nc, "vp", [128, NQT, 129], BF16, 3)
        qpool = RPool(es, nc, "qp", [128, 4, 512], BF16, 3)
        ptpool = RPool(es, nc, "pt", [128, 512], BF16, 4)
        ps_s = RPool(es, nc, "pss", [128, 512], F32, 3, psum=True)
        ps_a = RPool(es, nc, "psa", [128, 512], F32, 4, psum=True)
        pst = es.enter_context(nc.psum_tensor("pst", [128, 1024], BF16))
        pstb = Buf("pst")
        smallp = RPool(es, nc, "sm", [128, 8], F32, 12)
        o32p = RPool(es, nc, "o32", [128, 128], F32, 10)
        sqjp = RPool(es, nc, "sqj", [128, 128], F32, 2)
        obfp = RPool(es, nc, "obf", [128, 128], BF16, 4)

        def exp_act(pt, ps, lo, hi, rows, bias_ap, reads, writes):
            if bias_ap is None:
                P.op("act", lambda e: e.activation(out=pt[0:rows, lo:hi], in_=ps[0:rows, lo:hi], func=AF.Exp),
                     reads=reads, writes=writes)
            else:
                P.op("act", lambda e: e.activation(out=pt[0:rows, lo:hi], in_=ps[0:rows, lo:hi], func=AF.Exp,
                                                   bias=bias_ap), reads=reads + [biasb], writes=writes)

        mem_t = sb("mem_t", [128, NKT, 256], F32)
        memb = Buf("mem")
        P.dma("sp", mem_t[:], memT.rearrange("(kt p) m -> p kt m", p=128), writes=[memb])
        gmem_t, gmemb = const("gmem_t", gmem, [128, NKT], F32)
        ones = sb("ones", [128, 128], BF16)
        onesb = Buf("ones")
        P.op("dve", lambda e: e.memset(ones[:], 1.0), writes=[onesb])
        memn = sb("memn", [128, NKT, 256], BF16)
        memnb = Buf("memn")
        wmk_t = sb("wmk_t", [128, NKT, 128], BF16)
        wmkb = Buf("wmk")
        wmv_t = sb("wmv_t", [128, NKT, 128], BF16)
        wmvb = Buf("wmv")
        P.dma("pool", wmk_t[:], wmk.rearrange("(kt p) f -> p kt f", p=128), writes=[wmkb])
        P.dma("pool", wmv_t[:], wmv.rearrange("(kt p) f -> p kt f", p=128), writes=[wmvb])
        ps, pb = ps_s.next()
        for kt in range(NKT):
            sq, sqb = ptpool.next()
            P.op("act", lambda e, kt=kt, sq=sq: e.activation(out=sq[:, 0:256], in_=mem_t[:, kt, :], func=AF.Square),
                 reads=[memb], writes=[sqb])
            P.op("pe", lambda e, kt=kt, sq=sq, ps=ps: e.matmul(ps[:, 0:256], lhsT=ones[:], rhs=sq[:, 0:256],
                                                                start=(kt == 0), stop=(kt == NKT - 1)),
                 reads=[sqb, onesb], writes=[pb])
        rstd_m = sb("rstd_m", [128, 256], F32)
        rstdmb = Buf("rstd_m")
        P.op("act", lambda e, ps=ps: e.activation(out=rstd_m[:], in_=ps[:, 0:256], func=AF.Sqrt, bias=epsc[:],
                                                  scale=1.0 / D), reads=[pb, epsb], writes=[rstdmb])
        P.op("dve", lambda e: e.reciprocal(out=rstd_m[:], in_=rstd_m[:]), reads=[rstdmb], writes=[rstdmb])
        for kt in range(NKT):
            P.op("dve", lambda e, kt=kt: e.scalar_tensor_tensor(
                out=memn[:, kt, :], in0=mem_t[:, kt, :], scalar=gmem_t[:, kt:kt + 1], in1=rstd_m[:],
                op0=ALU.mult, op1=ALU.mult), reads=[memb, rstdmb, gmemb], writes=[memnb])
        kmT = sb("kmT", [128, 256], BF16)
        kmTb = Buf("kmT")
        vm = sb("vm", [128, 2, 129], BF16)
        vmb = Buf("vm")
        ps, pb = ps_s.next()
        for kt in range(NKT):
            P.op("pe", lambda e, kt=kt, ps=ps: e.matmul(ps[:, 0:256], lhsT=wmk_t[:, kt, :], rhs=memn[:, kt, :],
                                                        start=(kt == 0), stop=(kt == NKT - 1)),
                 reads=[wmkb, memnb], writes=[pb])
        P.op("act", lambda e, ps=ps: e.activation(out=kmT[:], in_=ps[:, 0:256], func=AF.Copy), reads=[pb], writes=[kmTb])
        P.op("dve", lambda e: e.memset(vm[:, :, 128:129], 1.0), writes=[vmb])
        for mt in range(2):
            ps, pb = ps_s.next()
            for kt in range(NKT):
                P.op("pe", lambda e, kt=kt, ps=ps, mt=mt: e.matmul(
                    ps[:, 0:128], lhsT=memn[:, kt, mt * 128:(mt + 1) * 128], rhs=wmv_t[:, kt, :],
                    start=(kt == 0), stop=(kt == NKT - 1)), reads=[wmvb, memnb], writes=[pb])
            P.op("act", lambda e, ps=ps, mt=mt: e.activation(out=vm[:, mt, 0:128], in_=ps[:, 0:128], func=AF.Copy),
                 reads=[pb], writes=[vmb])
        class Pipe:
            DEPTH = 2

            def __init__(self):
                self.q = []

            def _npv(self):
                return sum(1 for k, _ in self.q if k == "pv")

            def _drain_epis(self):
                while self.q and self.q[0][0] == "epi":
                    self.q.pop(0)[1]()

            def step(self, qk, exp, pv):
                qk()
                exp()
                self.q.append(("pv", pv))
                while self._npv() > self.DEPTH:
                    k, fn = self.q.pop(0)
                    fn()
                    if k == "pv":
                        break
                self._drain_epis()

            def end_pass(self, epi):
                self.q.append(("epi", epi))
                self._drain_epis()

            def flush(self):
                while self.q:
                    self.q.pop(0)[1]()

        pipe = Pipe()

        def run_pass(steps, epi):
            for (qk, ex, pv) in steps:
                pipe.step(qk, ex, pv)
            pipe.end_pass(epi)

        for c in range(NCH):
            qt_, qb_ = qpool.next()
            P.dma("sp", qt_[:, 0, :], mq[:, c * 512:(c + 1) * 512], writes=[qb_])
            accs = [ps_a.next() for _ in range(4)]
            steps = []
            for mt in range(2):
                st = {}

                def qk(st=st, mt=mt, qt_=qt_, qb_=qb_):
                    ps, pb = ps_s.next()
                    st["ps"], st["pb"] = ps, pb
                    P.op("pe", lambda e: e.matmul(ps[:], lhsT=kmT[:, mt * 128:(mt + 1) * 128], rhs=qt_[:, 0, :],
                                                  start=True, stop=True), reads=[kmTb, qb_], writes=[pb])

                def ex(st=st):
                    pt, ptb = ptpool.next()
                    st["pt"], st["ptb"] = pt, ptb
                    exp_act(pt, st["ps"], 0, 512, 128, None, [st["pb"]], [ptb])

                def pv(st=st, mt=mt, accs=accs):
                    pt, ptb = st["pt"], st["ptb"]
                    for q4 in range(4):
                        acc, accb = accs[q4]
                        P.op("pe", lambda e, acc=acc, q4=q4: e.matmul(
                            acc[:, 0:129], lhsT=pt[:, q4 * 128:(q4 + 1) * 128], rhs=vm[:, mt, :],
                            start=(mt == 0), stop=(mt == 1)), reads=[ptb, vmb], writes=[accb])
                steps.append((qk, ex, pv))

            def epi(accs=accs, c=c):
                for q4 in range(4):
                    acc, accb = accs[q4]
                    rinv, rinvb = smallp.next()
                    P.op("dve", lambda e, acc=acc, rinv=rinv: e.reciprocal(out=rinv[:, 0:1], in_=acc[:, 128:129]),
                         reads=[accb], writes=[rinvb])
                    ob, obb = obfp.next()
                    P.op("dve", lambda e, acc=acc, rinv=rinv, ob=ob: e.tensor_scalar(
                        out=ob[:], in0=acc[:, 0:128], scalar1=rinv[:, 0:1], scalar2=None, op0=ALU.mult),
                        reads=[accb, rinvb], writes=[obb])
                    q0 = c * 512 + q4 * 128
                    P.dma("sp", o_mem[q0:q0 + 128, :], ob[:], reads=[obb], is_output=True)
            run_pass(steps, epi)
        pipe.flush()
        lprod = sb("lprod", [128, 2, 64], F32)
        lprodb = Buf("lprod")
        lam4 = sb("lam4", [128, 4], F32)
        lam4b = Buf("lam4")
        gsub = sb("gsub", [128, 128], F32)
        gsubb = Buf("gsub")
        P.op("dve", lambda e: e.tensor_tensor(out=lprod[:], in0=lamp_t[:, 0:4:2, :], in1=lamp_t[:, 1:4:2, :], op=ALU.mult),
             reads=[lampb], writes=[lprodb])
        P.op("dve", lambda e: e.reduce_sum(out=lam4[:, 0:2], in_=lprod[:], axis=AX.X), reads=[lprodb], writes=[lam4b])
        P.op("act", lambda e: e.activation(out=lam4[:, 0:2], in_=lam4[:, 0:2], func=AF.Exp), reads=[lam4b], writes=[lam4b])
        P.op("dve", lambda e: e.tensor_tensor(out=lam4[:, 2:3], in0=lam4[:, 0:1], in1=lam4[:, 1:2], op=ALU.subtract),
             reads=[lam4b], writes=[lam4b])
        P.op("dve", lambda e: e.tensor_tensor(out=lam4[:, 2:3], in0=lam4[:, 2:3], in1=lami_t[:, 0:1], op=ALU.add),
             reads=[lam4b, lamib], writes=[lam4b])
        P.op("dve", lambda e: e.tensor_scalar(out=lam4[:, 3:4], in0=lam4[:, 2:3], scalar1=-1.0, scalar2=None, op0=ALU.mult),
             reads=[lam4b], writes=[lam4b])
        oml = sb("oml", [128, 1], F32)
        omlb = Buf("oml")
        P.op("dve", lambda e: e.tensor_scalar(out=oml[:], in0=lami_t[:], scalar1=-1.0, scalar2=1.0, op0=ALU.mult, op1=ALU.add),
             reads=[lamib], writes=[omlb])
        P.op("dve", lambda e: e.tensor_scalar(out=gsub[:], in0=subg_t[:], scalar1=oml[:, 0:1], scalar2=None, op0=ALU.mult),
             reads=[subgb, omlb], writes=[gsubb])

        def causal_steps(ka, kab, krows, qab, qrows_sel, va, vab, bias_idx, c, accs, extra_mm=None):
            steps = []
            nk = 4 * c + 4
            for kt in range(nk):
                qlo = max(0, kt - 4 * c)
                lo = qlo * 128
                diag = kt >= 4 * c
                nmm = 1 + (1 if extra_mm is not None else 0) + (1 if diag else 0)
                st = {}

                def qk(st=st, kt=kt, lo=lo, diag=diag, nmm=nmm):
                    ps, pb = ps_s.next()
                    st["ps"], st["pb"] = ps, pb
                    P.op("pe", lambda e: e.matmul(
                        ps[:, lo:512], lhsT=ka[0:krows, kt * 128:(kt + 1) * 128], rhs=qrows_sel(lo),
                        start=True, stop=(nmm == 1)), reads=[kab, qab], writes=[pb])
                    k_done = 1
                    if extra_mm is not None:
                        k_done += 1
                        extra_mm(ps, pb, kt, lo, k_done == nmm)
                    if diag:
                        P.op("pe", lambda e: e.matmul(
                            ps[:, lo:lo + 128], lhsT=ident[:], rhs=causal[:], start=False, stop=True),
                            reads=[identb, causalb], writes=[pb])

                def ex(st=st, kt=kt, lo=lo):
                    pt, ptb = ptpool.next()
                    st["pt"], st["ptb"] = pt, ptb
                    exp_act(pt, st["ps"], lo, 512, 128, biasT[:, bias_idx, kt - 4 * c + 28:kt - 4 * c + 29],
                            [st["pb"]], [ptb])

                def pv(st=st, kt=kt, qlo=qlo):
                    pt, ptb = st["pt"], st["ptb"]
                    for q4 in range(qlo, 4):
                        acc, accb = accs[q4]
                        P.op("pe", lambda e, acc=acc, q4=q4: e.matmul(
                            acc[:, 0:129], lhsT=pt[:, q4 * 128:(q4 + 1) * 128], rhs=va[:, kt, :],
                            start=(kt == 0), stop=(kt == 4 * c + q4)), reads=[ptb, vab], writes=[accb])
                steps.append((qk, ex, pv))
            return steps

        for hl in range(2):
            va, vab = vpool.next()
            P.dma("sp", va[:], dva[hl], writes=[vab])
            kas = []
            for m in range(2):
                ka, kab = kpool.next()
                P.dma("sp", ka[0:66, :], dka[hl, m], writes=[kab])
                kas.append((ka, kab))
            for c in range(NCH):
                qa, qab = qpool.next()
                for m in range(2):
                    P.dma("sp", qa[0:66, m, :], dqa[hl, m, :, c * 512:(c + 1) * 512], writes=[qab])
                o0s = [o32p.next() for _ in range(4)]
                for m in range(2):
                    ka, kab = kas[m]
                    accs = [ps_a.next() for _ in range(4)]
                    steps = causal_steps(ka, kab, 66, qab, (lambda lo, qa=qa, m=m: qa[0:66, m, lo:512]),
                                         va, vab, hl, c, accs)

                    def epi(accs=accs, m=m, o0s=o0s, c=c, hl=hl):
                        for q4 in range(4):
                            acc, accb = accs[q4]
                            rinv, rinvb = smallp.next()
                            P.op("dve", lambda e, acc=acc, rinv=rinv: e.reciprocal(out=rinv[:, 0:1], in_=acc[:, 128:129]),
                                 reads=[accb], writes=[rinvb])
                            o0, o0b = o0s[q4]
                            if m == 0:
                                P.op("dve", lambda e, acc=acc, rinv=rinv, o0=o0: e.tensor_scalar(
                                    out=o0[:], in0=acc[:, 0:128], scalar1=rinv[:, 0:1], scalar2=None, op0=ALU.mult),
                                    reads=[accb, rinvb], writes=[o0b])
                            else:
                                P.op("dve", lambda e, rinv=rinv: e.tensor_tensor(
                                    out=rinv[:, 0:1], in0=rinv[:, 0:1], in1=lam4[:, 3:4], op=ALU.mult),
                                    reads=[rinvb, lam4b], writes=[rinvb])
                                P.op("dve", lambda e, acc=acc, rinv=rinv, o0=o0: e.scalar_tensor_tensor(
                                    out=o0[:], in0=acc[:, 0:128], scalar=rinv[:, 0:1], in1=o0[:], op0=ALU.mult, op1=ALU.add),
                                    reads=[accb, rinvb, o0b], writes=[o0b])
                                sqj, sqjb = sqjp.next()
                                ss, ssb = smallp.next()
                                P.op("dve", lambda e, sqj=sqj, o0=o0: e.tensor_tensor(
                                    out=sqj[:], in0=o0[:], in1=o0[:], op=ALU.mult), reads=[o0b], writes=[sqjb])
                                P.op("dve", lambda e, sqj=sqj, ss=ss: e.reduce_sum(
                                    out=ss[:, 0:1], in_=sqj[:], axis=AX.X), reads=[sqjb], writes=[ssb])
                                P.op("act", lambda e, ss=ss: e.activation(
                                    out=ss[:, 1:2], in_=ss[:, 0:1], func=AF.Sqrt, bias=epsc[:], scale=1.0 / 128.0),
                                    reads=[ssb, epsb], writes=[ssb])
                                P.op("dve", lambda e, ss=ss: e.reciprocal(out=ss[:, 2:3], in_=ss[:, 1:2]),
                                     reads=[ssb], writes=[ssb])
                                ob, obb = obfp.next()
                                P.op("dve", lambda e, o0=o0, ss=ss, ob=ob: e.scalar_tensor_tensor(
                                    out=ob[:], in0=o0[:], scalar=ss[:, 2:3], in1=gsub[:], op0=ALU.mult, op1=ALU.mult),
                                    reads=[o0b, ssb, gsubb], writes=[obb])
                                q0 = c * 512 + q4 * 128
                                P.dma("sp", o_diff[q0:q0 + 128, hl, :], ob[:], reads=[obb], is_output=True)
                    run_pass(steps, epi)
        pipe.flush()
        f256 = RPool(es, nc, "f256", [128, 256], F32, 4)
        impp = RPool(es, nc, "imp", [128, 64], F32, 10)
        onsa = RPool(es, nc, "onsa", [128, 128], F32, 18)
        w1_t, w1b, w2_t, w2b = [], [], [], []
        w1_shared = sb("w1_s", [128, 32, 128], BF16)
        w1_sb = Buf("w1_s")
        for i in range(2):
            w1_t.append(w1_shared)
            w1b.append(w1_sb)
            t = sb("w2_%d" % i, [128, 128], BF16)
            b = Buf("w2_%d" % i)
            P.dma("pool", t[:], w2[i], writes=[b])
            w2_t.append(t)
            w2b.append(b)
        posT_t = sb("posT_t", [128, 2, 32], BF16)
        posTb = Buf("posT")
        P.dma("pool", posT_t[:], posT.rearrange("k d l -> d k l"), writes=[posTb])
        kcmpT = sb("kcmpT", [128, 256], BF16)
        kcmpTb = Buf("kcmpT")
        vaug2 = sb("vaug2", [128, 2, 193], BF16)
        vaug2b = Buf("vaug2")
        P.dma("sp", vaug2[:, :, 128:193], c_ovl.rearrange("ct p e -> p ct e"), writes=[vaug2b])
        selT = sb("selT", [66, S], BF16)
        selTb = Buf("selT")
        P.dma("sp", selT[64:66, :], c_selaug, writes=[selTb])
        hilo = sb("hilo", [2, 512], BF16)
        hilob = Buf("hilo")
        P.dma("sp", hilo[:], c_selaug[:, 0:512], writes=[hilob])
        slrow = sb("slrow", [2, 2, 128], BF16)
        slrowb = Buf("slrow")
        P.dma("sp", slrow[:], c_slrow.rearrange("h r k -> r h k"), writes=[slrowb])
        eaug, eaugb = [], []
        for hl in range(2):
            t = sb("eaug%d" % hl, [66, S], BF16)
            b = Buf("eaug%d" % hl)
            P.dma("sp", t[:], c_eaug[hl], writes=[b])
            eaug.append(t)
            eaugb.append(b)
        gate_t = sb("gate_t", [128, NQT, 6], F32)
        gateb = Buf("gate")
        P.dma("sp", gate_t[:], ngate, writes=[gateb])

        for i in range(2):
            src, srcb = kpool.next()
            P.dma("sp", src[:], kcs[i], writes=[srcb])
            P.dma("pool", w1_shared[:], w1[i].rearrange("(l d) j -> d l j", d=128), writes=[w1_sb])
            ps, pb = ps_s.next()
            for l in range(32):
                P.op("pe", lambda e, ps=ps, l=l, i=i: e.matmul(ps[:, 0:1], lhsT=w1_t[i][:, l, :], rhs=posT_t[:, i, l:l + 1],
                                                                start=(l == 0), stop=(l == 31)),
                     reads=[w1b[i], posTb], writes=[pb])
            pbias, pbiasb = smallp.next()
            P.op("dve", lambda e, ps=ps, pbias=pbias: e.tensor_copy(out=pbias[:, 0:1], in_=ps[:, 0:1]),
                 reads=[pb], writes=[pbiasb])
            ps, pb = ps_s.next()
            for l in range(32):
                P.op("pe", lambda e, ps=ps, l=l, i=i, src=src: e.matmul(
                    ps[:, 0:NCMP], lhsT=w1_t[i][:, l, :], rhs=src[:, l:l + 16 * (NCMP - 1) + 1:16],
                    start=(l == 0), stop=(l == 31)), reads=[w1b[i], srcb], writes=[pb])
            pre, preb = f256.next()
            tq, tqb = f256.next()
            P.op("dve", lambda e, ps=ps, pre=pre, pbias=pbias: e.tensor_scalar(
                out=pre[:, 0:NCMP], in0=ps[:, 0:NCMP], scalar1=pbias[:, 0:1], scalar2=None, op0=ALU.add),
                reads=[pb, pbiasb], writes=[preb])
            P.op("dve", lambda e, pre=pre, tq=tq: e.tensor_tensor(out=tq[:, 0:NCMP], in0=pre[:, 0:NCMP], in1=pre[:, 0:NCMP],
                                                                  op=ALU.mult), reads=[preb], writes=[tqb])
            P.op("dve", lambda e, tq=tq: e.tensor_scalar(out=tq[:, 0:NCMP], in0=tq[:, 0:NCMP], scalar1=0.044715, scalar2=1.0,
                                                         op0=ALU.mult, op1=ALU.add), reads=[tqb], writes=[tqb])
            P.op("dve", lambda e, pre=pre, tq=tq: e.tensor_tensor(out=tq[:, 0:NCMP], in0=tq[:, 0:NCMP], in1=pre[:, 0:NCMP],
                                                                  op=ALU.mult), reads=[preb, tqb], writes=[tqb])
            P.op("act", lambda e, tq=tq: e.activation(out=tq[:, 0:NCMP], in_=tq[:, 0:NCMP], func=AF.Sigmoid,
                                                      scale=1.5957691216057308), reads=[tqb], writes=[tqb])
            gl, glb = ptpool.next()
            P.op("dve", lambda e, pre=pre, tq=tq, gl=gl: e.tensor_tensor(out=gl[:, 0:NCMP], in0=tq[:, 0:NCMP],
                                                                          in1=pre[:, 0:NCMP], op=ALU.mult),
                 reads=[preb, tqb], writes=[glb])
            if i == 0:
                ps2, pb2 = ps_s.next()
                P.op("pe", lambda e, ps2=ps2, gl=gl: e.matmul(ps2[:, 0:NCMP], lhsT=w2_t[0][:], rhs=gl[:, 0:NCMP],
                                                              start=True, stop=True), reads=[w2b[0], glb], writes=[pb2])
                P.op("act", lambda e, ps2=ps2: e.activation(out=kcmpT[:, 0:NCMP], in_=ps2[:, 0:NCMP], func=AF.Copy),
                     reads=[pb2], writes=[kcmpTb])
            else:
                for ct, n in ((0, 128), (1, 127)):
                    ps2, pb2 = ps_s.next()
                    P.op("pe", lambda e, ps2=ps2, gl=gl, ct=ct, n=n: e.matmul(
                        ps2[0:n, 0:128], lhsT=gl[:, ct * 128:ct * 128 + n], rhs=w2_t[1][:], start=True, stop=True),
                        reads=[w2b[1], glb], writes=[pb2])
                    P.op("act", lambda e, ps2=ps2, ct=ct, n=n: e.activation(
                        out=vaug2[0:n, ct, 0:128], in_=ps2[0:n, 0:128], func=AF.Copy), reads=[pb2], writes=[vaug2b])

        kslc_t, kslcb = kpool.next()
        P.dma("sp", kslc_t[:], kslc, writes=[kslcb])
        kwin_t, kwinb = kpool.next()
        P.dma("sp", kwin_t[:], kwin, writes=[kwinb])
        vslc_t, vslcb = vpool.next()
        P.dma("sp", vslc_t[:], vslc, writes=[vslcb])
        vwin_t, vwinb = vpool.next()
        P.dma("sp", vwin_t[:], vwin, writes=[vwinb])

        for c in range(NCH):
            qn, qnb = qpool.next()
            P.dma("sp", qn[:], nq[:, :, c * 512:(c + 1) * 512].rearrange("h p t -> p h t"), writes=[qnb])
            imps = [impp.next() for _ in range(4)]
            oaccs = [[onsa.next() for _ in range(4)] for _ in range(2)]
            cts = [0] if c < 4 else [0, 1]
            for hh in range(4):
                accs = [ps_a.next() for _ in range(4)]
                steps = []
                for ct in cts:
                    n = 128 if ct == 0 else 127
                    need_mask = (ct == 1) or (c <= 4)
                    off = 512 * c - 2048 * ct
                    st = {}

                    def qk(st=st, ct=ct, n=n, need_mask=need_mask, off=off, hh=hh, qn=qn, qnb=qnb):
                        ps, pb = ps_s.next()
                        st["ps"], st["pb"] = ps, pb
                        P.op("pe", lambda e: e.matmul(
                            ps[0:n, :], lhsT=kcmpT[:, ct * 128:ct * 128 + n], rhs=qn[:, hh, :], start=True,
                            stop=(not need_mask)), reads=[kcmpTb, qnb], writes=[pb])
                        if need_mask:
                            P.op("pe", lambda e: e.matmul(
                                ps[0:n, :], lhsT=ident[:, 0:n], rhs=tneg[:, off:off + 512], start=False, stop=True),
                                reads=[identb, tnegb], writes=[pb])

                    def ex(st=st, n=n):
                        pt, ptb = ptpool.next()
                        st["pt"], st["ptb"] = pt, ptb
                        exp_act(pt, st["ps"], 0, 512, n, None, [st["pb"]], [ptb])

                    def pv(st=st, ct=ct, n=n, accs=accs, cts=cts):
                        pt, ptb = st["pt"], st["ptb"]
                        for q4 in range(4):
                            acc, accb = accs[q4]
                            P.op("pe", lambda e, acc=acc, q4=q4: e.matmul(
                                acc[:, 0:193], lhsT=pt[0:n, q4 * 128:(q4 + 1) * 128], rhs=vaug2[0:n, ct, :],
                                start=(ct == cts[0]), stop=(ct == cts[-1])), reads=[ptb, vaug2b], writes=[accb])
                    steps.append((qk, ex, pv))

                def epi(accs=accs, hh=hh, c=c, imps=imps, oaccs=oaccs):
                    for q4 in range(4):
                        acc, accb = accs[q4]
                        qg = 4 * c + q4
                        rs, rsb = smallp.next()
                        P.op("dve", lambda e, acc=acc, rs=rs: e.tensor_scalar(
                            out=rs[:, 0:1], in0=acc[:, 128:129], scalar1=1e-30, scalar2=None, op0=ALU.max),
                            reads=[accb], writes=[rsb])
                        P.op("dve", lambda e, rs=rs: e.reciprocal(out=rs[:, 1:2], in_=rs[:, 0:1]), reads=[rsb], writes=[rsb])
                        im, imb = imps[q4]
                        if hh == 0:
                            P.op("dve", lambda e, acc=acc, rs=rs, im=im: e.tensor_scalar(
                                out=im[:], in0=acc[:, 129:193], scalar1=rs[:, 1:2], scalar2=None, op0=ALU.mult),
                                reads=[accb, rsb], writes=[imb])
                        else:
                            P.op("dve", lambda e, acc=acc, rs=rs, im=im: e.scalar_tensor_tensor(
                                out=im[:], in0=acc[:, 129:193], scalar=rs[:, 1:2], in1=im[:], op0=ALU.mult, op1=ALU.add),
                                reads=[accb, rsb, imb], writes=[imb])
                        if hh < 2:
                            oa, oab = oaccs[hh][q4]
                            P.op("dve", lambda e, rs=rs, qg=qg: e.tensor_tensor(
                                out=rs[:, 2:3], in0=rs[:, 1:2], in1=gate_t[:, qg, hh * 3:hh * 3 + 1], op=ALU.mult),
                                reads=[rsb, gateb], writes=[rsb])
                            P.op("dve", lambda e, acc=acc, rs=rs, oa=oa: e.tensor_scalar(
                                out=oa[:], in0=acc[:, 0:128], scalar1=rs[:, 2:3], scalar2=None, op0=ALU.mult),
                                reads=[accb, rsb], writes=[oab])
                    if hh == 3:
                        for q4 in range(4):
                            qg = 4 * c + q4
                            im, imb = imps[q4]
                            adj, adjb = impp.next()
                            P.op("dve", lambda e, im=im, adj=adj, qg=qg: e.tensor_tensor(
                                out=adj[:], in0=im[:], in1=ufix[:, qg, :], op=ALU.add), reads=[imb, ufixb], writes=[adjb])
                            t8, t8b = smallp.next()
                            P.op("dve", lambda e, adj=adj, t8=t8: e.max(out=t8[:, 0:8], in_=adj[:]), reads=[adjb], writes=[t8b])
                            thr, thrb = smallp.next()
                            P.op("dve", lambda e, t8=t8, thr=thr: e.tensor_reduce(
                                out=thr[:, 0:1], in_=t8[:, 0:8], axis=AX.X, op=ALU.min), reads=[t8b], writes=[thrb])
                            selm, selmb = obfp.next()
                            P.op("dve", lambda e, adj=adj, thr=thr, selm=selm: e.tensor_scalar(
                                out=selm[:, 0:64], in0=adj[:], scalar1=thr[:, 0:1], scalar2=1.0, op0=ALU.is_ge,
                                op1=ALU.subtract), reads=[adjb, thrb], writes=[selmb])
                            P.op("pe", lambda e, selm=selm, q4=q4: e.transpose(
                                out=pst[0:64, q4 * 128:(q4 + 1) * 128], in_=selm[:, 0:64], identity=ident[:]),
                                reads=[selmb, identb], writes=[pstb])
                        P.op("act", lambda e: e.activation(out=selT[0:64, c * 512:(c + 1) * 512], in_=pst[0:64, 0:512],
                                                           func=AF.Copy), reads=[pstb], writes=[selTb])
                run_pass(steps, epi)
            for hl in range(2):
                accs = [ps_a.next() for _ in range(4)]
                steps = []
                for kt in range(max(0, 4 * c - 4), 4 * c + 4):
                    qlo = max(kt, 4 * c) - 4 * c
                    qhi = min(kt + 4, 4 * c + 3) - 4 * c
                    lo, hi = qlo * 128, (qhi + 1) * 128
                    diag = kt >= 4 * c
                    far = (kt + 4 >= 4 * c) and (kt + 4 <= 4 * c + 3)
                    st = {}

                    def qk(st=st, kt=kt, lo=lo, hi=hi, diag=diag, far=far, hl=hl, qn=qn, qnb=qnb, c=c):
                        ps, pb = ps_s.next()
                        st["ps"], st["pb"] = ps, pb
                        P.op("pe", lambda e: e.matmul(
                            ps[:, lo:hi], lhsT=kwin_t[:, kt * 128:(kt + 1) * 128], rhs=qn[:, hl, lo:hi],
                            start=True, stop=False), reads=[kwinb, qnb], writes=[pb])
                        P.op("pe", lambda e: e.matmul(
                            ps[:, lo:hi], lhsT=slrow[0:2, hl, :], rhs=hilo[0:2, lo:hi],
                            start=False, stop=(not diag and not far)), reads=[slrowb, hilob], writes=[pb])
                        if diag:
                            q4d = kt - 4 * c
                            P.op("pe", lambda e: e.matmul(
                                ps[:, q4d * 128:(q4d + 1) * 128], lhsT=ident[:], rhs=causal[:], start=False, stop=(not far)),
                                reads=[identb, causalb], writes=[pb])
                        if far:
                            q4f = kt + 4 - 4 * c
                            P.op("pe", lambda e: e.matmul(
                                ps[:, q4f * 128:(q4f + 1) * 128], lhsT=ident[:], rhs=winneg[:], start=False, stop=True),
                                reads=[identb, winnegb], writes=[pb])

                    def ex(st=st, kt=kt, lo=lo, hi=hi, hl=hl, c=c):
                        pt, ptb = ptpool.next()
                        st["pt"], st["ptb"] = pt, ptb
                        exp_act(pt, st["ps"], lo, hi, 128, biasT[:, 2 + hl, kt - 4 * c + 28:kt - 4 * c + 29],
                                [st["pb"]], [ptb])

                    def pv(st=st, kt=kt, qlo=qlo, qhi=qhi, accs=accs, c=c):
                        pt, ptb = st["pt"], st["ptb"]
                        for q4 in range(qlo, qhi + 1):
                            acc, accb = accs[q4]
                            qg = 4 * c + q4
                            P.op("pe", lambda e, acc=acc, q4=q4, qg=qg: e.matmul(
                                acc[:, 0:129], lhsT=pt[:, q4 * 128:(q4 + 1) * 128], rhs=vwin_t[:, kt, :],
                                start=(kt == max(0, qg - 4)), stop=(kt == qg)), reads=[ptb, vwinb], writes=[accb])
                    steps.append((qk, ex, pv))

                def epi(accs=accs, hl=hl, c=c, oaccs=oaccs):
                    for q4 in range(4):
                        acc, accb = accs[q4]
                        qg = 4 * c + q4
                        rs, rsb = smallp.next()
                        P.op("dve", lambda e, acc=acc, rs=rs: e.reciprocal(out=rs[:, 0:1], in_=acc[:, 128:129]),
                             reads=[accb], writes=[rsb])
                        P.op("dve", lambda e, rs=rs, qg=qg: e.tensor_tensor(
                            out=rs[:, 1:2], in0=rs[:, 0:1], in1=gate_t[:, qg, hl * 3 + 2:hl * 3 + 3], op=ALU.mult),
                            reads=[rsb, gateb], writes=[rsb])
                        oa, oab = oaccs[hl][q4]
                        P.op("dve", lambda e, acc=acc, rs=rs, oa=oa: e.scalar_tensor_tensor(
                            out=oa[:], in0=acc[:, 0:128], scalar=rs[:, 1:2], in1=oa[:], op0=ALU.mult, op1=ALU.add),
                            reads=[accb, rsb, oab], writes=[oab])
                run_pass(steps, epi)
            for hl in range(2):
                def mask_mm(ps, pb, kt, lo, last, hl=hl, c=c):
                    P.op("pe", lambda e: e.matmul(
                        ps[:, lo:512], lhsT=eaug[hl][0:66, kt * 128:(kt + 1) * 128],
                        rhs=selT[0:66, c * 512 + lo:(c + 1) * 512], start=False, stop=last),
                        reads=[eaugb[hl], selTb], writes=[pb])
                accs = [ps_a.next() for _ in range(4)]
                steps = causal_steps(kslc_t, kslcb, 128, qnb, (lambda lo, qn=qn, hl=hl: qn[:, hl, lo:512]),
                                     vslc_t, vslcb, 2 + hl, c, accs, extra_mm=mask_mm)

                def epi(accs=accs, hl=hl, c=c, oaccs=oaccs):
                    for q4 in range(4):
                        acc, accb = accs[q4]
                        qg = 4 * c + q4
                        rs, rsb = smallp.next()
                        P.op("dve", lambda e, acc=acc, rs=rs: e.reciprocal(out=rs[:, 0:1], in_=acc[:, 128:129]),
                             reads=[accb], writes=[rsb])
                        P.op("dve", lambda e, rs=rs, qg=qg: e.tensor_tensor(
                            out=rs[:, 1:2], in0=rs[:, 0:1], in1=gate_t[:, qg, hl * 3 + 1:hl * 3 + 2], op=ALU.mult),
                            reads=[rsb, gateb], writes=[rsb])
                        oa, oab = oaccs[hl][q4]
                        ob, obb = obfp.next()
                        P.op("dve", lambda e, acc=acc, rs=rs, oa=oa, ob=ob: e.scalar_tensor_tensor(
                            out=ob[:], in0=acc[:, 0:128], scalar=rs[:, 1:2], in1=oa[:], op0=ALU.mult, op1=ALU.add),
                            reads=[accb, rsb, oab], writes=[obb])
                        q0 = qg * 128
                        P.dma("sp", o_nsa[q0:q0 + 128, hl, :], ob[:], reads=[obb], is_output=True)
                run_pass(steps, epi)
        pipe.flush()
        P.emit()
    return nc


import ml_dtypes

BF = ml_dtypes.bfloat16
_NC = {}
DEBUG = {}


def _prog(name):
    if name not in _NC:
        _NC[name] = build_token(name) if name in ("pre", "post") else build_attn()
    return _NC[name]


def _run(name, in_maps):
    res = run_bass_kernel_spmd(_prog(name), in_maps, core_ids=list(range(8)))
    return res.results


def _lay(g):
    return np.ascontiguousarray(np.asarray(g, np.float32).reshape(NKT, 128).T)


def _tile_tok(a):
    e = a.shape[1]
    return np.ascontiguousarray(a.reshape(NQT, 128, e).transpose(1, 0, 2))


_CONST = {}


def _consts():
    if _CONST:
        return _CONST
    kk = np.arange(128)[:, None]
    qq = np.arange(128)[None, :]
    _CONST["c_ident"] = np.eye(128, dtype=np.float32).astype(BF)
    _CONST["c_causal"] = np.where(kk > qq, NEG, 0.0).astype(np.float32).astype(BF)
    _CONST["c_winneg"] = np.where(kk <= qq, NEG, 0.0).astype(np.float32).astype(BF)
    z = np.arange(2560)[None, :]
    _CONST["c_tneg"] = np.where(16 * kk + 31 <= z, 0.0, NEG).astype(np.float32).astype(BF)
    p = np.arange(128)[:, None, None]
    qg = np.arange(NQT)[None, :, None]
    j = np.arange(64)[None, None, :]
    cur = 2 * qg + (p >= 64)
    forced = (j == 0) | (j == cur) | (j == cur - 1)
    future = j > cur
    _CONST["c_ufix"] = np.where(forced, BIGSEL, np.where(future, -BIGSEL, 0.0)).astype(np.float32)
    q512 = np.arange(S) % 512
    hi = (q512 // 2) * 2
    lo = q512 % 2
    _CONST["c_selaug"] = (-np.stack([hi, lo]).astype(np.float32)).astype(BF)
    c = np.arange(256)[:, None]
    jj = np.arange(64)[None, :]
    ovl = ((16 * c < 64 * jj + 64) & (16 * c + 32 > 64 * jj) & (c < NCMP)).astype(np.float32)
    o65 = np.concatenate([(c < NCMP).astype(np.float32), ovl], axis=1)
    _CONST["c_ovl"] = o65.reshape(2, 128, 65).astype(BF)
    col = np.arange(S)[None, :]
    _CONST["erows"] = np.where(col // 64 == np.arange(64)[:, None], 30000.0, 0.0).astype(np.float32)
    return _CONST


def _slope(h):
    return 2.0 ** (-(h + 1.0))


def kernel(**inputs):
    inp = {k: np.asarray(v) for k, v in inputs.items()}
    x = inp["x"].astype(np.float32).reshape(8 * TOK, D)
    mem = inp["mem"].astype(np.float32)
    cst = _consts()
    xT = [np.ascontiguousarray(x[c * TOK:(c + 1) * TOK].T) for c in range(8)]
    depth = inp["w_in"].shape[0]
    yT = None
    for l in range(depth):
        maps = [{"xT": xT[c], "gn": _lay(inp["ffn1_norm"][l]), "gm": _lay(inp["mix_norm"][l]),
                 "wg": inp["ffn1_w_gate"][l], "wu": inp["ffn1_w_up"][l], "wd": inp["ffn1_w_down"][l],
                 "win": inp["w_in"][l]} for c in range(8)]
        r1 = _run("pre", maps)
        x1T = [r["xoT"] for r in r1]
        gT = [r["gT"] for r in r1]
        lam_init = 0.8 - 0.6 * math.exp(-0.3 * l)
        maps = []
        for b in range(2):
            featB = np.concatenate([np.asarray(r1[4 * b + i]["featT"]) for i in range(4)], axis=1)
            tokVB = np.concatenate([np.asarray(r1[4 * b + i]["tokV"]) for i in range(4)], axis=0)
            gatesB = np.concatenate([np.asarray(r1[4 * b + i]["gates"]) for i in range(4)], axis=0)
            onesc = np.ones((S, 1), BF)
            for r in range(4):
                g = r // 2
                m = {}
                dqa = np.zeros((2, 2, 66, S), BF)
                dka = np.zeros((2, 2, 66, S), BF)
                dva = np.zeros((2, 128, NQT, 129), BF)
                for hl in range(2):
                    h = 2 * r + hl
                    for mm in range(2):
                        dqa[hl, mm, 0:64] = featB[h * 128 + mm * 64:h * 128 + mm * 64 + 64]
                        dqa[hl, mm, 64:66] = cst["c_selaug"]
                        dka[hl, mm, 0:64] = featB[1024 + h * 128 + mm * 64:1024 + h * 128 + mm * 64 + 64]
                        dka[hl, mm, 64:66] = np.float32(_slope(h))
                    dva[hl] = _tile_tok(np.concatenate([tokVB[:, h * 128:(h + 1) * 128], onesc], axis=1))
                m["dqa"], m["dka"], m["dva"] = dqa, dka, dva
                m["lamp"] = np.ascontiguousarray(np.broadcast_to(inp["diff_lambda"][l].astype(np.float32)[None], (128, 4, 64)))
                m["subg"] = np.ascontiguousarray(np.broadcast_to(inp["diff_subln"][l].astype(np.float32)[None], (128, 128)))
                m["lami"] = np.full((128, 1), lam_init, np.float32)
                outh = [2 * r, 2 * r + 1]
                hh_list = outh + [hh for hh in range(4 * g, 4 * g + 4) if hh not in outh]
                m["nq"] = np.stack([featB[2048 + hh * 128:2048 + (hh + 1) * 128] for hh in hh_list])
                m["kcs"] = np.stack([featB[3072 + g * 128:3072 + (g + 1) * 128], featB[3328 + g * 128:3328 + (g + 1) * 128]])
                m["kslc"] = np.ascontiguousarray(featB[3584 + g * 128:3584 + (g + 1) * 128])
                m["kwin"] = np.ascontiguousarray(featB[3840 + g * 128:3840 + (g + 1) * 128])
                m["vslc"] = _tile_tok(np.concatenate([tokVB[:, 1024 + g * 128:1024 + (g + 1) * 128], onesc], axis=1))
                m["vwin"] = _tile_tok(np.concatenate([tokVB[:, 1280 + g * 128:1280 + (g + 1) * 128], onesc], axis=1))
                m["ngate"] = _tile_tok(np.concatenate([gatesB[:, hh * 3:hh * 3 + 3] for hh in outh], axis=1))
                m["w1"] = inp["nsa_cmp_w1"][l].astype(np.float32)
                m["w2"] = inp["nsa_cmp_w2"][l].astype(np.float32)
                m["posT"] = np.ascontiguousarray(inp["nsa_cmp_pos"][l].astype(np.float32).transpose(0, 2, 1))
                m["mq"] = np.ascontiguousarray(featB[4096 + r * 128:4096 + (r + 1) * 128])
                m["memT"] = np.ascontiguousarray(mem[b].T)
                m["gmem"] = _lay(inp["mem_norm"][l])
                m["wmk"] = np.ascontiguousarray(inp["w_mem_kv"][l][:, r * 128:(r + 1) * 128].astype(np.float32))
                m["wmv"] = np.ascontiguousarray(inp["w_mem_kv"][l][:, 512 + r * 128:512 + (r + 1) * 128].astype(np.float32))
                for k in ("c_ident", "c_causal", "c_winneg", "c_tneg", "c_ufix", "c_selaug", "c_ovl"):
                    m[k] = cst[k]
                eaug = np.zeros((2, 66, S), np.float32)
                slrow = np.zeros((2, 2, 128), np.float32)
                bias = np.zeros((128, 4, 32), np.float32)
                pp = np.arange(128, dtype=np.float32)[:, None]
                idx = np.arange(32, dtype=np.float32)[None, :]
                for hl in range(2):
                    eaug[hl, 0:64] = cst["erows"]
                    eaug[hl, 64:66] = _slope(outh[hl])
                    slrow[hl] = _slope(outh[hl])
                    bias[:, hl, :] = _slope(2 * r + hl) * (pp + 128.0 * (idx - 28.0))
                    bias[:, 2 + hl, :] = _slope(outh[hl]) * (pp + 128.0 * (idx - 28.0))
                m["c_eaug"] = eaug.astype(BF)
                m["c_slrow"] = slrow.astype(BF)
                m["c_bias"] = bias
                maps.append(m)
        r2 = _run("attn", maps)
        if DEBUG.get("on"):
            DEBUG["r1_%d" % l] = r1
            DEBUG["r2_%d" % l] = r2
        maps = []
        for c in range(8):
            b, tq = c // 4, c % 4
            sl = slice(tq * TOK, (tq + 1) * TOK)
            parts = [np.asarray(r2[4 * b + r]["o_diff"])[sl].reshape(TOK, 256) for r in range(4)]
            parts += [np.asarray(r2[4 * b + r]["o_nsa"])[sl].reshape(TOK, 256) for r in range(4)]
            parts += [np.asarray(r2[4 * b + r]["o_mem"])[sl] for r in range(4)]
            oT = np.ascontiguousarray(np.concatenate(parts, axis=1).T)
            maps.append({"xT": x1T[c], "gn": _lay(inp["ffn2_norm"][l]), "wg": inp["ffn2_w_gate"][l],
                         "wu": inp["ffn2_w_up"][l], "wd": inp["ffn2_w_down"][l], "oT": oT, "gT": gT[c],
                         "wud": inp["w_up_diff"][l], "wun": inp["w_up_nsa"][l], "wum": inp["w_up_mem"][l],
                         "wo": inp["w_out"][l], "gf": _lay(inp["final_norm"])})
        r3 = _run("post", maps)
        xT = [r["xoT"] for r in r3]
        yT = [r["yT"] for r in r3]
        if DEBUG.get("on"):
            DEBUG["r3_%d" % l] = r3
            if DEBUG.get("stop_after") == l:
                break
    out = np.concatenate([np.asarray(y).T for y in yT], axis=0).reshape(2, S, D).astype(np.float32)
    return out
```
